# Optimizing a Trainium2 kernel written in Bass

```python
import math
import jax, jax.numpy as jnp
from jax import lax
import numpy as np

D_MODEL = 2048
BATCH = 4
SEQ = 4096
DEPTH = 2

SSD_D_INNER = D_MODEL
SSD_HEAD_DIM = 64
SSD_HEADS = SSD_D_INNER // SSD_HEAD_DIM
SSD_GROUPS = 4
SSD_STATE = 128
SSD_CONV = 5
SSD_CHUNK = 128
SSD_XBC = SSD_D_INNER + 2 * SSD_GROUPS * SSD_STATE

ATTN_HEAD_DIM = 128
ATTN_PATTERNS = ((128, 1), (512, 4), (2048, 16))
ATTN_HEADS_PER_GROUP = 4
ATTN_HEADS = ATTN_HEADS_PER_GROUP * len(ATTN_PATTERNS)
ATTN_WIDTH = ATTN_HEADS * ATTN_HEAD_DIM
ATTN_OUT = ATTN_HEADS_PER_GROUP * ATTN_HEAD_DIM
ALIBI_MAX_EXP = 8.0
NEG_INF = -1e30

CONV_CHANNELS = D_MODEL
CONV_WIDTH = 31

MLP_HIDDEN = 4 * D_MODEL

N_BRANCHES = 3
NORM_EPS = 1e-6

COL_Z = 0
COL_XBC = COL_Z + SSD_D_INNER
COL_DT = COL_XBC + SSD_XBC
COL_Q = COL_DT + 2 * SSD_HEADS
COL_K = COL_Q + ATTN_WIDTH
COL_V = COL_K + ATTN_WIDTH
COL_GLU = COL_V + ATTN_WIDTH
COL_GATE = COL_GLU + 2 * CONV_CHANNELS
N_IN = COL_GATE + N_BRANCHES * D_MODEL

kernel_name = "hybrid_gated_ssd_dilattn_conformer_encoder"


def rms_norm(x, g):
    xf = x.astype(jnp.float32)
    y = xf * lax.rsqrt(jnp.mean(xf * xf, axis=-1, keepdims=True) + NORM_EPS)
    return y.astype(x.dtype) * g


def layer_norm(x, g, b):
    xf = x.astype(jnp.float32)
    mu = jnp.mean(xf, axis=-1, keepdims=True)
    var = jnp.mean(jnp.square(xf - mu), axis=-1, keepdims=True)
    y = (xf - mu) * lax.rsqrt(var + NORM_EPS)
    return y.astype(x.dtype) * g + b


def depthwise_conv(x, w, b):
    k = w.shape[0]
    y = lax.conv_general_dilated(
        x, w[:, None, :], window_strides=(1,), padding=[(k // 2, k // 2)],
        dimension_numbers=("NWC", "WIO", "NWC"), feature_group_count=x.shape[-1])
    return y + b


def ssd_chunked(xh, dt, a, bm, cm):
    bsz, seqlen, nh, hp = xh.shape
    ng, ns = bm.shape[2], bm.shape[3]
    nr = nh // ng
    q = SSD_CHUNK
    nc = seqlen // q
    xdt = (xh * dt[..., None]).reshape(bsz, nc, q, ng, nr, hp)
    adt = (dt * a).reshape(bsz, nc, q, ng, nr).transpose(0, 3, 4, 1, 2)
    a_cs = jnp.cumsum(adt, axis=-1)
    seg = a_cs[..., :, None] - a_cs[..., None, :]
    lower = jnp.tril(jnp.ones((q, q), dtype=bool))
    lmat = jnp.exp(jnp.where(lower, seg, -jnp.inf))
    bc = bm.reshape(bsz, nc, q, ng, ns)
    cc = cm.reshape(bsz, nc, q, ng, ns)
    cb = jnp.einsum("bclgn,bcsgn->bgcls", cc, bc)
    y_diag = jnp.einsum("bgrcls,bcsgrp->bclgrp", cb[:, :, None] * lmat, xdt)
    decay_states = jnp.exp(a_cs[..., -1:] - a_cs)
    states = jnp.einsum("bcsgn,bgrcs,bcsgrp->cbgrpn", bc, decay_states, xdt)
    chunk_decay = jnp.exp(a_cs[..., -1]).transpose(3, 0, 1, 2)

    def step(h, inp):
        dec, st = inp
        return dec[..., None, None] * h + st, h

    h0 = jnp.zeros(states.shape[1:], states.dtype)
    _, h_in = lax.scan(step, h0, (chunk_decay, states))
    y_off = jnp.einsum("bclgn,cbgrpn,bgrcl->bclgrp", cc, h_in, jnp.exp(a_cs))
    return (y_diag + y_off).reshape(bsz, seqlen, nh, hp)


def ssd_mixer(z, xbc, dt_raw, conv_w, conv_b, dt_bias, a_log, d_skip, norm_g):
    bsz, seqlen = z.shape[:2]
    xbc = jax.nn.silu(depthwise_conv(xbc, conv_w, conv_b))
    xs = xbc[..., :SSD_D_INNER]
    bm = xbc[..., SSD_D_INNER:SSD_D_INNER + SSD_GROUPS * SSD_STATE]
    cm = xbc[..., SSD_D_INNER + SSD_GROUPS * SSD_STATE:]
    xh = xs.reshape(bsz, seqlen, SSD_HEADS, SSD_HEAD_DIM)
    bm = bm.reshape(bsz, seqlen, SSD_GROUPS, SSD_STATE)
    cm = cm.reshape(bsz, seqlen, SSD_GROUPS, SSD_STATE)
    a = -jnp.exp(a_log)
    dt = jax.nn.softplus(dt_raw.reshape(bsz, seqlen, 2, SSD_HEADS) + dt_bias)
    flip = lambda t: jnp.flip(t, axis=1)
    y_fwd = ssd_chunked(xh, dt[:, :, 0], a[0], bm, cm)
    y_bwd = flip(ssd_chunked(flip(xh), flip(dt[:, :, 1]), a[1], flip(bm), flip(cm)))
    y = y_fwd + y_bwd + d_skip[:, None] * xh
    y = y.reshape(bsz, seqlen, SSD_D_INNER) * jax.nn.silu(z)
    return rms_norm(y, norm_g)


def alibi_slopes():
    return jnp.exp2(-ALIBI_MAX_EXP * jnp.arange(1, ATTN_HEADS + 1, dtype=jnp.float32) / ATTN_HEADS)


def dilated_window_attention(q, k, v, slopes, window, dilation):
    bsz, seqlen, nh, hd = q.shape
    radius = window // (2 * dilation)
    blk = radius
    sub = seqlen // dilation
    nb = -(-sub // blk)
    lp = nb * blk

    def to_sub(t):
        t = t.reshape(bsz, sub, dilation, nh, hd).transpose(0, 2, 1, 3, 4)
        return jnp.pad(t, ((0, 0), (0, 0), (0, lp - sub), (0, 0), (0, 0)))

    def neighbours(t):
        t = jnp.pad(to_sub(t), ((0, 0), (0, 0), (blk, blk), (0, 0), (0, 0)))
        t = t.reshape(bsz, dilation, nb + 2, blk, nh, hd)
        return jnp.concatenate([t[:, :, :-2], t[:, :, 1:-1], t[:, :, 2:]], axis=3)

    qs = to_sub(q).reshape(bsz, dilation, nb, blk, nh, hd)
    ks, vs = neighbours(k), neighbours(v)
    qi = jnp.arange(nb)[:, None] * blk + jnp.arange(blk)[None, :]
    kj = (jnp.arange(nb)[:, None] - 1) * blk + jnp.arange(3 * blk)[None, :]
    rel = kj[:, None, :] - qi[:, :, None]
    valid = (jnp.abs(rel) <= radius) & (kj[:, None, :] >= 0) & (kj[:, None, :] < sub)
    dist = (dilation * jnp.abs(rel)).astype(jnp.float32)
    s = jnp.einsum("brnqhe,brnkhe->brnhqk", qs, ks).astype(jnp.float32) * (hd ** -0.5)
    s = s - slopes[:, None, None] * dist[:, None]
    s = jnp.where(valid[:, None], s, NEG_INF)
    lse = jax.nn.logsumexp(s, axis=-1)
    p = jnp.exp(s - lse[..., None])
    o = jnp.einsum("brnhqk,brnkhe->brnqhe", p.astype(v.dtype), vs)

    def from_sub(t):
        t = t.reshape((bsz, dilation, lp) + t.shape[4:])[:, :, :sub]
        return jnp.swapaxes(t, 1, 2).reshape((bsz, seqlen) + t.shape[3:])

    return from_sub(o), from_sub(jnp.moveaxis(lse, 3, 4))


def dilated_attention_mixer(q, k, v, q_g, k_g):
    bsz, seqlen = q.shape[:2]
    shape = (bsz, seqlen, ATTN_HEADS, ATTN_HEAD_DIM)
    q = rms_norm(q.reshape(shape), q_g)
    k = rms_norm(k.reshape(shape), k_g)
    v = v.reshape(shape)
    slopes = alibi_slopes()
    outs, lses = [], []
    for g, (window, dilation) in enumerate(ATTN_PATTERNS):
        sl = slice(g * ATTN_HEADS_PER_GROUP, (g + 1) * ATTN_HEADS_PER_GROUP)
        o, l = dilated_window_attention(q[:, :, sl], k[:, :, sl], v[:, :, sl], slopes[sl], window, dilation)
        outs.append(o)
        lses.append(l)
    wts = jax.nn.softmax(jnp.stack(lses, axis=0), axis=0)
    o = jnp.sum(wts[..., None].astype(v.dtype) * jnp.stack(outs, axis=0), axis=0)
    return o.reshape(bsz, seqlen, ATTN_OUT)


def conformer_conv(u, dw_w, dw_b, ln_g, ln_b):
    a = u[..., :CONV_CHANNELS]
    gate = u[..., CONV_CHANNELS:]
    y = a * jax.nn.sigmoid(gate)
    y = depthwise_conv(y, dw_w, dw_b)
    y = layer_norm(y, ln_g, ln_b)
    return jax.nn.silu(y)


def setup_inputs(seed: int = 0) -> dict:
    key = jax.random.key(seed)
    ks = jax.random.split(key, 24)
    L = DEPTH

    def nrm(k, shape, scale):
        return jax.random.normal(k, shape, jnp.float32) * scale

    def gain(k, shape, scale=0.02):
        return 1.0 + scale * jax.random.normal(k, shape, jnp.float32)

    dt0 = jnp.exp(jax.random.uniform(ks[5], (L, 2, SSD_HEADS), jnp.float32,
                                     math.log(1e-3), math.log(1e-1)))
    return {
        "x": nrm(ks[0], (BATCH, SEQ, D_MODEL), 1.0),
        "norm1_g": gain(ks[1], (L, D_MODEL)),
        "w_in": nrm(ks[2], (L, D_MODEL, N_IN), D_MODEL ** -0.5),
        "ssd_conv_w": nrm(ks[3], (L, SSD_CONV, SSD_XBC), SSD_CONV ** -0.5),
        "ssd_conv_b": nrm(ks[4], (L, SSD_XBC), 0.02),
        "ssd_dt_bias": dt0 + jnp.log(-jnp.expm1(-dt0)),
        "ssd_a_log": jnp.log(jax.random.uniform(ks[6], (L, 2, SSD_HEADS), jnp.float32, 1.0, 16.0)),
        "ssd_d": gain(ks[7], (L, SSD_HEADS), 0.1),
        "ssd_norm_g": gain(ks[8], (L, SSD_D_INNER)),
        "w_ssd_o": nrm(ks[9], (L, SSD_D_INNER, D_MODEL), SSD_D_INNER ** -0.5),
        "q_norm_g": gain(ks[10], (L, ATTN_HEAD_DIM)),
        "k_norm_g": gain(ks[11], (L, ATTN_HEAD_DIM)),
        "w_attn_o": nrm(ks[12], (L, ATTN_OUT, D_MODEL), ATTN_OUT ** -0.5),
        "conv_dw_w": nrm(ks[13], (L, CONV_WIDTH, CONV_CHANNELS), CONV_WIDTH ** -0.5),
        "conv_dw_b": nrm(ks[14], (L, CONV_CHANNELS), 0.02),
        "conv_ln_g": gain(ks[15], (L, CONV_CHANNELS)),
        "conv_ln_b": nrm(ks[16], (L, CONV_CHANNELS), 0.02),
        "w_conv_o": nrm(ks[17], (L, CONV_CHANNELS, D_MODEL), CONV_CHANNELS ** -0.5),
        "w_out": nrm(ks[18], (L, D_MODEL, D_MODEL), D_MODEL ** -0.5),
        "norm2_g": gain(ks[19], (L, D_MODEL)),
        "w_mlp_up": nrm(ks[20], (L, D_MODEL, MLP_HIDDEN), D_MODEL ** -0.5),
        "w_mlp_down": nrm(ks[21], (L, MLP_HIDDEN, D_MODEL), MLP_HIDDEN ** -0.5),
    }


def reference(x, norm1_g, w_in, ssd_conv_w, ssd_conv_b, ssd_dt_bias, ssd_a_log, ssd_d,
              ssd_norm_g, w_ssd_o, q_norm_g, k_norm_g, w_attn_o, conv_dw_w, conv_dw_b,
              conv_ln_g, conv_ln_b, w_conv_o, w_out, norm2_g, w_mlp_up, w_mlp_down):
    bsz, seqlen = x.shape[:2]
    for i in range(DEPTH):
        h = rms_norm(x, norm1_g[i])
        proj = h @ w_in[i]
        y_ssd = ssd_mixer(proj[..., COL_Z:COL_XBC], proj[..., COL_XBC:COL_DT],
                          proj[..., COL_DT:COL_Q], ssd_conv_w[i], ssd_conv_b[i],
                          ssd_dt_bias[i], ssd_a_log[i], ssd_d[i], ssd_norm_g[i]) @ w_ssd_o[i]
        y_att = dilated_attention_mixer(proj[..., COL_Q:COL_K], proj[..., COL_K:COL_V],
                                        proj[..., COL_V:COL_GLU], q_norm_g[i], k_norm_g[i]) @ w_attn_o[i]
        y_conv = conformer_conv(proj[..., COL_GLU:COL_GATE], conv_dw_w[i], conv_dw_b[i],
                                conv_ln_g[i], conv_ln_b[i]) @ w_conv_o[i]
        gates = jax.nn.sigmoid(proj[..., COL_GATE:]).reshape(bsz, seqlen, N_BRANCHES, D_MODEL)
        merged = gates[:, :, 0] * y_ssd + gates[:, :, 1] * y_att + gates[:, :, 2] * y_conv
        x = x + merged @ w_out[i]
        h2 = rms_norm(x, norm2_g[i])
        x = x + jnp.square(jax.nn.relu(h2 @ w_mlp_up[i])) @ w_mlp_down[i]
    return x
```

```python
import contextlib
import numpy as np
import concourse.bass as bass
import concourse.mybir as mybir
from concourse.bass_utils import run_bass_kernel_spmd

F32 = mybir.dt.float32
BF16 = mybir.dt.bfloat16
AF = mybir.ActivationFunctionType
ALU = mybir.AluOpType

ENGS = ["tensor", "vector", "scalar", "gpsimd", "sync"]
EPOCH = 30000

D = 2048
KC = 16
N_IN = 20032
C_Z, C_XBC, C_DT, C_Q, C_K, C_V, C_GLU, C_GATE = 0, 2048, 5120, 5184, 6720, 8256, 9792, 13888
EPS = 1e-6
DEPTH = 2


class Tok:
    __slots__ = ("name", "w", "r", "dsem")

    def __init__(self, name=""):
        self.name = name
        self.w = None
        self.r = {}
        self.dsem = None


class Prog:
    def __init__(self, nc, stack):
        self.nc = nc
        self.stack = stack
        self.q = {e: [] for e in ENGS}
        self.cnt = {e: 0 for e in ENGS}
        self.sems = {}
        self.waited = {e: {} for e in ENGS}
        self.nsem = 0
        self.dcount = {}
        self.free_dsems = {False: [], True: []}
        self.phase_sems = None

    def begin_phase(self):
        self.phase_sems = []

    def end_phase(self):
        for sw, k in self.phase_sems:
            self.free_dsems[sw].append(k)
        self.phase_sems = None

    def _sem(self, key):
        if key not in self.sems:
            self.sems[key] = self.stack.enter_context(self.nc.semaphore("s%d" % self.nsem))
            self.nsem += 1
        return self.sems[key]

    def tok(self, name=""):
        return Tok(name)

    def dtok(self, name="", sw=False):
        t = Tok(name)
        if self.free_dsems[sw]:
            t.dsem = self.free_dsems[sw].pop()
        else:
            t.dsem = ("d", self.nsem, name)
            self._sem(t.dsem)
            self.dcount[t.dsem] = 0
        if self.phase_sems is not None:
            self.phase_sems.append((sw, t.dsem))
        return t

    def _deps(self, eng, reads, writes, self_waw_ok):
        deps = {}

        def add(d):
            if d is None:
                return
            k, v = d
            if deps.get(k, 0) < v:
                deps[k] = v
        for t in reads:
            add(t.w)
        for t in writes:
            if t.w is not None:
                if not (self_waw_ok and t.w[0][0] == "e" and t.w[0][1] == eng):
                    add(t.w)
            for k, v in t.r.items():
                if k[0] == "e" and k[1] == eng:
                    continue
                add((k, v))
        out = []
        wd = self.waited[eng]
        for k, v in deps.items():
            if eng == "tensor" and k[0] == "e" and k[1] == "tensor":
                continue
            if wd.get(k, 0) >= v:
                continue
            wd[k] = v
            out.append((k, v))
        return out

    def _mark(self, comp, reads, writes):
        k, v = comp
        for t in reads:
            if t.r.get(k, 0) < v:
                t.r[k] = v
        for t in writes:
            t.w = comp
            t.r = {}

    def op(self, eng, fn, reads=(), writes=(), self_waw_ok=False):
        waits = self._deps(eng, reads, writes, self_waw_ok)
        self.cnt[eng] += 1
        ep, v = divmod(self.cnt[eng] - 1, EPOCH)
        key = ("e", eng, ep)
        comp = (key, v + 1)
        self._sem(key)
        self.q[eng].append((waits, fn, key, 1))
        self._mark(comp, reads, writes)
        return comp

    def dma(self, eng, fn, reads=(), writes=(), sem_tok=None):
        waits = self._deps(eng, reads, writes, False)
        key = sem_tok.dsem
        self.dcount[key] += 16
        comp = (key, self.dcount[key])
        self.q[eng].append((waits, fn, key, 16))
        self._mark(comp, reads, writes)
        return comp

    def barrier(self):
        allw = []
        for e in ENGS:
            if self.cnt[e] > 0:
                ep, v = divmod(self.cnt[e] - 1, EPOCH)
                allw.append((("e", e, ep), v + 1))
        for k, v in self.dcount.items():
            if v > 0:
                allw.append((k, v))
        for e in ENGS:
            wd = self.waited[e]
            waits = []
            for k, v in allw:
                if k[0] == "e" and k[1] == e:
                    continue
                if wd.get(k, 0) >= v:
                    continue
                wd[k] = v
                waits.append((k, v))
            if waits:
                self.q[e].append((waits, None, None, 0))

    def emit(self, block):
        fin = [(k, v) for k, v in self.dcount.items() if v > 0]
        sems = self.sems
        q = self.q

        def runner(e):
            def run(engine):
                for waits, fn, key, inc in q[e]:
                    for k, v in waits:
                        engine.wait_ge(sems[k], v)
                    if fn is None:
                        continue
                    ins = fn(engine)
                    ins.then_inc(sems[key], inc)
                if e == "sync":
                    for k, v in fin:
                        engine.wait_ge(sems[k], v)
            return run
        block.tensor(runner("tensor"))
        block.vector(runner("vector"))
        block.scalar(runner("scalar"))
        block.gpsimd(runner("gpsimd"))
        block.sync(runner("sync"))


class Rot:
    def __init__(self, items):
        self.items = items
        self.i = 0

    def next(self):
        it = self.items[self.i % len(self.items)]
        self.i += 1
        return it


ATTN_PATTERNS = ((128, 1), (512, 4), (2048, 16))


def host_consts():
    c = {}
    i = np.arange(128)
    lp, l = i[:, None], i[None, :]
    c["c_ident"] = np.eye(128, dtype=np.float32)
    tri = np.stack([(lp <= l), (lp > l), (lp < l), (lp >= l)], axis=1).astype(np.float32)
    c["c_tri"] = np.ascontiguousarray(tri)
    s, ll = i[:, None], i[None, :]
    neg = np.stack([np.where(ll < s, -30000.0, 0.0), np.where(ll > s, 30000.0, 0.0)], axis=1).astype(np.float32)
    c["c_neg"] = np.ascontiguousarray(neg)
    slopes = np.exp2(-8.0 * np.arange(1, 13, dtype=np.float64) / 12.0)
    a, b = i[:, None], i[None, :]
    ab = np.zeros((128, 12, 2, 128), np.float32)
    for h in range(12):
        d = ATTN_PATTERNS[h // 4][1]
        relA = a - b - 64
        relB = a - b + 64
        ab[:, h, 0, :] = np.where(a >= b, -slopes[h] * d * np.abs(relA), -30000.0)
        ab[:, h, 1, :] = np.where(a <= b, -slopes[h] * d * np.abs(relB), -30000.0)
    c["c_abias"] = ab.reshape(128, 24, 128)
    return c


def host_layout(inp):
    L = DEPTH
    o = {}
    o["p_cw"] = np.ascontiguousarray(inp["ssd_conv_w"].transpose(0, 2, 1).reshape(L, 24, 128, 5).transpose(0, 2, 1, 3))
    o["p_cb"] = np.ascontiguousarray(inp["ssd_conv_b"].reshape(L, 24, 128).transpose(0, 2, 1))
    o["p_dw"] = np.ascontiguousarray(inp["conv_dw_w"].transpose(0, 2, 1).reshape(L, 16, 128, 31).transpose(0, 2, 1, 3))
    o["p_dwb"] = np.ascontiguousarray(inp["conv_dw_b"].reshape(L, 16, 128).transpose(0, 2, 1))
    o["p_lng"] = np.ascontiguousarray(inp["conv_ln_g"].reshape(L, 16, 128).transpose(0, 2, 1))
    o["p_lnb"] = np.ascontiguousarray(inp["conv_ln_b"].reshape(L, 16, 128).transpose(0, 2, 1))
    o["p_dtb"] = np.ascontiguousarray(inp["ssd_dt_bias"].reshape(L, 1, 64))
    o["p_alog"] = np.ascontiguousarray(inp["ssd_a_log"].reshape(L, 1, 64))
    o["p_dsk"] = np.ascontiguousarray(inp["ssd_d"].reshape(L, 1, 32))
    o["p_qg"] = np.ascontiguousarray(inp["q_norm_g"].reshape(L, 128, 1))
    o["p_kg"] = np.ascontiguousarray(inp["k_norm_g"].reshape(L, 128, 1))
    for k in ["norm1_g", "norm2_g", "ssd_norm_g"]:
        o["p_" + k] = np.ascontiguousarray(inp[k].reshape(L, 1, D))
    return o


def build(T, depth=DEPTH, dbg=False, phases=None):
    nc = bass.Bass("TRN2", target_bir_lowering=False)
    TS = min(2048, T)
    NS = T // TS
    NT = T // 128
    NTS = TS // 128

    def din(name, shape, dt=F32):
        return nc.dram_tensor(name, list(shape), dt, kind="ExternalInput").ap()

    def dscr(name, shape, dt):
        return nc.dram_tensor(name, list(shape), dt, kind=("ExternalOutput" if dbg else "Internal")).ap()

    x_in = din("x", [T, D])
    w_in = din("w_in", [depth, D, N_IN])
    w_ssd_o = din("w_ssd_o", [depth, D, D])
    w_attn_o = din("w_attn_o", [depth, 512, D])
    w_conv_o = din("w_conv_o", [depth, D, D])
    w_out = din("w_out", [depth, D, D])
    w_up = din("w_mlp_up", [depth, D, 4 * D])
    w_down = din("w_mlp_down", [depth, 4 * D, D])
    p_cw = din("p_cw", [depth, 128, 24, 5])
    p_cb = din("p_cb", [depth, 128, 24])
    p_dw = din("p_dw", [depth, 128, 16, 31])
    p_dwb = din("p_dwb", [depth, 128, 16])
    p_lng = din("p_lng", [depth, 128, 16])
    p_lnb = din("p_lnb", [depth, 128, 16])
    p_dtb = din("p_dtb", [depth, 1, 64])
    p_alog = din("p_alog", [depth, 1, 64])
    p_dsk = din("p_dsk", [depth, 1, 32])
    p_qg = din("p_qg", [depth, 128, 1])
    p_kg = din("p_kg", [depth, 128, 1])
    p_n1 = din("p_norm1_g", [depth, 1, D])
    p_n2 = din("p_norm2_g", [depth, 1, D])
    p_ng = din("p_ssd_norm_g", [depth, 1, D])
    c_ident = din("c_ident", [128, 128])
    c_tri = din("c_tri", [128, 4, 128])
    c_neg = din("c_neg", [128, 2, 128])
    c_abias = din("c_abias", [128, 24, 128])
    out = nc.dram_tensor("out", [T, D], F32, kind="ExternalOutput").ap()

    xres = dscr("xres", [T, D], F32)
    zs = dscr("zs", [T, D], BF16)
    xbc_pre = dscr("xbc_pre", [3072, T], BF16)
    xbcT = dscr("xbcT", [3072, T], BF16)
    dts = dscr("dts", [T, 64], F32)
    qT = dscr("qT", [12, 128, T], BF16)
    kT = dscr("kT", [12, 128, T], BF16)
    vs = dscr("vs", [T, 1536], BF16)
    uT = dscr("uT", [D, T], BF16)
    gT = dscr("gT", [3 * D, T], BF16)
    yb = dscr("yb", [T, D], F32)
    yg = dscr("yg", [T, D], F32)
    oT = dscr("oT", [512, T], BF16)
    ycT = dscr("ycT", [D, T], BF16)
    cT = dscr("cT", [D, T], BF16)

    with contextlib.ExitStack() as st:
        P = Prog(nc, st)

        uniq = [0]

        def sb(name, shape, dt, stack=st):
            uniq[0] += 1
            return stack.enter_context(nc.sbuf_tensor("%s_%d" % (name, uniq[0]), list(shape), dt))

        pbanks = []
        for i in range(8):
            pbanks.append((st.enter_context(nc.psum_tensor("pb%d" % i, [128, 512], F32)), P.tok("pb%d" % i)))
        prot = Rot(pbanks)
        bank = prot.next

        identf = sb("identf", [128, 128], F32)
        identb = sb("identb", [128, 128], BF16)
        onesb = sb("onesb", [128, 128], BF16)
        onesf = sb("onesf", [128, 128], F32)
        tri = sb("tri", [128, 4, 128], F32)
        negb = sb("negb", [128, 2, 128], BF16)
        trib = sb("trib", [128, 4, 128], BF16)
        abias = sb("abias", [128, 24, 128], F32)
        tconst = P.dtok("const")
        tconst2 = P.dtok("const2", sw=True)

        block = st.enter_context(nc.Block())

        P.dma("sync", lambda e: e.dma_start(out=identf[:], in_=c_ident[:]), writes=[tconst], sem_tok=tconst)
        P.dma("sync", lambda e: e.dma_start(out=tri[:], in_=c_tri[:]), writes=[tconst], sem_tok=tconst)
        P.dma("sync", lambda e: e.dma_start(out=abias[:], in_=c_abias[:]), writes=[tconst], sem_tok=tconst)
        P.dma("gpsimd", lambda e: e.dma_start(out=negb[:], in_=c_neg[:]), writes=[tconst2], sem_tok=tconst2)
        P.dma("gpsimd", lambda e: e.dma_start(out=identb[:], in_=c_ident[:]), writes=[tconst2], sem_tok=tconst2)
        P.dma("gpsimd", lambda e: e.dma_start(out=trib[:], in_=c_tri[:]), writes=[tconst2], sem_tok=tconst2)
        P.op("vector", lambda e: e.memset(onesb[:], 1.0), writes=[tconst])
        P.op("vector", lambda e: e.memset(onesf[:], 1.0), writes=[tconst])
        P.barrier()

        def rms_rstd(eng_ss, ss, rstd, tss, n):
            P.op("scalar", lambda e: e.activation(out=rstd, in_=ss, func=AF.Sqrt, bias=EPS, scale=1.0 / n), reads=[tss], writes=[tss])
            P.op("vector", lambda e: e.reciprocal(out=rstd, in_=rstd), reads=[tss], writes=[tss])

        def norm_transpose(ph, src, tsrc, gb, tgb, row0, ntok, actT, tact):
            for t in range(ntok // 128):
                xt, txt = ph["xrot"].next()
                r0 = row0 + t * 128
                P.dma("sync", lambda e, xt=xt, r0=r0: e.dma_start(out=xt[:], in_=src[r0:r0 + 128, :]), reads=[tsrc], writes=[txt], sem_tok=txt)
                junk, tj = ph["junk"]
                ss, tss = ph["ssrot"].next()
                P.op("scalar", lambda e, xt=xt, junk=junk, ss=ss: e.activation(out=junk[:], in_=xt[:], func=AF.Square, accum_out=ss[:, 0:1]), reads=[txt], writes=[tj, tss])
                rstd = ss[:, 1:2]
                rms_rstd("vector", ss[:, 0:1], rstd, tss, D)
                xn, txn = ph["xnrot"].next()
                P.op("vector", lambda e, xt=xt, xn=xn, rstd=rstd: e.scalar_tensor_tensor(out=xn[:], in0=xt[:], scalar=rstd, in1=gb[:], op0=ALU.mult, op1=ALU.mult), reads=[txt, tss, tgb], writes=[txn])
                for half in range(2):
                    ps, tps = bank()
                    psb = ps[:].bitcast(BF16)
                    for k in range(8):
                        kc = half * 8 + k
                        P.op("tensor", lambda e, psb=psb, xn=xn, k=k, kc=kc: e.transpose(psb[:, k * 128:(k + 1) * 128], xn[:, kc * 128:(kc + 1) * 128], identb[:]), reads=[txn], writes=[tps])
                    dst = actT[:, half * 8:half * 8 + 8, t * 128:(t + 1) * 128]
                    srcp = psb[:, 0:1024].rearrange("p (k c) -> p k c", c=128)
                    if half == 0:
                        P.op("scalar", lambda e, dst=dst, srcp=srcp: e.activation(out=dst, in_=srcp, func=AF.Copy), reads=[tps], writes=[tact[t]])
                    else:
                        P.op("vector", lambda e, dst=dst, srcp=srcp: e.tensor_copy(out=dst, in_=srcp), reads=[tps], writes=[tact[t]])

        def load_w(ph, W2d, kcn, r0, c0, cw):
            buf, tk = ph["wrot"].next()
            src = W2d[r0:r0 + kcn * 128, c0:c0 + cw].rearrange("(kc p) n -> p kc n", p=128)
            dst = buf[:, 0:kcn * cw].rearrange("p (kc c) -> p kc c", c=cw)
            P.dma("gpsimd", lambda e: e.dma_start(out=dst, in_=src), writes=[tk], sem_tok=tk)
            return dst, tk

        def mm_fm(wv, wtk, j, kcn, actT, tact, tt, ntok=512):
            ps, tps = bank()
            rd = [wtk] + tact[(tt * 512) // 128:(tt * 512 + ntok) // 128]
            for kc in range(kcn):
                P.op("tensor", lambda e, ps=ps, kc=kc: e.matmul(ps[:, 0:ntok], lhsT=wv[:, kc, j * 128:(j + 1) * 128], rhs=actT[:, kc, tt * 512:tt * 512 + ntok], start=(kc == 0), stop=(kc == kcn - 1)), reads=rd, writes=[tps])
            return ps, tps

        def mm_tm(wv, wtk, cw, kcn, actT, tact, t, pst=None, first=True, last=True, kc0=0):
            ps, tps = pst if pst is not None else bank()
            rd = [wtk, tact[t]]
            for kc in range(kcn):
                P.op("tensor", lambda e, ps=ps, kc=kc: e.matmul(ps[:, 0:cw], lhsT=actT[:, kc0 + kc, t * 128:(t + 1) * 128], rhs=wv[:, kc, 0:cw], start=(first and kc == 0), stop=(last and kc == kcn - 1)), reads=rd, writes=[tps])
            return ps, tps

        def load_bcast(dst, src_row, tk, n):
            P.dma("sync", lambda e: e.dma_start(out=dst, in_=src_row.to_broadcast([128, n])), writes=[tk], sem_tok=tk)

        def make_phase(pst, TSUB, nw=3, nx=2):
            ph = {}
            ph["actT"] = sb("actT", [128, KC, TSUB], BF16, pst)
            ph["tact"] = [P.tok("act%d" % i) for i in range(TSUB // 128)]
            ph["wrot"] = Rot([(sb("wb%d" % i, [128, 8192], BF16, pst), P.dtok("wb%d" % i, sw=True)) for i in range(nw)])
            ph["xrot"] = Rot([(sb("xt%d" % i, [128, D], F32, pst), P.dtok("xt%d" % i)) for i in range(nx)])
            ph["xnrot"] = Rot([(sb("xn%d" % i, [128, D], BF16, pst), P.tok("xn%d" % i)) for i in range(2)])
            ph["junk"] = (sb("junk", [128, D], BF16, pst), P.tok("junk"))
            ph["ssrot"] = Rot([(sb("ss%d" % i, [128, 2], F32, pst), P.tok("ss%d" % i)) for i in range(4)])
            ph["gb"] = (sb("gb", [128, D], F32, pst), P.dtok("gb"))
            ph["stb"] = Rot([(sb("stb%d" % i, [128, 512], BF16, pst), P.dtok("stb%d" % i)) for i in range(4)])
            ph["stf"] = Rot([(sb("stf%d" % i, [128, 512], F32, pst), P.dtok("stf%d" % i)) for i in range(4)])
            return ph

        dumps = {}

        def dump(name, ap, shape, tk):
            if not dbg:
                return
            dt_ = nc.dram_tensor("d_" + name, list(shape), F32, kind="ExternalOutput").ap()
            tkd = P.dtok("dump")
            P.dma("gpsimd", lambda e: e.dma_start(out=dt_, in_=ap), reads=[tk], sem_tok=tkd)

        def store(dst, src, tsrc):
            P.dma("sync", lambda e: e.dma_start(out=dst, in_=src), reads=[tsrc], sem_tok=tsrc)

        def phase_A(l, xsrc):
            with contextlib.ExitStack() as pst:
                ph = make_phase(pst, TS)
                actT, tact = ph["actT"], ph["tact"]
                gb, tgb = ph["gb"]
                load_bcast(gb[:], p_n1[l], tgb, D)
                qg = sb("qg", [128, 2], F32, pst)
                tqg = P.dtok("qg")
                P.dma("sync", lambda e: e.dma_start(out=qg[:, 0:1], in_=p_qg[l]), writes=[tqg], sem_tok=tqg)
                P.dma("sync", lambda e: e.dma_start(out=qg[:, 1:2], in_=p_kg[l]), writes=[tqg], sem_tok=tqg)
                P.op("vector", lambda e: e.tensor_scalar(out=qg[:, 1:2], in0=qg[:, 1:2], scalar1=float(np.sqrt(128.0)), scalar2=None, op0=ALU.mult), reads=[tqg], writes=[tqg])
                sq = [(sb("sq%d" % i, [128, 512], BF16, pst), P.tok("sq%d" % i)) for i in range(2)]
                sqrot = Rot(sq)
                W = w_in[l]
                tnone = P.tok("none")
                for s in range(NS):
                    g0 = s * TS
                    norm_transpose(ph, xsrc, tnone, gb, tgb, g0, TS, actT, tact)
                    NTT = TS // 512
                    for nb in range(4):
                        wv, wtk = load_w(ph, W, 16, 0, C_Z + nb * 512, 512)
                        for t in range(NTS):
                            ps, tps = mm_tm(wv, wtk, 512, 16, actT, tact, t)
                            stg, tst = ph["stb"].next()
                            P.op("scalar", lambda e, stg=stg, ps=ps: e.activation(out=stg[:], in_=ps[:], func=AF.Silu), reads=[tps], writes=[tst])
                            store(zs[g0 + t * 128:g0 + (t + 1) * 128, nb * 512:(nb + 1) * 512], stg[:], tst)
                    for nb in range(6):
                        wv, wtk = load_w(ph, W, 16, 0, C_XBC + nb * 512, 512)
                        for j in range(4):
                            for tt in range(NTT):
                                ps, tps = mm_fm(wv, wtk, j, 16, actT, tact, tt)
                                stg, tst = ph["stb"].next()
                                P.op("vector", lambda e, stg=stg, ps=ps: e.tensor_copy(out=stg[:], in_=ps[:]), reads=[tps], writes=[tst])
                                c0 = nb * 512 + j * 128
                                store(xbc_pre[c0:c0 + 128, g0 + tt * 512:g0 + (tt + 1) * 512], stg[:], tst)
                    wv, wtk = load_w(ph, W, 16, 0, C_DT, 64)
                    for t in range(NTS):
                        ps, tps = mm_tm(wv, wtk, 64, 16, actT, tact, t)
                        stg, tst = ph["stf"].next()
                        P.op("vector", lambda e, stg=stg, ps=ps: e.tensor_copy(out=stg[:, 0:64], in_=ps[:, 0:64]), reads=[tps], writes=[tst])
                        store(dts[g0 + t * 128:g0 + (t + 1) * 128, :], stg[:, 0:64], tst)
                    for qk in range(2):
                        dstT = qT if qk == 0 else kT
                        for nb in range(3):
                            wv, wtk = load_w(ph, W, 16, 0, (C_Q if qk == 0 else C_K) + nb * 512, 512)
                            for j in range(4):
                                h = nb * 4 + j
                                for tt in range(NTT):
                                    ps, tps = mm_fm(wv, wtk, j, 16, actT, tact, tt)
                                    sqb, tsq = sqrot.next()
                                    P.op("scalar", lambda e, sqb=sqb, ps=ps: e.activation(out=sqb[:], in_=ps[:], func=AF.Square), reads=[tps], writes=[tsq])
                                    ps2, tps2 = bank()
                                    P.op("tensor", lambda e, ps2=ps2, sqb=sqb: e.matmul(ps2[:], lhsT=onesb[:], rhs=sqb[:], start=True, stop=True), reads=[tsq], writes=[tps2])
                                    rr, trr = ph["stf"].next()
                                    P.op("scalar", lambda e, rr=rr, ps2=ps2: e.activation(out=rr[:], in_=ps2[:], func=AF.Sqrt, bias=128.0 * EPS), reads=[tps2], writes=[trr])
                                    P.op("vector", lambda e, rr=rr: e.reciprocal(out=rr[:], in_=rr[:]), reads=[trr], writes=[trr])
                                    stg, tst = ph["stb"].next()
                                    P.op("vector", lambda e, stg=stg, ps=ps, rr=rr, qk=qk: e.scalar_tensor_tensor(out=stg[:], in0=ps[:], scalar=qg[:, qk:qk + 1], in1=rr[:], op0=ALU.mult, op1=ALU.mult), reads=[tps, trr, tqg], writes=[tst])
                                    store(dstT[h, :, g0 + tt * 512:g0 + (tt + 1) * 512], stg[:], tst)
                    for nb in range(3):
                        wv, wtk = load_w(ph, W, 16, 0, C_V + nb * 512, 512)
                        for t in range(NTS):
                            ps, tps = mm_tm(wv, wtk, 512, 16, actT, tact, t)
                            stg, tst = ph["stb"].next()
                            P.op("scalar", lambda e, stg=stg, ps=ps: e.activation(out=stg[:], in_=ps[:], func=AF.Copy), reads=[tps], writes=[tst])
                            store(vs[g0 + t * 128:g0 + (t + 1) * 128, nb * 512:(nb + 1) * 512], stg[:], tst)
                    for nb in range(4):
                        wa, wta = load_w(ph, W, 16, 0, C_GLU + nb * 512, 512)
                        wg, wtg = load_w(ph, W, 16, 0, C_GLU + D + nb * 512, 512)
                        for j in range(4):
                            for tt in range(NTT):
                                psg, tpsg = mm_fm(wg, wtg, j, 16, actT, tact, tt)
                                sg, tsg = ph["stf"].next()
                                P.op("scalar", lambda e, sg=sg, psg=psg: e.activation(out=sg[:], in_=psg[:], func=AF.Sigmoid), reads=[tpsg], writes=[tsg])
                                psa, tpsa = mm_fm(wa, wta, j, 16, actT, tact, tt)
                                stg, tst = ph["stb"].next()
                                P.op("vector", lambda e, stg=stg, psa=psa, sg=sg: e.tensor_tensor(out=stg[:], in0=psa[:], in1=sg[:], op=ALU.mult), reads=[tpsa, tsg], writes=[tst])
                                c0 = nb * 512 + j * 128
                                store(uT[c0:c0 + 128, g0 + tt * 512:g0 + (tt + 1) * 512], stg[:], tst)
                    for nb in range(12):
                        wv, wtk = load_w(ph, W, 16, 0, C_GATE + nb * 512, 512)
                        for j in range(4):
                            for tt in range(NTT):
                                ps, tps = mm_fm(wv, wtk, j, 16, actT, tact, tt)
                                stg, tst = ph["stb"].next()
                                P.op("scalar", lambda e, stg=stg, ps=ps: e.activation(out=stg[:], in_=ps[:], func=AF.Sigmoid), reads=[tps], writes=[tst])
                                c0 = nb * 512 + j * 128
                                store(gT[c0:c0 + 128, g0 + tt * 512:g0 + (tt + 1) * 512], stg[:], tst)
                P.barrier()

        def phase_ssd(l):
            with contextlib.ExitStack() as pst:
                cw = sb("cw", [128, 24, 5], F32, pst)
                cbv = sb("cbv", [128, 24], F32, pst)
                tcw = P.dtok("cw")
                P.dma("sync", lambda e: e.dma_start(out=cw[:], in_=p_cw[l]), writes=[tcw], sem_tok=tcw)
                P.dma("sync", lambda e: e.dma_start(out=cbv[:], in_=p_cb[l]), writes=[tcw], sem_tok=tcw)
                cins = [(sb("cin%d" % i, [128, T + 4], BF16, pst), P.dtok("cin%d" % i)) for i in range(2)]
                for cin, tcin in cins:
                    P.op("vector", lambda e, cin=cin: e.memset(cin[:, 0:2], 0.0), writes=[tcin])
                    P.op("vector", lambda e, cin=cin: e.memset(cin[:, T + 2:T + 4], 0.0), writes=[tcin])
                cinrot = Rot(cins)
                accrot = Rot([(sb("cacc%d" % i, [128, T], F32, pst), P.tok("cacc%d" % i)) for i in range(2)])
                xcrot = Rot([(sb("xc%d" % i, [128, T], BF16, pst), P.dtok("xc%d" % i)) for i in range(2)])
                for c in range(24):
                    cin, tcin = cinrot.next()
                    P.dma("sync", lambda e, cin=cin, c=c: e.dma_start(out=cin[:, 2:2 + T], in_=xbc_pre[c * 128:(c + 1) * 128, :]), writes=[tcin], sem_tok=tcin)
                    acc, tacc = accrot.next()
                    veng = "vector"
                    P.op(veng, lambda e, acc=acc, cin=cin, c=c: e.tensor_scalar(out=acc[:], in0=cin[:, 0:T], scalar1=cw[:, c, 0:1], scalar2=cbv[:, c:c + 1], op0=ALU.mult, op1=ALU.add), reads=[tcin, tcw], writes=[tacc])
                    for k in range(1, 5):
                        P.op(veng, lambda e, acc=acc, cin=cin, c=c, k=k: e.scalar_tensor_tensor(out=acc[:], in0=cin[:, k:k + T], scalar=cw[:, c, k:k + 1], in1=acc[:], op0=ALU.mult, op1=ALU.add), reads=[tcin, tacc], writes=[tacc])
                    xc, txc = xcrot.next()
                    P.op("scalar", lambda e, xc=xc, acc=acc: e.activation(out=xc[:], in_=acc[:], func=AF.Silu), reads=[tacc], writes=[txc])
                    store(xbcT[c * 128:(c + 1) * 128, :], xc[:], txc)
                P.barrier()
            with contextlib.ExitStack() as pst:
                adt = sb("adt", [128, NT, 64], F32, pst)
                dec = sb("dec", [128, NT, 64], F32, pst)
                adh = sb("adh", [128, NT, 64], BF16, pst)
                adl = sb("adl", [128, NT, 64], BF16, pst)
                biasd = [sb("biasd%d" % i, [128, NT, 32], F32, pst) for i in range(2)]
                wgt = [sb("wgt%d" % i, [128, NT, 32], F32, pst) for i in range(2)]
                esc = [sb("esc%d" % i, [128, NT, 32], F32, pst) for i in range(2)]
                dskb = sb("dskb", [128, 32], F32, pst)
                DI = sb("DI", [128, 32, 128], BF16, pst)
                tprep = P.dtok("prep")
                with contextlib.ExitStack() as pst2:
                    dtr = sb("dtr", [128, NT, 64], F32, pst2)
                    dtv = sb("dtv", [128, NT, 64], F32, pst2)
                    lndt = sb("lndt", [128, NT, 64], F32, pst2)
                    cs = [sb("cs%d" % i, [128, NT, 64], F32, pst2) for i in range(4)]
                    dtb = sb("dtb", [128, 64], F32, pst2)
                    alg = sb("alg", [128, 64], F32, pst2)
                    P.dma("sync", lambda e: e.dma_start(out=dtr[:], in_=dts.rearrange("(c p) j -> p c j", p=128)), writes=[tprep], sem_tok=tprep)
                    load_bcast(dtb[:], p_dtb[l], tprep, 64)
                    load_bcast(alg[:], p_alog[l], tprep, 64)
                    load_bcast(dskb[:], p_dsk[l], tprep, 32)
                    tp = [tprep]
                    P.op("vector", lambda e: e.tensor_tensor(out=dtr[:], in0=dtr[:], in1=dtb[:].unsqueeze(1).to_broadcast([128, NT, 64]), op=ALU.add), reads=tp, writes=tp)
                    P.op("scalar", lambda e: e.activation(out=dtr[:], in_=dtr[:], func=AF.Exp), reads=tp, writes=tp)
                    P.op("scalar", lambda e: e.activation(out=dtv[:], in_=dtr[:], func=AF.Ln, bias=1.0), reads=tp, writes=tp)
                    sm = cs[0]
                    mk = cs[1]
                    P.op("vector", lambda e: e.tensor_scalar(out=sm[:], in0=dtr[:], scalar1=-0.25, scalar2=1.0 / 3.0, op0=ALU.mult, op1=ALU.add), reads=tp, writes=tp)
                    P.op("vector", lambda e: e.tensor_tensor(out=sm[:], in0=sm[:], in1=dtr[:], op=ALU.mult), reads=tp, writes=tp)
                    P.op("vector", lambda e: e.tensor_scalar(out=sm[:], in0=sm[:], scalar1=-0.5, scalar2=None, op0=ALU.add), reads=tp, writes=tp)
                    P.op("vector", lambda e: e.tensor_tensor(out=sm[:], in0=sm[:], in1=dtr[:], op=ALU.mult), reads=tp, writes=tp)
                    P.op("vector", lambda e: e.tensor_scalar(out=sm[:], in0=sm[:], scalar1=1.0, scalar2=None, op0=ALU.add), reads=tp, writes=tp)
                    P.op("vector", lambda e: e.tensor_tensor(out=sm[:], in0=sm[:], in1=dtr[:], op=ALU.mult), reads=tp, writes=tp)
                    P.op("vector", lambda e: e.tensor_scalar(out=mk[:], in0=dtr[:], scalar1=0.1, scalar2=None, op0=ALU.is_lt), reads=tp, writes=tp)
                    P.op("vector", lambda e: e.tensor_tensor(out=sm[:], in0=sm[:], in1=dtv[:], op=ALU.subtract), reads=tp, writes=tp)
                    P.op("vector", lambda e: e.tensor_tensor(out=sm[:], in0=sm[:], in1=mk[:], op=ALU.mult), reads=tp, writes=tp)
                    P.op("vector", lambda e: e.tensor_tensor(out=dtv[:], in0=dtv[:], in1=sm[:], op=ALU.add), reads=tp, writes=tp)
                    dump("dt", dtv[:], [128, NT, 64], tprep)
                    P.op("scalar", lambda e: e.activation(out=lndt[:], in_=dtv[:], func=AF.Ln), reads=tp, writes=tp)
                    P.op("scalar", lambda e: e.activation(out=alg[:], in_=alg[:], func=AF.Exp), reads=tp, writes=tp)
                    P.op("vector", lambda e: e.scalar_tensor_tensor(out=adt[:], in0=dtv[:], scalar=-1.0, in1=alg[:].unsqueeze(1).to_broadcast([128, NT, 64]), op0=ALU.mult, op1=ALU.mult), reads=tp, writes=tp)
                    P.op("vector", lambda e: e.tensor_copy(out=adh[:], in_=adt[:]), reads=tp, writes=tp)
                    P.op("vector", lambda e: e.tensor_tensor(out=dtr[:], in0=adt[:], in1=adh[:], op=ALU.subtract), reads=tp, writes=tp)
                    P.op("vector", lambda e: e.tensor_copy(out=adl[:], in_=dtr[:]), reads=tp, writes=tp)
                    adf = adt[:].rearrange("p c j -> p (c j)")
                    npc = (NT * 64 + 511) // 512
                    for m in range(5):
                        dstt = (cs[m] if m < 4 else dec)[:].rearrange("p c j -> p (c j)")
                        for pc in range(npc):
                            n0 = pc * 512
                            n1 = min(NT * 64, n0 + 512)
                            ps, tps = bank()
                            lh = tri[:, m, :] if m < 4 else onesf[:]
                            P.op("tensor", lambda e, ps=ps, lh=lh, n0=n0, n1=n1: e.matmul(ps[:, 0:n1 - n0], lhsT=lh, rhs=adf[:, n0:n1], start=True, stop=True), reads=tp + [tconst], writes=[tps])
                            if m < 4:
                                P.op("scalar", lambda e, ps=ps, dstt=dstt, n0=n0, n1=n1: e.activation(out=dstt[:, n0:n1], in_=ps[:, 0:n1 - n0], func=AF.Copy), reads=[tps], writes=tp)
                            else:
                                P.op("scalar", lambda e, ps=ps, dstt=dstt, n0=n0, n1=n1: e.activation(out=dstt[:, n0:n1], in_=ps[:, 0:n1 - n0], func=AF.Exp), reads=[tps], writes=tp)
                    P.op("vector", lambda e: e.tensor_tensor(out=biasd[0][:], in0=lndt[:, :, 0:32], in1=cs[0][:, :, 0:32], op=ALU.subtract), reads=tp, writes=tp)
                    P.op("vector", lambda e: e.tensor_tensor(out=wgt[0][:], in0=cs[1][:, :, 0:32], in1=lndt[:, :, 0:32], op=ALU.add), reads=tp, writes=tp)
                    P.op("scalar", lambda e: e.activation(out=wgt[0][:], in_=wgt[0][:], func=AF.Exp), reads=tp, writes=tp)
                    P.op("scalar", lambda e: e.activation(out=esc[0][:], in_=cs[0][:, :, 0:32], func=AF.Exp), reads=tp, writes=tp)
                    P.op("vector", lambda e: e.tensor_tensor(out=biasd[1][:], in0=cs[2][:, :, 32:64], in1=lndt[:, :, 32:64], op=ALU.add), reads=tp, writes=tp)
                    P.op("scalar", lambda e: e.activation(out=wgt[1][:], in_=biasd[1][:], func=AF.Exp), reads=tp, writes=tp)
                    P.op("scalar", lambda e: e.activation(out=esc[1][:], in_=cs[3][:, :, 32:64], func=AF.Exp), reads=tp, writes=tp)
                    dump("lndt", lndt[:], [128, NT, 64], tprep)
                    dump("adt", adt[:], [128, NT, 64], tprep)
                    dump("dec", dec[:], [128, NT, 64], tprep)
                    dump("bias1", biasd[1][:], [128, NT, 32], tprep)
                    dump("wgt1", wgt[1][:], [128, NT, 32], tprep)
                    dump("esc1", esc[1][:], [128, NT, 32], tprep)
                    dump("cs2", cs[2][:], [128, NT, 64], tprep)
                    for h in range(32):
                        P.op("vector", lambda e, h=h: e.tensor_scalar(out=DI[:, h, :], in0=identf[:], scalar1=dskb[:, h:h + 1], scalar2=None, op0=ALU.mult), reads=tp, writes=tp)
                    P.barrier()
                xcl = Rot([(sb("xcl%d" % i, [128, 24, 128], BF16, pst), P.dtok("xcl%d" % i)) for i in range(2)])
                xtmr = Rot([(sb("xtm%d" % i, [128, 2560], BF16, pst), P.tok("xtm%d" % i)) for i in range(2)])
                xwr = Rot([(sb("xw%d" % i, [128, 2048], BF16, pst), P.tok("xw%d" % i)) for i in range(2)])
                ltr = Rot([(sb("lt%d" % i, [128, 8, 128], BF16, pst), P.tok("lt%d" % i)) for i in range(2)])
                mpr = Rot([(sb("mp%d" % i, [128, 8, 128], BF16, pst), P.tok("mp%d" % i)) for i in range(2)])
                cbr = Rot([(sb("cbs%d" % i, [128, 128], F32, pst), P.tok("cbs%d" % i)) for i in range(2)])
                hst = sb("hst", [128, 2048], F32, pst)
                hbf = sb("hbf", [128, 2048], BF16, pst)
                thst = [P.tok("hst%d" % g) for g in range(4)]
                thbf = [P.tok("hbf%d" % g) for g in range(4)]
                ytr = Rot([(sb("ytmp%d" % i, [128, 512], F32, pst), P.tok("ytmp%d" % i)) for i in range(2)])
                yaccr = Rot([(sb("yacc%d" % i, [128, 2048], F32, pst), P.dtok("yacc%d" % i)) for i in range(2)])
                yblr = Rot([(sb("ybl%d" % i, [128, 2048], F32, pst), P.dtok("ybl%d" % i)) for i in range(1)])
                ztr = Rot([(sb("zt%d" % i, [128, 2048], BF16, pst), P.dtok("zt%d" % i)) for i in range(1)])
                xbv = xbcT.rearrange("(k p) t -> p k t", p=128)
                if dbg:
                    dlt = nc.dram_tensor("d_lt", [128, 1024], BF16, kind="ExternalOutput").ap()
                    dmp = nc.dram_tensor("d_mp", [128, 1024], BF16, kind="ExternalOutput").ap()
                    dcb = nc.dram_tensor("d_cb", [128, 128], F32, kind="ExternalOutput").ap()
                    tdbg = P.dtok("dbgst")
                for dr_ in (1, 0):
                    P.op("vector", lambda e: e.memset(hst[:], 0.0), writes=thst)
                    P.op("vector", lambda e: e.memset(hbf[:], 0.0), writes=thbf)
                    order = range(NT) if dr_ == 0 else range(NT - 1, -1, -1)
                    mtri = 0 if dr_ == 0 else 2
                    for c in order:
                        xa, txa = xcl.next()
                        P.dma("sync", lambda e, xa=xa, c=c: e.dma_start(out=xa[:], in_=xbv[:, :, c * 128:(c + 1) * 128]), writes=[txa], sem_tok=txa)
                        xtm, txtm = xtmr.next()
                        for part in range(3):
                            ps, tps = bank()
                            psb = ps[:].bitcast(BF16)
                            nk = 8 if part < 2 else 4
                            for k in range(nk):
                                P.op("tensor", lambda e, psb=psb, xa=xa, k=k, part=part: e.transpose(psb[:, k * 128:(k + 1) * 128], xa[:, part * 8 + k, :], identb[:]), reads=[txa], writes=[tps])
                            if part == 1:
                                P.op("vector", lambda e, xtm=xtm, psb=psb, part=part, nk=nk: e.tensor_copy(out=xtm[:, part * 1024:part * 1024 + nk * 128], in_=psb[:, 0:nk * 128]), reads=[tps], writes=[txtm])
                            else:
                                P.op("scalar", lambda e, xtm=xtm, psb=psb, part=part, nk=nk: e.activation(out=xtm[:, part * 1024:part * 1024 + nk * 128], in_=psb[:, 0:nk * 128], func=AF.Copy), reads=[tps], writes=[txtm])
                        xw, txw = xwr.next()
                        P.op("gpsimd", lambda e, xw=xw, xtm=xtm, c=c, dr_=dr_: e.tensor_tensor(out=xw[:].rearrange("p (h d) -> p h d", d=64), in0=xtm[:, 0:2048].rearrange("p (h d) -> p h d", d=64), in1=wgt[dr_][:, c, :].unsqueeze(2).to_broadcast([128, 32, 64]), op=ALU.mult), reads=[txtm], writes=[txw])
                        yacc, tyacc = yaccr.next()
                        for g in range(4):
                            psc, tpsc = bank()
                            P.op("tensor", lambda e, psc=psc, xa=xa, g=g: e.matmul(psc[:, 0:128], lhsT=xa[:, 16 + g, :], rhs=xa[:, 20 + g, :], start=True, stop=True), reads=[txa], writes=[tpsc])
                            cbs, tcbs = cbr.next()
                            P.op("scalar", lambda e, cbs=cbs, psc=psc: e.activation(out=cbs[:], in_=psc[:, 0:128], func=AF.Copy), reads=[tpsc], writes=[tcbs])
                            lt, tlt = ltr.next()
                            pds = [bank(), bank()]
                            for hh in range(8):
                                h = g * 8 + hh
                                pd, tpd = pds[hh // 4]
                                tgt = pd[:, (hh % 4) * 128:(hh % 4 + 1) * 128]
                                col = dr_ * 32 + h
                                P.op("tensor", lambda e, tgt=tgt, c=c, col=col, mtri=mtri: e.matmul(tgt, lhsT=adh[:, c, col:col + 1].to_broadcast([128, 128]), rhs=trib[:, mtri, :], start=True, stop=False), writes=[tpd])
                                P.op("tensor", lambda e, tgt=tgt, c=c, col=col, mtri=mtri: e.matmul(tgt, lhsT=adl[:, c, col:col + 1].to_broadcast([128, 128]), rhs=trib[:, mtri, :], start=False, stop=False), writes=[tpd])
                                P.op("tensor", lambda e, tgt=tgt, dr_=dr_: e.matmul(tgt, lhsT=identb[:], rhs=negb[:, dr_, :], start=False, stop=True), writes=[tpd])
                            for hh in range(8):
                                h = g * 8 + hh
                                pd, tpd = pds[hh // 4]
                                tgt = pd[:, (hh % 4) * 128:(hh % 4 + 1) * 128]
                                P.op("scalar", lambda e, lt=lt, hh=hh, tgt=tgt, c=c, h=h, dr_=dr_: e.activation(out=lt[:, hh, :], in_=tgt, func=AF.Exp, bias=biasd[dr_][:, c, h:h + 1], scale=(1.0 if dr_ == 0 else -1.0)), reads=[tpd], writes=[tlt], self_waw_ok=True)
                            mp, tmp_ = mpr.next()
                            P.op("vector", lambda e, mp=mp, lt=lt, cbs=cbs: e.tensor_tensor(out=mp[:], in0=lt[:], in1=cbs[:].unsqueeze(1).to_broadcast([128, 8, 128]), op=ALU.mult), reads=[tlt, tcbs], writes=[tmp_])
                            if dbg and dr_ == 1 and c == NT - 1 and g == 0:
                                P.dma("sync", lambda e, lt=lt: e.dma_start(out=dlt, in_=lt[:].rearrange("p a b -> p (a b)")), reads=[tlt], sem_tok=tdbg)
                                P.dma("sync", lambda e, mp=mp: e.dma_start(out=dmp, in_=mp[:].rearrange("p a b -> p (a b)")), reads=[tmp_], sem_tok=tdbg)
                                P.dma("sync", lambda e, cbs=cbs: e.dma_start(out=dcb, in_=cbs[:]), reads=[tcbs], sem_tok=tdbg)
                            psy, tpsy = bank()
                            for hh in range(8):
                                h = g * 8 + hh
                                P.op("tensor", lambda e, psy=psy, mp=mp, hh=hh, h=h, xtm=xtm, dr_=dr_: e.matmul(psy[:, hh * 64:(hh + 1) * 64], lhsT=mp[:, hh, :], rhs=xtm[:, h * 64:(h + 1) * 64], start=True, stop=(dr_ == 1), skip_group_check=True), reads=[tmp_, txtm], writes=[tpsy])
                                if dr_ == 0:
                                    P.op("tensor", lambda e, psy=psy, hh=hh, h=h, xtm=xtm: e.matmul(psy[:, hh * 64:(hh + 1) * 64], lhsT=DI[:, h, :], rhs=xtm[:, h * 64:(h + 1) * 64], start=False, stop=True, skip_group_check=True), reads=[txtm], writes=[tpsy])
                            pso, tpso = bank()
                            P.op("tensor", lambda e, pso=pso, xa=xa, g=g: e.matmul(pso[:], lhsT=xa[:, 20 + g, :], rhs=hbf[:, g * 512:(g + 1) * 512], start=True, stop=True), reads=[txa, thbf[g]], writes=[tpso])
                            yt, tyt = ytr.next()
                            P.op("vector", lambda e, yt=yt, pso=pso, c=c, g=g, dr_=dr_: e.tensor_tensor(out=yt[:].rearrange("p (h d) -> p h d", d=64), in0=pso[:].rearrange("p (h d) -> p h d", d=64), in1=esc[dr_][:, c, g * 8:(g + 1) * 8].unsqueeze(2).to_broadcast([128, 8, 64]), op=ALU.mult), reads=[tpso], writes=[tyt])
                            P.op("vector", lambda e, yacc=yacc, psy=psy, yt=yt, g=g: e.tensor_tensor(out=yacc[:, g * 512:(g + 1) * 512], in0=psy[:], in1=yt[:], op=ALU.add), reads=[tpsy, tyt], writes=[tyacc], self_waw_ok=True)
                            pss, tpss = bank()
                            P.op("tensor", lambda e, pss=pss, xtm=xtm, xw=xw, g=g: e.matmul(pss[:], lhsT=xtm[:, 2048 + g * 128:2048 + (g + 1) * 128], rhs=xw[:, g * 512:(g + 1) * 512], start=True, stop=True), reads=[txtm, txw], writes=[tpss])
                            hs = hst[:, g * 512:(g + 1) * 512]
                            P.op("vector", lambda e, hs=hs, c=c, g=g, dr_=dr_: e.tensor_tensor(out=hs.rearrange("p (h d) -> p h d", d=64), in0=hs.rearrange("p (h d) -> p h d", d=64), in1=dec[:, c, dr_ * 32 + g * 8:dr_ * 32 + (g + 1) * 8].unsqueeze(2).to_broadcast([128, 8, 64]), op=ALU.mult), reads=[thst[g]], writes=[thst[g]])
                            P.op("vector", lambda e, hs=hs, pss=pss: e.tensor_tensor(out=hs, in0=hs, in1=pss[:], op=ALU.add), reads=[thst[g], tpss], writes=[thst[g]])
                            P.op("gpsimd", lambda e, hs=hs, g=g: e.tensor_copy(out=hbf[:, g * 512:(g + 1) * 512], in_=hs), reads=[thst[g]], writes=[thbf[g]])
                        rows = slice(c * 128, (c + 1) * 128)
                        if dr_ == 1:
                            store(yb[rows, :], yacc[:], tyacc)
                        else:
                            ybl, tybl = yblr.next()
                            zt, tzt = ztr.next()
                            P.dma("sync", lambda e, ybl=ybl, rows=rows: e.dma_start(out=ybl[:], in_=yb[rows, :]), writes=[tybl], sem_tok=tybl)
                            P.dma("sync", lambda e, zt=zt, rows=rows: e.dma_start(out=zt[:], in_=zs[rows, :]), writes=[tzt], sem_tok=tzt)
                            P.op("gpsimd", lambda e, yacc=yacc, ybl=ybl: e.tensor_tensor(out=yacc[:], in0=yacc[:], in1=ybl[:], op=ALU.add), reads=[tyacc, tybl], writes=[tyacc])
                            P.op("gpsimd", lambda e, yacc=yacc, zt=zt: e.tensor_tensor(out=yacc[:], in0=yacc[:], in1=zt[:], op=ALU.mult), reads=[tyacc, tzt], writes=[tyacc])
                            store(yg[rows, :], yacc[:], tyacc)
                    P.barrier()


        def phase_attn(l):
            with contextlib.ExitStack() as pst:
                qr = Rot([(sb("qsb%d" % i, [128, T], BF16, pst), P.dtok("qsb%d" % i)) for i in range(2)])
                kr = Rot([(sb("ksb%d" % i, [128, T], BF16, pst), P.dtok("ksb%d" % i)) for i in range(2)])
                vr = Rot([(sb("vt%d" % i, [128, NT, 128], BF16, pst), P.dtok("vt%d" % i)) for i in range(2)])
                num = sb("num", [128, T], F32, pst)
                den = sb("den", [128, T], F32, pst)
                tnum, tden = P.tok("num"), P.tok("den")
                osb = sb("osb", [128, T], BF16, pst)
                tosb = P.dtok("osb")
                ptr = Rot([(sb("pT%d" % i, [128, 128], BF16, pst), P.tok("pT%d" % i)) for i in range(3)])
                for j in range(4):
                    for g in range(3):
                        h = 4 * g + j
                        d = ATTN_PATTERNS[g][1]
                        S = T // d
                        NK = S // 128
                        qs, tqs = qr.next()
                        ks, tks = kr.next()
                        vt, tvt = vr.next()
                        P.dma("sync", lambda e, qs=qs, h=h: e.dma_start(out=qs[:], in_=qT[h]), writes=[tqs], sem_tok=tqs)
                        P.dma("sync", lambda e, ks=ks, h=h: e.dma_start(out=ks[:], in_=kT[h]), writes=[tks], sem_tok=tks)
                        vv = vt[:].rearrange("a (r kt) e -> a r kt e", r=d)
                        for r in range(d):
                            P.dma("sync", lambda e, vv=vv, h=h, d=d, r=r: e.dma_start(out=vv[:, r], in_=vs[:, h * 128:(h + 1) * 128].rearrange("(kt a r) e -> a r kt e", a=128, r=d)[:, r]), writes=[tvt], sem_tok=tvt)
                        for r in range(d):
                            for m in range(NK + 1):
                                b0 = 64 if m == 0 else 0
                                b1 = 64 if m == NK else 128
                                nq = b1 - b0
                                i0 = 128 * m - 64 + b0
                                q0 = i0 * d + r
                                qsl = qs[:, q0:q0 + (nq - 1) * d + 1:d]
                                pso, tpso = bank()
                                psd, tpsd = bank()
                                kts = [kt for kt in (m - 1, m) if 0 <= kt < NK]
                                for idx, kt in enumerate(kts):
                                    typ = 0 if kt == m - 1 else 1
                                    pss, tpss = bank()
                                    k0 = 128 * kt * d + r
                                    ksl = ks[:, k0:k0 + 127 * d + 1:d]
                                    P.op("tensor", lambda e, pss=pss, ksl=ksl, qsl=qsl, nq=nq: e.matmul(pss[:, 0:nq], lhsT=ksl, rhs=qsl, start=True, stop=False), reads=[tqs, tks], writes=[tpss])
                                    P.op("tensor", lambda e, pss=pss, nq=nq, h=h, typ=typ, b0=b0, b1=b1: e.matmul(pss[:, 0:nq], lhsT=identf[:], rhs=abias[:, h * 2 + typ, b0:b1], start=False, stop=True), writes=[tpss])
                                    pt, tpt = ptr.next()
                                    P.op("scalar", lambda e, pt=pt, pss=pss, nq=nq: e.activation(out=pt[:, 0:nq], in_=pss[:, 0:nq], func=AF.Exp), reads=[tpss], writes=[tpt])
                                    fl = dict(start=(idx == 0), stop=(idx == len(kts) - 1))
                                    P.op("tensor", lambda e, pso=pso, vv=vv, r=r, kt=kt, pt=pt, nq=nq, fl=fl: e.matmul(pso[:, 0:nq], lhsT=vv[:, r, kt, :], rhs=pt[:, 0:nq], **fl), reads=[tvt, tpt], writes=[tpso])
                                    P.op("tensor", lambda e, psd=psd, pt=pt, nq=nq, fl=fl: e.matmul(psd[:, 0:nq], lhsT=onesb[:], rhs=pt[:, 0:nq], **fl), reads=[tpt], writes=[tpsd])
                                nsl = num[:, q0:q0 + (nq - 1) * d + 1:d]
                                dsl = den[:, q0:q0 + (nq - 1) * d + 1:d]
                                if g == 0:
                                    P.op("vector", lambda e, nsl=nsl, pso=pso, nq=nq: e.tensor_copy(out=nsl, in_=pso[:, 0:nq]), reads=[tpso], writes=[tnum], self_waw_ok=True)
                                    P.op("scalar", lambda e, dsl=dsl, psd=psd, nq=nq: e.activation(out=dsl, in_=psd[:, 0:nq], func=AF.Copy), reads=[tpsd], writes=[tden], self_waw_ok=True)
                                else:
                                    P.op("vector", lambda e, nsl=nsl, pso=pso, nq=nq: e.tensor_tensor(out=nsl, in0=nsl, in1=pso[:, 0:nq], op=ALU.add), reads=[tpso], writes=[tnum], self_waw_ok=True)
                                    P.op("vector", lambda e, dsl=dsl, psd=psd, nq=nq: e.tensor_tensor(out=dsl, in0=dsl, in1=psd[:, 0:nq], op=ALU.add), reads=[tpsd], writes=[tden], self_waw_ok=True)
                    P.op("vector", lambda e: e.reciprocal(out=den[:], in_=den[:]), reads=[tden], writes=[tden])
                    P.op("vector", lambda e: e.tensor_tensor(out=osb[:], in0=num[:], in1=den[:], op=ALU.mult), reads=[tnum, tden], writes=[tosb])
                    store(oT[j * 128:(j + 1) * 128, :], osb[:], tosb)
                P.barrier()

        def phase_conf(l):
            with contextlib.ExitStack() as pst:
                dwp = sb("dwp", [128, 16, 31], F32, pst)
                dwb = sb("dwb", [128, 16], F32, pst)
                tdw = P.dtok("dw")
                P.dma("sync", lambda e: e.dma_start(out=dwp[:], in_=p_dw[l]), writes=[tdw], sem_tok=tdw)
                P.dma("sync", lambda e: e.dma_start(out=dwb[:], in_=p_dwb[l]), writes=[tdw], sem_tok=tdw)
                cins = [(sb("ccin%d" % i, [128, T + 30], BF16, pst), P.dtok("ccin%d" % i)) for i in range(2)]
                for cin, tcin in cins:
                    P.op("vector", lambda e, cin=cin: e.memset(cin[:, 0:15], 0.0), writes=[tcin])
                    P.op("vector", lambda e, cin=cin: e.memset(cin[:, T + 15:T + 30], 0.0), writes=[tcin])
                cinrot = Rot(cins)
                dgr = Rot([(sb("dg%d" % i, [128, 31, 128], BF16, pst), P.tok("dg%d" % i)) for i in range(2)])
                str_ = Rot([(sb("cst%d" % i, [128, 512], BF16, pst), P.dtok("cst%d" % i)) for i in range(3)])
                for cc in range(16):
                    cin, tcin = cinrot.next()
                    P.dma("sync", lambda e, cin=cin, cc=cc: e.dma_start(out=cin[:, 15:15 + T], in_=uT[cc * 128:(cc + 1) * 128, :]), writes=[tcin], sem_tok=tcin)
                    dg, tdg = dgr.next()
                    for k in range(31):
                        eng = "vector" if k % 2 == 0 else "gpsimd"
                        P.op(eng, lambda e, dg=dg, k=k, cc=cc: e.tensor_scalar(out=dg[:, k, :], in0=identf[:], scalar1=dwp[:, cc, k:k + 1], scalar2=None, op0=ALU.mult), reads=[tdw], writes=[tdg], self_waw_ok=True)
                    P.op("vector", lambda e, dg=dg: e.tensor_copy(out=dg[:, 30, 0:1], in_=dg[:, 30, 0:1]), reads=[tdg], writes=[tdg])
                    for tt in range(T // 512):
                        ps, tps = bank()
                        for k in range(31):
                            P.op("tensor", lambda e, ps=ps, dg=dg, cin=cin, k=k, tt=tt: e.matmul(ps[:], lhsT=dg[:, k, :], rhs=cin[:, tt * 512 + k:tt * 512 + k + 512], start=(k == 0), stop=(k == 30)), reads=[tdg, tcin], writes=[tps])
                        stg, tst = str_.next()
                        P.op("scalar", lambda e, stg=stg, ps=ps, cc=cc: e.activation(out=stg[:], in_=ps[:], func=AF.Identity, bias=dwb[:, cc:cc + 1]), reads=[tps, tdw], writes=[tst])
                        store(ycT[cc * 128:(cc + 1) * 128, tt * 512:(tt + 1) * 512], stg[:], tst)
                P.barrier()
            with contextlib.ExitStack() as pst:
                lng = sb("lng", [128, 16], F32, pst)
                lnb = sb("lnb", [128, 16], F32, pst)
                tln = P.dtok("ln")
                P.dma("sync", lambda e: e.dma_start(out=lng[:], in_=p_lng[l]), writes=[tln], sem_tok=tln)
                P.dma("sync", lambda e: e.dma_start(out=lnb[:], in_=p_lnb[l]), writes=[tln], sem_tok=tln)
                ylr = Rot([(sb("yl%d" % i, [128, 16, 512], BF16, pst), P.dtok("yl%d" % i)) for i in range(2)])
                sqr = Rot([(sb("sqy%d" % i, [128, 16, 512], BF16, pst), P.tok("sqy%d" % i)) for i in range(1)])
                mur = Rot([(sb("mu%d" % i, [128, 3, 512], F32, pst), P.tok("mu%d" % i)) for i in range(2)])
                t1r = Rot([(sb("t1%d" % i, [128, 512], F32, pst), P.tok("t1%d" % i)) for i in range(3)])
                str_ = Rot([(sb("cst2%d" % i, [128, 512], BF16, pst), P.dtok("cst2%d" % i)) for i in range(3)])
                ycv = ycT.rearrange("(k p) t -> p k t", p=128)
                for tt in range(T // 512):
                    yl, tyl = ylr.next()
                    P.dma("sync", lambda e, yl=yl, tt=tt: e.dma_start(out=yl[:], in_=ycv[:, :, tt * 512:(tt + 1) * 512]), writes=[tyl], sem_tok=tyl)
                    sqy, tsq = sqr.next()
                    P.op("gpsimd", lambda e, sqy=sqy, yl=yl: e.tensor_tensor(out=sqy[:], in0=yl[:], in1=yl[:], op=ALU.mult), reads=[tyl], writes=[tsq])
                    ps1, tps1 = bank()
                    ps2, tps2 = bank()
                    for cc in range(16):
                        P.op("tensor", lambda e, ps1=ps1, yl=yl, cc=cc: e.matmul(ps1[:], lhsT=onesb[:], rhs=yl[:, cc, :], start=(cc == 0), stop=(cc == 15)), reads=[tyl], writes=[tps1])
                    for cc in range(16):
                        P.op("tensor", lambda e, ps2=ps2, sqy=sqy, cc=cc: e.matmul(ps2[:], lhsT=onesb[:], rhs=sqy[:, cc, :], start=(cc == 0), stop=(cc == 15)), reads=[tsq], writes=[tps2])
                    mu, tmu = mur.next()
                    P.op("scalar", lambda e, mu=mu, ps1=ps1: e.activation(out=mu[:, 0, :], in_=ps1[:], func=AF.Copy, scale=1.0 / D), reads=[tps1], writes=[tmu])
                    P.op("vector", lambda e, mu=mu: e.tensor_tensor(out=mu[:, 1, :], in0=mu[:, 0, :], in1=mu[:, 0, :], op=ALU.mult), reads=[tmu], writes=[tmu])
                    P.op("vector", lambda e, mu=mu, ps2=ps2: e.scalar_tensor_tensor(out=mu[:, 2, :], in0=ps2[:], scalar=1.0 / D, in1=mu[:, 1, :], op0=ALU.mult, op1=ALU.subtract), reads=[tmu, tps2], writes=[tmu])
                    P.op("scalar", lambda e, mu=mu: e.activation(out=mu[:, 2, :], in_=mu[:, 2, :], func=AF.Sqrt, bias=EPS), reads=[tmu], writes=[tmu])
                    P.op("vector", lambda e, mu=mu: e.reciprocal(out=mu[:, 2, :], in_=mu[:, 2, :]), reads=[tmu], writes=[tmu])
                    for cc in range(16):
                        t1, tt1 = t1r.next()
                        eng = "vector" if cc % 2 == 0 else "gpsimd"
                        P.op(eng, lambda e, t1=t1, yl=yl, mu=mu, cc=cc: e.tensor_tensor(out=t1[:], in0=yl[:, cc, :], in1=mu[:, 0, :], op=ALU.subtract), reads=[tyl, tmu], writes=[tt1])
                        P.op(eng, lambda e, t1=t1, mu=mu: e.tensor_tensor(out=t1[:], in0=t1[:], in1=mu[:, 2, :], op=ALU.mult), reads=[tt1, tmu], writes=[tt1])
                        stg, tst = str_.next()
                        P.op("scalar", lambda e, stg=stg, t1=t1, cc=cc: e.activation(out=stg[:], in_=t1[:], func=AF.Silu, bias=lnb[:, cc:cc + 1], scale=lng[:, cc:cc + 1]), reads=[tt1, tln], writes=[tst])
                        store(cT[cc * 128:(cc + 1) * 128, tt * 512:(tt + 1) * 512], stg[:], tst)
                P.barrier()

        def phase_B(l, xsrc):
            with contextlib.ExitStack() as pst:
                TB = min(1024, T)
                NTB = TB // 128
                ph = make_phase(pst, TB)
                actT, tact = ph["actT"], ph["tact"]
                gb, tgb = ph["gb"]
                load_bcast(gb[:], p_ng[l], tgb, D)
                mergedT = sb("mergedT", [128, KC, TB], BF16, pst)
                NTT = TB // 512
                tm = [P.tok("mer%d" % i) for i in range(NTT)]
                tmer_list = [tm[t // 4] for t in range(NTB)]
                glr = Rot([(sb("gl%d" % i, [128, 512], BF16, pst), P.dtok("gl%d" % i)) for i in range(3)])
                tactld = P.dtok("actld")
                tnone = P.tok("none")

                def branch(W, kcn, br, first, g0):
                    for nb in range(4):
                        wv, wtk = load_w(ph, W, kcn, 0, nb * 512, 512)
                        for j in range(4):
                            cb = nb * 4 + j
                            for tt in range(NTT):
                                gl, tgl = glr.next()
                                P.dma("sync", lambda e, gl=gl, cb=cb, tt=tt: e.dma_start(out=gl[:], in_=gT[br * D + cb * 128:br * D + (cb + 1) * 128, g0 + tt * 512:g0 + (tt + 1) * 512]), writes=[tgl], sem_tok=tgl)
                                ps, tps = mm_fm(wv, wtk, j, kcn, actT, tact, tt)
                                msl = mergedT[:, cb, tt * 512:(tt + 1) * 512]
                                if first:
                                    P.op("vector", lambda e, msl=msl, ps=ps, gl=gl: e.tensor_tensor(out=msl, in0=ps[:], in1=gl[:], op=ALU.mult), reads=[tps, tgl], writes=[tm[tt]], self_waw_ok=True)
                                else:
                                    tmpb, ttmp = ph["stf"].next()
                                    P.op("vector", lambda e, tmpb=tmpb, ps=ps, gl=gl: e.tensor_tensor(out=tmpb[:], in0=ps[:], in1=gl[:], op=ALU.mult), reads=[tps, tgl], writes=[ttmp])
                                    P.op("gpsimd", lambda e, msl=msl, tmpb=tmpb: e.tensor_tensor(out=msl, in0=msl, in1=tmpb[:], op=ALU.add), reads=[ttmp, tm[tt]], writes=[tm[tt]])

                for s in range(T // TB):
                    g0 = s * TB
                    norm_transpose(ph, yg, tnone, gb, tgb, g0, TB, actT, tact)
                    branch(w_ssd_o[l], 16, 0, True, g0)
                    for kc in range(4):
                        P.dma("sync", lambda e, kc=kc, g0=g0: e.dma_start(out=actT[:, kc, :], in_=oT[kc * 128:(kc + 1) * 128, g0:g0 + TB]), writes=tact, sem_tok=tactld)
                    branch(w_attn_o[l], 4, 1, False, g0)
                    for kc in range(16):
                        P.dma("sync", lambda e, kc=kc, g0=g0: e.dma_start(out=actT[:, kc, :], in_=cT[kc * 128:(kc + 1) * 128, g0:g0 + TB]), writes=tact, sem_tok=tactld)
                    branch(w_conv_o[l], 16, 2, False, g0)
                    for nb in range(4):
                        wv, wtk = load_w(ph, w_out[l], 16, 0, nb * 512, 512)
                        for t in range(NTB):
                            xo, txo = ph["stf"].next()
                            rows = slice(g0 + t * 128, g0 + (t + 1) * 128)
                            P.dma("sync", lambda e, xo=xo, rows=rows, nb=nb: e.dma_start(out=xo[:], in_=xsrc[rows, nb * 512:(nb + 1) * 512]), writes=[txo], sem_tok=txo)
                            ps, tps = mm_tm(wv, wtk, 512, 16, mergedT, tmer_list, t)
                            P.op("vector", lambda e, xo=xo, ps=ps: e.tensor_tensor(out=xo[:], in0=xo[:], in1=ps[:], op=ALU.add), reads=[tps, txo], writes=[txo])
                            store(xres[rows, nb * 512:(nb + 1) * 512], xo[:], txo)
                P.barrier()

        def phase_mlp(l, dst):
            with contextlib.ExitStack() as pst:
                ph = make_phase(pst, 512, nw=2, nx=4)
                actT, tact = ph["actT"], ph["tact"]
                gb, tgb = ph["gb"]
                load_bcast(gb[:], p_n2[l], tgb, D)
                aT = sb("aT", [128, 64, 512], BF16, pst)
                ta = [P.tok("aT%d" % i) for i in range(4)]
                rr = Rot([(sb("relu%d" % i, [128, 512], BF16, pst), P.tok("relu%d" % i)) for i in range(3)])
                tnone = P.tok("none")
                for ti in range(T // 512):
                    r0 = ti * 512
                    i0 = ph["xrot"].i
                    norm_transpose(ph, xres, tnone, gb, tgb, r0, 512, actT, tact)
                    xts = [ph["xrot"].items[(i0 + t) % 4] for t in range(4)]
                    for nb in range(16):
                        wv, wtk = load_w(ph, w_up[l], 16, 0, nb * 512, 512)
                        for j in range(4):
                            hc = nb * 4 + j
                            ps, tps = mm_fm(wv, wtk, j, 16, actT, tact, 0)
                            rl, trl = rr.next()
                            P.op("scalar", lambda e, rl=rl, ps=ps: e.activation(out=rl[:], in_=ps[:], func=AF.Relu), reads=[tps], writes=[trl])
                            P.op("gpsimd", lambda e, rl=rl, hc=hc: e.tensor_tensor(out=aT[:, hc, :], in0=rl[:], in1=rl[:], op=ALU.mult), reads=[trl], writes=[ta[hc // 16]], self_waw_ok=True)
                    for nb in range(4):
                        b4 = [bank() for _ in range(4)]
                        for kq in range(4):
                            wv, wtk = load_w(ph, w_down[l], 16, kq * 2048, nb * 512, 512)
                            for t in range(4):
                                mm_tm(wv, wtk, 512, 16, aT, [ta[kq]] * 4, t, pst=b4[t], first=(kq == 0), last=(kq == 3), kc0=kq * 16)
                        for t in range(4):
                            xt, txt = xts[t]
                            ps, tps = b4[t]
                            P.op("vector", lambda e, xt=xt, ps=ps, nb=nb: e.tensor_tensor(out=xt[:, nb * 512:(nb + 1) * 512], in0=xt[:, nb * 512:(nb + 1) * 512], in1=ps[:], op=ALU.add), reads=[tps, txt], writes=[txt])
                    for t in range(4):
                        xt, txt = xts[t]
                        store(dst[r0 + t * 128:r0 + (t + 1) * 128, :], xt[:], txt)
                P.barrier()

        for l in range(depth):
            xsrc = x_in if l == 0 else xres
            for fn, args in [(phase_A, (l, xsrc)), (phase_ssd, (l,)), (phase_attn, (l,)), (phase_conf, (l,)),
                             (phase_B, (l, xsrc)), (phase_mlp, (l, out if l == depth - 1 else xres))]:
                if phases is not None and fn.__name__ not in phases:
                    continue
                P.begin_phase()
                fn(*args)
                P.end_phase()
        P.emit(block)
    return nc


_WNAMES = ["w_in", "w_ssd_o", "w_attn_o", "w_conv_o", "w_out", "w_mlp_up", "w_mlp_down"]


def run_cores(x_list, inputs, T, depth=DEPTH, dbg=False, phases=None):
    nc = build(T, depth=depth, dbg=dbg, phases=phases)
    consts = host_consts()
    lay = host_layout(inputs)
    common = {}
    for k in _WNAMES:
        common[k] = np.ascontiguousarray(inputs[k][:depth])
    for k, v in lay.items():
        common[k] = np.ascontiguousarray(v[:depth])
    common.update(consts)
    in_maps = []
    for xb in x_list:
        m = dict(common)
        m["x"] = np.ascontiguousarray(xb)
        in_maps.append(m)
    res = run_bass_kernel_spmd(nc, in_maps, core_ids=list(range(len(x_list))))
    return res.results


def kernel(**inputs):
    x = np.asarray(inputs["x"], dtype=np.float32)
    B, T, _ = x.shape
    inp = {k: np.asarray(v, dtype=np.float32) for k, v in inputs.items()}
    x_list = [x[c % B] for c in range(8)]
    results = run_cores(x_list, inp, T)
    return np.stack([results[b]["out"] for b in range(B)], axis=0).astype(np.float32)
```

```python
import contextlib
import numpy as np
import concourse.bass as bass
import concourse.mybir as mybir
from concourse.bass_utils import run_bass_kernel_spmd

F32 = mybir.dt.float32
BF16 = mybir.dt.bfloat16
AF = mybir.ActivationFunctionType
ALU = mybir.AluOpType

ENGS = ["tensor", "vector", "scalar", "gpsimd", "sync"]
EPOCH = 30000

D = 2048
KC = 16
N_IN = 20032
C_Z, C_XBC, C_DT, C_Q, C_K, C_V, C_GLU, C_GATE = 0, 2048, 5120, 5184, 6720, 8256, 9792, 13888
EPS = 1e-6
DEPTH = 2


class Tok:
    __slots__ = ("name", "w", "r", "dsem")

    def __init__(self, name=""):
        self.name = name
        self.w = None
        self.r = {}
        self.dsem = None


class Prog:
    def __init__(self, nc, stack):
        self.nc = nc
        self.stack = stack
        self.q = {e: [] for e in ENGS}
        self.cnt = {e: 0 for e in ENGS}
        self.sems = {}
        self.waited = {e: {} for e in ENGS}
        self.nsem = 0
        self.dcount = {}
        self.free_dsems = {False: [], True: []}
        self.phase_sems = None

    def begin_phase(self):
        self.phase_sems = []

    def end_phase(self):
        for sw, k in self.phase_sems:
            self.free_dsems[sw].append(k)
        self.phase_sems = None

    def _sem(self, key):
        if key not in self.sems:
            self.sems[key] = self.stack.enter_context(self.nc.semaphore("s%d" % self.nsem))
            self.nsem += 1
        return self.sems[key]

    def tok(self, name=""):
        return Tok(name)

    def dtok(self, name="", sw=False):
        t = Tok(name)
        if self.free_dsems[sw]:
            t.dsem = self.free_dsems[sw].pop()
        else:
            t.dsem = ("d", self.nsem, name)
            self._sem(t.dsem)
            self.dcount[t.dsem] = 0
        if self.phase_sems is not None:
            self.phase_sems.append((sw, t.dsem))
        return t

    def _deps(self, eng, reads, writes, self_waw_ok):
        deps = {}

        def add(d):
            if d is None:
                return
            k, v = d
            if deps.get(k, 0) < v:
                deps[k] = v
        for t in reads:
            add(t.w)
        for t in writes:
            if t.w is not None:
                if not (self_waw_ok and t.w[0][0] == "e" and t.w[0][1] == eng):
                    add(t.w)
            for k, v in t.r.items():
                if k[0] == "e" and k[1] == eng:
                    continue
                add((k, v))
        out = []
        wd = self.waited[eng]
        for k, v in deps.items():
            if eng == "tensor" and k[0] == "e" and k[1] == "tensor":
                continue
            if wd.get(k, 0) >= v:
                continue
            wd[k] = v
            out.append((k, v))
        return out

    def _mark(self, comp, reads, writes):
        k, v = comp
        for t in reads:
            if t.r.get(k, 0) < v:
                t.r[k] = v
        for t in writes:
            t.w = comp
            t.r = {}

    def op(self, eng, fn, reads=(), writes=(), self_waw_ok=False):
        waits = self._deps(eng, reads, writes, self_waw_ok)
        self.cnt[eng] += 1
        ep, v = divmod(self.cnt[eng] - 1, EPOCH)
        key = ("e", eng, ep)
        comp = (key, v + 1)
        self._sem(key)
        self.q[eng].append((waits, fn, key, 1))
        self._mark(comp, reads, writes)
        return comp

    def dma(self, eng, fn, reads=(), writes=(), sem_tok=None):
        waits = self._deps(eng, reads, writes, False)
        key = sem_tok.dsem
        self.dcount[key] += 16
        comp = (key, self.dcount[key])
        self.q[eng].append((waits, fn, key, 16))
        self._mark(comp, reads, writes)
        return comp

    def barrier(self):
        allw = []
        for e in ENGS:
            if self.cnt[e] > 0:
                ep, v = divmod(self.cnt[e] - 1, EPOCH)
                allw.append((("e", e, ep), v + 1))
        for k, v in self.dcount.items():
            if v > 0:
                allw.append((k, v))
        for e in ENGS:
            wd = self.waited[e]
            waits = []
            for k, v in allw:
                if k[0] == "e" and k[1] == e:
                    continue
                if wd.get(k, 0) >= v:
                    continue
                wd[k] = v
                waits.append((k, v))
            if waits:
                self.q[e].append((waits, None, None, 0))

    def emit(self, block):
        fin = [(k, v) for k, v in self.dcount.items() if v > 0]
        sems = self.sems
        q = self.q

        def runner(e):
            def run(engine):
                for waits, fn, key, inc in q[e]:
                    for k, v in waits:
                        engine.wait_ge(sems[k], v)
                    if fn is None:
                        continue
                    ins = fn(engine)
                    ins.then_inc(sems[key], inc)
                if e == "sync":
                    for k, v in fin:
                        engine.wait_ge(sems[k], v)
            return run
        block.tensor(runner("tensor"))
        block.vector(runner("vector"))
        block.scalar(runner("scalar"))
        block.gpsimd(runner("gpsimd"))
        block.sync(runner("sync"))


class Rot:
    def __init__(self, items):
        self.items = items
        self.i = 0

    def next(self):
        it = self.items[self.i % len(self.items)]
        self.i += 1
        return it


ATTN_PATTERNS = ((128, 1), (512, 4), (2048, 16))


def host_consts():
    c = {}
    i = np.arange(128)
    lp, l = i[:, None], i[None, :]
    c["c_ident"] = np.eye(128, dtype=np.float32)
    tri = np.stack([(lp <= l), (lp > l), (lp < l), (lp >= l)], axis=1).astype(np.float32)
    c["c_tri"] = np.ascontiguousarray(tri)
    s, ll = i[:, None], i[None, :]
    neg = np.stack([np.where(ll < s, -30000.0, 0.0), np.where(ll > s, 30000.0, 0.0)], axis=1).astype(np.float32)
    c["c_neg"] = np.ascontiguousarray(neg)
    slopes = np.exp2(-8.0 * np.arange(1, 13, dtype=np.float64) / 12.0)
    a, b = i[:, None], i[None, :]
    ab = np.zeros((128, 12, 2, 128), np.float32)
    for h in range(12):
        d = ATTN_PATTERNS[h // 4][1]
        relA = a - b - 64
        relB = a - b + 64
        ab[:, h, 0, :] = np.where(a >= b, -slopes[h] * d * np.abs(relA), -30000.0)
        ab[:, h, 1, :] = np.where(a <= b, -slopes[h] * d * np.abs(relB), -30000.0)
    c["c_abias"] = ab.reshape(128, 24, 128)
    return c


def host_layout(inp):
    L = DEPTH
    o = {}
    o["p_cw"] = np.ascontiguousarray(inp["ssd_conv_w"].transpose(0, 2, 1).reshape(L, 24, 128, 5).transpose(0, 2, 1, 3))
    o["p_cb"] = np.ascontiguousarray(inp["ssd_conv_b"].reshape(L, 24, 128).transpose(0, 2, 1))
    o["p_dw"] = np.ascontiguousarray(inp["conv_dw_w"].transpose(0, 2, 1).reshape(L, 16, 128, 31).transpose(0, 2, 1, 3))
    o["p_dwb"] = np.ascontiguousarray(inp["conv_dw_b"].reshape(L, 16, 128).transpose(0, 2, 1))
    o["p_lng"] = np.ascontiguousarray(inp["conv_ln_g"].reshape(L, 16, 128).transpose(0, 2, 1))
    o["p_lnb"] = np.ascontiguousarray(inp["conv_ln_b"].reshape(L, 16, 128).transpose(0, 2, 1))
    o["p_dtb"] = np.ascontiguousarray(inp["ssd_dt_bias"].reshape(L, 1, 64))
    o["p_alog"] = np.ascontiguousarray(inp["ssd_a_log"].reshape(L, 1, 64))
    o["p_dsk"] = np.ascontiguousarray(inp["ssd_d"].reshape(L, 1, 32))
    o["p_qg"] = np.ascontiguousarray(inp["q_norm_g"].reshape(L, 128, 1))
    o["p_kg"] = np.ascontiguousarray(inp["k_norm_g"].reshape(L, 128, 1))
    for k in ["norm1_g", "norm2_g", "ssd_norm_g"]:
        o["p_" + k] = np.ascontiguousarray(inp[k].reshape(L, 1, D))
    return o


def build(T, depth=DEPTH, dbg=False, phases=None):
    nc = bass.Bass("TRN2", target_bir_lowering=False)
    TS = min(2048, T)
    NS = T // TS
    NT = T // 128
    NTS = TS // 128

    def din(name, shape, dt=F32):
        return nc.dram_tensor(name, list(shape), dt, kind="ExternalInput").ap()

    def dscr(name, shape, dt):
        return nc.dram_tensor(name, list(shape), dt, kind=("ExternalOutput" if dbg else "Internal")).ap()

    x_in = din("x", [T, D])
    w_in = din("w_in", [depth, D, N_IN])
    w_ssd_o = din("w_ssd_o", [depth, D, D])
    w_attn_o = din("w_attn_o", [depth, 512, D])
    w_conv_o = din("w_conv_o", [depth, D, D])
    w_out = din("w_out", [depth, D, D])
    w_up = din("w_mlp_up", [depth, D, 4 * D])
    w_down = din("w_mlp_down", [depth, 4 * D, D])
    p_cw = din("p_cw", [depth, 128, 24, 5])
    p_cb = din("p_cb", [depth, 128, 24])
    p_dw = din("p_dw", [depth, 128, 16, 31])
    p_dwb = din("p_dwb", [depth, 128, 16])
    p_lng = din("p_lng", [depth, 128, 16])
    p_lnb = din("p_lnb", [depth, 128, 16])
    p_dtb = din("p_dtb", [depth, 1, 64])
    p_alog = din("p_alog", [depth, 1, 64])
    p_dsk = din("p_dsk", [depth, 1, 32])
    p_qg = din("p_qg", [depth, 128, 1])
    p_kg = din("p_kg", [depth, 128, 1])
    p_n1 = din("p_norm1_g", [depth, 1, D])
    p_n2 = din("p_norm2_g", [depth, 1, D])
    p_ng = din("p_ssd_norm_g", [depth, 1, D])
    c_ident = din("c_ident", [128, 128])
    c_tri = din("c_tri", [128, 4, 128])
    c_neg = din("c_neg", [128, 2, 128])
    c_abias = din("c_abias", [128, 24, 128])
    out = nc.dram_tensor("out", [T, D], F32, kind="ExternalOutput").ap()

    xres = dscr("xres", [T, D], F32)
    zs = dscr("zs", [T, D], BF16)
    xbc_pre = dscr("xbc_pre", [3072, T], BF16)
    xbcT = dscr("xbcT", [3072, T], BF16)
    dts = dscr("dts", [T, 64], F32)
    qT = dscr("qT", [12, 128, T], BF16)
    kT = dscr("kT", [12, 128, T], BF16)
    vs = dscr("vs", [T, 1536], BF16)
    uT = dscr("uT", [D, T], BF16)
    gT = dscr("gT", [3 * D, T], BF16)
    yb = dscr("yb", [T, D], F32)
    yg = dscr("yg", [T, D], F32)
    oT = dscr("oT", [512, T], BF16)
    ycT = dscr("ycT", [D, T], BF16)
    cT = dscr("cT", [D, T], BF16)

    with contextlib.ExitStack() as st:
        P = Prog(nc, st)

        uniq = [0]

        def sb(name, shape, dt, stack=st):
            uniq[0] += 1
            return stack.enter_context(nc.sbuf_tensor("%s_%d" % (name, uniq[0]), list(shape), dt))

        pbanks = []
        for i in range(8):
            pbanks.append((st.enter_context(nc.psum_tensor("pb%d" % i, [128, 512], F32)), P.tok("pb%d" % i)))
        prot = Rot(pbanks)
        bank = prot.next

        identf = sb("identf", [128, 128], F32)
        identb = sb("identb", [128, 128], BF16)
        onesb = sb("onesb", [128, 128], BF16)
        onesf = sb("onesf", [128, 128], F32)
        tri = sb("tri", [128, 4, 128], F32)
        negb = sb("negb", [128, 2, 128], BF16)
        trib = sb("trib", [128, 4, 128], BF16)
        abias = sb("abias", [128, 24, 128], F32)
        tconst = P.dtok("const")
        tconst2 = P.dtok("const2", sw=True)

        block = st.enter_context(nc.Block())

        P.dma("sync", lambda e: e.dma_start(out=identf[:], in_=c_ident[:]), writes=[tconst], sem_tok=tconst)
        P.dma("sync", lambda e: e.dma_start(out=tri[:], in_=c_tri[:]), writes=[tconst], sem_tok=tconst)
        P.dma("sync", lambda e: e.dma_start(out=abias[:], in_=c_abias[:]), writes=[tconst], sem_tok=tconst)
        P.dma("gpsimd", lambda e: e.dma_start(out=negb[:], in_=c_neg[:]), writes=[tconst2], sem_tok=tconst2)
        P.dma("gpsimd", lambda e: e.dma_start(out=identb[:], in_=c_ident[:]), writes=[tconst2], sem_tok=tconst2)
        P.dma("gpsimd", lambda e: e.dma_start(out=trib[:], in_=c_tri[:]), writes=[tconst2], sem_tok=tconst2)
        P.op("vector", lambda e: e.memset(onesb[:], 1.0), writes=[tconst])
        P.op("vector", lambda e: e.memset(onesf[:], 1.0), writes=[tconst])
        P.barrier()

        def rms_rstd(eng_ss, ss, rstd, tss, n):
            P.op("scalar", lambda e: e.activation(out=rstd, in_=ss, func=AF.Sqrt, bias=EPS, scale=1.0 / n), reads=[tss], writes=[tss])
            P.op("vector", lambda e: e.reciprocal(out=rstd, in_=rstd), reads=[tss], writes=[tss])

        def norm_transpose(ph, src, tsrc, gb, tgb, row0, ntok, actT, tact):
            for t in range(ntok // 128):
                xt, txt = ph["xrot"].next()
                r0 = row0 + t * 128
                P.dma("sync", lambda e, xt=xt, r0=r0: e.dma_start(out=xt[:], in_=src[r0:r0 + 128, :]), reads=[tsrc], writes=[txt], sem_tok=txt)
                xn, txn = ph["xnrot"].next()
                ss, tss = ph["ssrot"].next()
                P.op("scalar", lambda e, xt=xt, xn=xn, ss=ss: e.activation(out=xn[:], in_=xt[:], func=AF.Square, accum_out=ss[:, 0:1]), reads=[txt], writes=[txn, tss])
                rstd = ss[:, 1:2]
                rms_rstd("vector", ss[:, 0:1], rstd, tss, D)
                P.op("vector", lambda e, xt=xt, xn=xn, rstd=rstd: e.scalar_tensor_tensor(out=xn[:], in0=xt[:], scalar=rstd, in1=gb[:], op0=ALU.mult, op1=ALU.mult), reads=[txt, tss, tgb], writes=[txn])
                for half in range(2):
                    ps, tps = bank()
                    psb = ps[:].bitcast(BF16)
                    for k in range(8):
                        kc = half * 8 + k
                        P.op("tensor", lambda e, psb=psb, xn=xn, k=k, kc=kc: e.transpose(psb[:, k * 128:(k + 1) * 128], xn[:, kc * 128:(kc + 1) * 128], identb[:]), reads=[txn], writes=[tps])
                    dst = actT[:, half * 8:half * 8 + 8, t * 128:(t + 1) * 128]
                    srcp = psb[:, 0:1024].rearrange("p (k c) -> p k c", c=128)
                    if half == 0:
                        P.op("scalar", lambda e, dst=dst, srcp=srcp: e.activation(out=dst, in_=srcp, func=AF.Copy), reads=[tps], writes=[tact[t]])
                    else:
                        P.op("vector", lambda e, dst=dst, srcp=srcp: e.tensor_copy(out=dst, in_=srcp), reads=[tps], writes=[tact[t]])

        def load_w(ph, W2d, kcn, r0, c0, cw):
            buf, tk = ph["wrot"].next()
            src = W2d[r0:r0 + kcn * 128, c0:c0 + cw].rearrange("(kc p) n -> p kc n", p=128)
            dst = buf[:, 0:kcn * cw].rearrange("p (kc c) -> p kc c", c=cw)
            P.dma("gpsimd", lambda e: e.dma_start(out=dst, in_=src), writes=[tk], sem_tok=tk)
            return dst, tk

        def mm_fm(wv, wtk, j, kcn, actT, tact, tt, ntok=512):
            ps, tps = bank()
            rd = [wtk] + tact[(tt * 512) // 128:(tt * 512 + ntok) // 128]
            for kc in range(kcn):
                P.op("tensor", lambda e, ps=ps, kc=kc: e.matmul(ps[:, 0:ntok], lhsT=wv[:, kc, j * 128:(j + 1) * 128], rhs=actT[:, kc, tt * 512:tt * 512 + ntok], start=(kc == 0), stop=(kc == kcn - 1)), reads=rd, writes=[tps])
            return ps, tps

        def mm_tm(wv, wtk, cw, kcn, actT, tact, t, pst=None, first=True, last=True, kc0=0):
            ps, tps = pst if pst is not None else bank()
            rd = [wtk, tact[t]]
            for kc in range(kcn):
                P.op("tensor", lambda e, ps=ps, kc=kc: e.matmul(ps[:, 0:cw], lhsT=actT[:, kc0 + kc, t * 128:(t + 1) * 128], rhs=wv[:, kc, 0:cw], start=(first and kc == 0), stop=(last and kc == kcn - 1)), reads=rd, writes=[tps])
            return ps, tps

        def load_bcast(dst, src_row, tk, n):
            P.dma("sync", lambda e: e.dma_start(out=dst, in_=src_row.to_broadcast([128, n])), writes=[tk], sem_tok=tk)

        def make_phase(pst, TSUB, nw=3, nx=2):
            ph = {}
            ph["actT"] = sb("actT", [128, KC, TSUB], BF16, pst)
            ph["tact"] = [P.tok("act%d" % i) for i in range(TSUB // 128)]
            ph["wrot"] = Rot([(sb("wb%d" % i, [128, 8192], BF16, pst), P.dtok("wb%d" % i, sw=True)) for i in range(nw)])
            ph["xrot"] = Rot([(sb("xt%d" % i, [128, D], F32, pst), P.dtok("xt%d" % i)) for i in range(nx)])
            ph["xnrot"] = Rot([(sb("xn%d" % i, [128, D], BF16, pst), P.tok("xn%d" % i)) for i in range(2)])
            ph["ssrot"] = Rot([(sb("ss%d" % i, [128, 2], F32, pst), P.tok("ss%d" % i)) for i in range(4)])
            ph["gb"] = (sb("gb", [128, D], F32, pst), P.dtok("gb"))
            ph["stb"] = Rot([(sb("stb%d" % i, [128, 512], BF16, pst), P.dtok("stb%d" % i)) for i in range(4)])
            ph["stf"] = Rot([(sb("stf%d" % i, [128, 512], F32, pst), P.dtok("stf%d" % i)) for i in range(4)])
            return ph

        dumps = {}

        def dump(name, ap, shape, tk):
            if not dbg:
                return
            dt_ = nc.dram_tensor("d_" + name, list(shape), F32, kind="ExternalOutput").ap()
            tkd = P.dtok("dump")
            P.dma("gpsimd", lambda e: e.dma_start(out=dt_, in_=ap), reads=[tk], sem_tok=tkd)

        def store(dst, src, tsrc):
            P.dma("sync", lambda e: e.dma_start(out=dst, in_=src), reads=[tsrc], sem_tok=tsrc)

        def phase_A(l, xsrc):
            with contextlib.ExitStack() as pst:
                ph = make_phase(pst, TS)
                actT, tact = ph["actT"], ph["tact"]
                gb, tgb = ph["gb"]
                load_bcast(gb[:], p_n1[l], tgb, D)
                qg = sb("qg", [128, 2], F32, pst)
                tqg = P.dtok("qg")
                P.dma("sync", lambda e: e.dma_start(out=qg[:, 0:1], in_=p_qg[l]), writes=[tqg], sem_tok=tqg)
                P.dma("sync", lambda e: e.dma_start(out=qg[:, 1:2], in_=p_kg[l]), writes=[tqg], sem_tok=tqg)
                P.op("vector", lambda e: e.tensor_scalar(out=qg[:, 1:2], in0=qg[:, 1:2], scalar1=float(np.sqrt(128.0)), scalar2=None, op0=ALU.mult), reads=[tqg], writes=[tqg])
                sq = [(sb("sq%d" % i, [128, 512], BF16, pst), P.tok("sq%d" % i)) for i in range(2)]
                sqrot = Rot(sq)
                W = w_in[l]
                tnone = P.tok("none")
                for s in range(NS):
                    g0 = s * TS
                    norm_transpose(ph, xsrc, tnone, gb, tgb, g0, TS, actT, tact)
                    NTT = TS // 512
                    for nb in range(4):
                        wv, wtk = load_w(ph, W, 16, 0, C_Z + nb * 512, 512)
                        for t in range(NTS):
                            ps, tps = mm_tm(wv, wtk, 512, 16, actT, tact, t)
                            stg, tst = ph["stb"].next()
                            P.op("scalar", lambda e, stg=stg, ps=ps: e.activation(out=stg[:], in_=ps[:], func=AF.Silu), reads=[tps], writes=[tst])
                            store(zs[g0 + t * 128:g0 + (t + 1) * 128, nb * 512:(nb + 1) * 512], stg[:], tst)
                    for nb in range(6):
                        wv, wtk = load_w(ph, W, 16, 0, C_XBC + nb * 512, 512)
                        for j in range(4):
                            for tt in range(NTT):
                                ps, tps = mm_fm(wv, wtk, j, 16, actT, tact, tt)
                                stg, tst = ph["stb"].next()
                                P.op("vector", lambda e, stg=stg, ps=ps: e.tensor_copy(out=stg[:], in_=ps[:]), reads=[tps], writes=[tst])
                                c0 = nb * 512 + j * 128
                                store(xbc_pre[c0:c0 + 128, g0 + tt * 512:g0 + (tt + 1) * 512], stg[:], tst)
                    wv, wtk = load_w(ph, W, 16, 0, C_DT, 64)
                    for t in range(NTS):
                        ps, tps = mm_tm(wv, wtk, 64, 16, actT, tact, t)
                        stg, tst = ph["stf"].next()
                        P.op("vector", lambda e, stg=stg, ps=ps: e.tensor_copy(out=stg[:, 0:64], in_=ps[:, 0:64]), reads=[tps], writes=[tst])
                        store(dts[g0 + t * 128:g0 + (t + 1) * 128, :], stg[:, 0:64], tst)
                    for qk in range(2):
                        dstT = qT if qk == 0 else kT
                        for nb in range(3):
                            wv, wtk = load_w(ph, W, 16, 0, (C_Q if qk == 0 else C_K) + nb * 512, 512)
                            for j in range(4):
                                h = nb * 4 + j
                                for tt in range(NTT):
                                    ps, tps = mm_fm(wv, wtk, j, 16, actT, tact, tt)
                                    sqb, tsq = sqrot.next()
                                    P.op("scalar", lambda e, sqb=sqb, ps=ps: e.activation(out=sqb[:], in_=ps[:], func=AF.Square), reads=[tps], writes=[tsq])
                                    ps2, tps2 = bank()
                                    P.op("tensor", lambda e, ps2=ps2, sqb=sqb: e.matmul(ps2[:], lhsT=onesb[:], rhs=sqb[:], start=True, stop=True), reads=[tsq], writes=[tps2])
                                    rr, trr = ph["stf"].next()
                                    P.op("scalar", lambda e, rr=rr, ps2=ps2: e.activation(out=rr[:], in_=ps2[:], func=AF.Sqrt, bias=128.0 * EPS), reads=[tps2], writes=[trr])
                                    P.op("vector", lambda e, rr=rr: e.reciprocal(out=rr[:], in_=rr[:]), reads=[trr], writes=[trr])
                                    stg, tst = ph["stb"].next()
                                    P.op("vector", lambda e, stg=stg, ps=ps, rr=rr, qk=qk: e.scalar_tensor_tensor(out=stg[:], in0=ps[:], scalar=qg[:, qk:qk + 1], in1=rr[:], op0=ALU.mult, op1=ALU.mult), reads=[tps, trr, tqg], writes=[tst])
                                    store(dstT[h, :, g0 + tt * 512:g0 + (tt + 1) * 512], stg[:], tst)
                    for nb in range(3):
                        wv, wtk = load_w(ph, W, 16, 0, C_V + nb * 512, 512)
                        for t in range(NTS):
                            ps, tps = mm_tm(wv, wtk, 512, 16, actT, tact, t)
                            stg, tst = ph["stb"].next()
                            P.op("scalar", lambda e, stg=stg, ps=ps: e.activation(out=stg[:], in_=ps[:], func=AF.Copy), reads=[tps], writes=[tst])
                            store(vs[g0 + t * 128:g0 + (t + 1) * 128, nb * 512:(nb + 1) * 512], stg[:], tst)
                    for nb in range(4):
                        wa, wta = load_w(ph, W, 16, 0, C_GLU + nb * 512, 512)
                        wg, wtg = load_w(ph, W, 16, 0, C_GLU + D + nb * 512, 512)
                        for j in range(4):
                            for tt in range(NTT):
                                psg, tpsg = mm_fm(wg, wtg, j, 16, actT, tact, tt)
                                sg, tsg = ph["stf"].next()
                                P.op("scalar", lambda e, sg=sg, psg=psg: e.activation(out=sg[:], in_=psg[:], func=AF.Sigmoid), reads=[tpsg], writes=[tsg])
                                psa, tpsa = mm_fm(wa, wta, j, 16, actT, tact, tt)
                                stg, tst = ph["stb"].next()
                                P.op("vector", lambda e, stg=stg, psa=psa, sg=sg: e.tensor_tensor(out=stg[:], in0=psa[:], in1=sg[:], op=ALU.mult), reads=[tpsa, tsg], writes=[tst])
                                c0 = nb * 512 + j * 128
                                store(uT[c0:c0 + 128, g0 + tt * 512:g0 + (tt + 1) * 512], stg[:], tst)
                    for nb in range(12):
                        wv, wtk = load_w(ph, W, 16, 0, C_GATE + nb * 512, 512)
                        for j in range(4):
                            for tt in range(NTT):
                                ps, tps = mm_fm(wv, wtk, j, 16, actT, tact, tt)
                                stg, tst = ph["stb"].next()
                                P.op("scalar", lambda e, stg=stg, ps=ps: e.activation(out=stg[:], in_=ps[:], func=AF.Sigmoid), reads=[tps], writes=[tst])
                                c0 = nb * 512 + j * 128
                                store(gT[c0:c0 + 128, g0 + tt * 512:g0 + (tt + 1) * 512], stg[:], tst)
                P.barrier()

        def phase_ssd(l):
            with contextlib.ExitStack() as pst:
                cw = sb("cw", [128, 24, 5], F32, pst)
                cbv = sb("cbv", [128, 24], F32, pst)
                tcw = P.dtok("cw")
                P.dma("sync", lambda e: e.dma_start(out=cw[:], in_=p_cw[l]), writes=[tcw], sem_tok=tcw)
                P.dma("sync", lambda e: e.dma_start(out=cbv[:], in_=p_cb[l]), writes=[tcw], sem_tok=tcw)
                cins = [(sb("cin%d" % i, [128, T + 4], BF16, pst), P.dtok("cin%d" % i)) for i in range(2)]
                for cin, tcin in cins:
                    P.op("vector", lambda e, cin=cin: e.memset(cin[:, 0:2], 0.0), writes=[tcin])
                    P.op("vector", lambda e, cin=cin: e.memset(cin[:, T + 2:T + 4], 0.0), writes=[tcin])
                cinrot = Rot(cins)
                accrot = Rot([(sb("cacc%d" % i, [128, T], F32, pst), P.tok("cacc%d" % i)) for i in range(2)])
                xcrot = Rot([(sb("xc%d" % i, [128, T], BF16, pst), P.dtok("xc%d" % i)) for i in range(2)])
                for c in range(24):
                    cin, tcin = cinrot.next()
                    P.dma("sync", lambda e, cin=cin, c=c: e.dma_start(out=cin[:, 2:2 + T], in_=xbc_pre[c * 128:(c + 1) * 128, :]), writes=[tcin], sem_tok=tcin)
                    acc, tacc = accrot.next()
                    veng = "vector"
                    P.op(veng, lambda e, acc=acc, cin=cin, c=c: e.tensor_scalar(out=acc[:], in0=cin[:, 0:T], scalar1=cw[:, c, 0:1], scalar2=cbv[:, c:c + 1], op0=ALU.mult, op1=ALU.add), reads=[tcin, tcw], writes=[tacc])
                    for k in range(1, 5):
                        P.op(veng, lambda e, acc=acc, cin=cin, c=c, k=k: e.scalar_tensor_tensor(out=acc[:], in0=cin[:, k:k + T], scalar=cw[:, c, k:k + 1], in1=acc[:], op0=ALU.mult, op1=ALU.add), reads=[tcin, tacc], writes=[tacc])
                    xc, txc = xcrot.next()
                    P.op("scalar", lambda e, xc=xc, acc=acc: e.activation(out=xc[:], in_=acc[:], func=AF.Silu), reads=[tacc], writes=[txc])
                    store(xbcT[c * 128:(c + 1) * 128, :], xc[:], txc)
                P.barrier()
            with contextlib.ExitStack() as pst:
                adt = sb("adt", [128, NT, 64], F32, pst)
                dec = sb("dec", [128, NT, 64], F32, pst)
                adh = sb("adh", [128, NT, 64], BF16, pst)
                adl = sb("adl", [128, NT, 64], BF16, pst)
                biasd = [sb("biasd%d" % i, [128, NT, 32], F32, pst) for i in range(2)]
                wgt = [sb("wgt%d" % i, [128, NT, 32], F32, pst) for i in range(2)]
                esc = [sb("esc%d" % i, [128, NT, 32], F32, pst) for i in range(2)]
                dskb = sb("dskb", [128, 32], F32, pst)
                DI = sb("DI", [128, 32, 128], BF16, pst)
                tprep = P.dtok("prep")
                with contextlib.ExitStack() as pst2:
                    dtr = sb("dtr", [128, NT, 64], F32, pst2)
                    dtv = sb("dtv", [128, NT, 64], F32, pst2)
                    lndt = sb("lndt", [128, NT, 64], F32, pst2)
                    cs = [sb("cs%d" % i, [128, NT, 64], F32, pst2) for i in range(4)]
                    dtb = sb("dtb", [128, 64], F32, pst2)
                    alg = sb("alg", [128, 64], F32, pst2)
                    P.dma("sync", lambda e: e.dma_start(out=dtr[:], in_=dts.rearrange("(c p) j -> p c j", p=128)), writes=[tprep], sem_tok=tprep)
                    load_bcast(dtb[:], p_dtb[l], tprep, 64)
                    load_bcast(alg[:], p_alog[l], tprep, 64)
                    load_bcast(dskb[:], p_dsk[l], tprep, 32)
                    tp = [tprep]
                    P.op("vector", lambda e: e.tensor_tensor(out=dtr[:], in0=dtr[:], in1=dtb[:].unsqueeze(1).to_broadcast([128, NT, 64]), op=ALU.add), reads=tp, writes=tp)
                    P.op("scalar", lambda e: e.activation(out=dtr[:], in_=dtr[:], func=AF.Exp), reads=tp, writes=tp)
                    P.op("scalar", lambda e: e.activation(out=dtv[:], in_=dtr[:], func=AF.Ln, bias=1.0), reads=tp, writes=tp)
                    sm = cs[0]
                    mk = cs[1]
                    P.op("vector", lambda e: e.tensor_scalar(out=sm[:], in0=dtr[:], scalar1=-0.25, scalar2=1.0 / 3.0, op0=ALU.mult, op1=ALU.add), reads=tp, writes=tp)
                    P.op("vector", lambda e: e.tensor_tensor(out=sm[:], in0=sm[:], in1=dtr[:], op=ALU.mult), reads=tp, writes=tp)
                    P.op("vector", lambda e: e.tensor_scalar(out=sm[:], in0=sm[:], scalar1=-0.5, scalar2=None, op0=ALU.add), reads=tp, writes=tp)
                    P.op("vector", lambda e: e.tensor_tensor(out=sm[:], in0=sm[:], in1=dtr[:], op=ALU.mult), reads=tp, writes=tp)
                    P.op("vector", lambda e: e.tensor_scalar(out=sm[:], in0=sm[:], scalar1=1.0, scalar2=None, op0=ALU.add), reads=tp, writes=tp)
                    P.op("vector", lambda e: e.tensor_tensor(out=sm[:], in0=sm[:], in1=dtr[:], op=ALU.mult), reads=tp, writes=tp)
                    P.op("vector", lambda e: e.tensor_scalar(out=mk[:], in0=dtr[:], scalar1=0.1, scalar2=None, op0=ALU.is_lt), reads=tp, writes=tp)
                    P.op("vector", lambda e: e.tensor_tensor(out=sm[:], in0=sm[:], in1=dtv[:], op=ALU.subtract), reads=tp, writes=tp)
                    P.op("vector", lambda e: e.tensor_tensor(out=sm[:], in0=sm[:], in1=mk[:], op=ALU.mult), reads=tp, writes=tp)
                    P.op("vector", lambda e: e.tensor_tensor(out=dtv[:], in0=dtv[:], in1=sm[:], op=ALU.add), reads=tp, writes=tp)
                    dump("dt", dtv[:], [128, NT, 64], tprep)
                    P.op("scalar", lambda e: e.activation(out=lndt[:], in_=dtv[:], func=AF.Ln), reads=tp, writes=tp)
                    P.op("scalar", lambda e: e.activation(out=alg[:], in_=alg[:], func=AF.Exp), reads=tp, writes=tp)
                    P.op("vector", lambda e: e.scalar_tensor_tensor(out=adt[:], in0=dtv[:], scalar=-1.0, in1=alg[:].unsqueeze(1).to_broadcast([128, NT, 64]), op0=ALU.mult, op1=ALU.mult), reads=tp, writes=tp)
                    P.op("vector", lambda e: e.tensor_copy(out=adh[:], in_=adt[:]), reads=tp, writes=tp)
                    P.op("vector", lambda e: e.tensor_tensor(out=dtr[:], in0=adt[:], in1=adh[:], op=ALU.subtract), reads=tp, writes=tp)
                    P.op("vector", lambda e: e.tensor_copy(out=adl[:], in_=dtr[:]), reads=tp, writes=tp)
                    adf = adt[:].rearrange("p c j -> p (c j)")
                    npc = (NT * 64 + 511) // 512
                    for m in range(5):
                        dstt = (cs[m] if m < 4 else dec)[:].rearrange("p c j -> p (c j)")
                        for pc in range(npc):
                            n0 = pc * 512
                            n1 = min(NT * 64, n0 + 512)
                            ps, tps = bank()
                            lh = tri[:, m, :] if m < 4 else onesf[:]
                            P.op("tensor", lambda e, ps=ps, lh=lh, n0=n0, n1=n1: e.matmul(ps[:, 0:n1 - n0], lhsT=lh, rhs=adf[:, n0:n1], start=True, stop=True), reads=tp + [tconst], writes=[tps])
                            if m < 4:
                                P.op("scalar", lambda e, ps=ps, dstt=dstt, n0=n0, n1=n1: e.activation(out=dstt[:, n0:n1], in_=ps[:, 0:n1 - n0], func=AF.Copy), reads=[tps], writes=tp)
                            else:
                                P.op("scalar", lambda e, ps=ps, dstt=dstt, n0=n0, n1=n1: e.activation(out=dstt[:, n0:n1], in_=ps[:, 0:n1 - n0], func=AF.Exp), reads=[tps], writes=tp)
                    P.op("vector", lambda e: e.tensor_tensor(out=biasd[0][:], in0=lndt[:, :, 0:32], in1=cs[0][:, :, 0:32], op=ALU.subtract), reads=tp, writes=tp)
                    P.op("vector", lambda e: e.tensor_tensor(out=wgt[0][:], in0=cs[1][:, :, 0:32], in1=lndt[:, :, 0:32], op=ALU.add), reads=tp, writes=tp)
                    P.op("scalar", lambda e: e.activation(out=wgt[0][:], in_=wgt[0][:], func=AF.Exp), reads=tp, writes=tp)
                    P.op("scalar", lambda e: e.activation(out=esc[0][:], in_=cs[0][:, :, 0:32], func=AF.Exp), reads=tp, writes=tp)
                    P.op("vector", lambda e: e.tensor_tensor(out=biasd[1][:], in0=cs[2][:, :, 32:64], in1=lndt[:, :, 32:64], op=ALU.add), reads=tp, writes=tp)
                    P.op("scalar", lambda e: e.activation(out=wgt[1][:], in_=biasd[1][:], func=AF.Exp), reads=tp, writes=tp)
                    P.op("scalar", lambda e: e.activation(out=esc[1][:], in_=cs[3][:, :, 32:64], func=AF.Exp), reads=tp, writes=tp)
                    dump("lndt", lndt[:], [128, NT, 64], tprep)
                    dump("adt", adt[:], [128, NT, 64], tprep)
                    dump("dec", dec[:], [128, NT, 64], tprep)
                    dump("bias1", biasd[1][:], [128, NT, 32], tprep)
                    dump("wgt1", wgt[1][:], [128, NT, 32], tprep)
                    dump("esc1", esc[1][:], [128, NT, 32], tprep)
                    dump("cs2", cs[2][:], [128, NT, 64], tprep)
                    for h in range(32):
                        P.op("vector", lambda e, h=h: e.tensor_scalar(out=DI[:, h, :], in0=identf[:], scalar1=dskb[:, h:h + 1], scalar2=None, op0=ALU.mult), reads=tp, writes=tp)
                    P.barrier()
                xcl = Rot([(sb("xcl%d" % i, [128, 24, 128], BF16, pst), P.dtok("xcl%d" % i)) for i in range(2)])
                xtmr = Rot([(sb("xtm%d" % i, [128, 2560], BF16, pst), P.tok("xtm%d" % i)) for i in range(2)])
                xwr = Rot([(sb("xw%d" % i, [128, 2048], BF16, pst), P.tok("xw%d" % i)) for i in range(2)])
                ltr = Rot([(sb("lt%d" % i, [128, 8, 128], BF16, pst), P.tok("lt%d" % i)) for i in range(2)])
                mpr = Rot([(sb("mp%d" % i, [128, 8, 128], BF16, pst), P.tok("mp%d" % i)) for i in range(2)])
                cbr = Rot([(sb("cbs%d" % i, [128, 128], F32, pst), P.tok("cbs%d" % i)) for i in range(2)])
                hst = sb("hst", [128, 2048], F32, pst)
                hbf = sb("hbf", [128, 2048], BF16, pst)
                thst = [P.tok("hst%d" % g) for g in range(4)]
                thbf = [P.tok("hbf%d" % g) for g in range(4)]
                ytr = Rot([(sb("ytmp%d" % i, [128, 512], F32, pst), P.tok("ytmp%d" % i)) for i in range(2)])
                yaccr = Rot([(sb("yacc%d" % i, [128, 2048], F32, pst), P.dtok("yacc%d" % i)) for i in range(2)])
                yblr = Rot([(sb("ybl%d" % i, [128, 2048], F32, pst), P.dtok("ybl%d" % i)) for i in range(1)])
                ztr = Rot([(sb("zt%d" % i, [128, 2048], BF16, pst), P.dtok("zt%d" % i)) for i in range(1)])
                xbv = xbcT.rearrange("(k p) t -> p k t", p=128)
                if dbg:
                    dlt = nc.dram_tensor("d_lt", [128, 1024], BF16, kind="ExternalOutput").ap()
                    dmp = nc.dram_tensor("d_mp", [128, 1024], BF16, kind="ExternalOutput").ap()
                    dcb = nc.dram_tensor("d_cb", [128, 128], F32, kind="ExternalOutput").ap()
                    tdbg = P.dtok("dbgst")
                for dr_ in (1, 0):
                    P.op("vector", lambda e: e.memset(hst[:], 0.0), writes=thst)
                    P.op("vector", lambda e: e.memset(hbf[:], 0.0), writes=thbf)
                    order = range(NT) if dr_ == 0 else range(NT - 1, -1, -1)
                    mtri = 0 if dr_ == 0 else 2
                    for c in order:
                        xa, txa = xcl.next()
                        P.dma("sync", lambda e, xa=xa, c=c: e.dma_start(out=xa[:], in_=xbv[:, :, c * 128:(c + 1) * 128]), writes=[txa], sem_tok=txa)
                        xtm, txtm = xtmr.next()
                        for part in range(3):
                            ps, tps = bank()
                            psb = ps[:].bitcast(BF16)
                            nk = 8 if part < 2 else 4
                            for k in range(nk):
                                P.op("tensor", lambda e, psb=psb, xa=xa, k=k, part=part: e.transpose(psb[:, k * 128:(k + 1) * 128], xa[:, part * 8 + k, :], identb[:]), reads=[txa], writes=[tps])
                            if part == 1:
                                P.op("vector", lambda e, xtm=xtm, psb=psb, part=part, nk=nk: e.tensor_copy(out=xtm[:, part * 1024:part * 1024 + nk * 128], in_=psb[:, 0:nk * 128]), reads=[tps], writes=[txtm])
                            else:
                                P.op("scalar", lambda e, xtm=xtm, psb=psb, part=part, nk=nk: e.activation(out=xtm[:, part * 1024:part * 1024 + nk * 128], in_=psb[:, 0:nk * 128], func=AF.Copy), reads=[tps], writes=[txtm])
                        xw, txw = xwr.next()
                        P.op("gpsimd", lambda e, xw=xw, xtm=xtm, c=c, dr_=dr_: e.tensor_tensor(out=xw[:].rearrange("p (h d) -> p h d", d=64), in0=xtm[:, 0:2048].rearrange("p (h d) -> p h d", d=64), in1=wgt[dr_][:, c, :].unsqueeze(2).to_broadcast([128, 32, 64]), op=ALU.mult), reads=[txtm], writes=[txw])
                        yacc, tyacc = yaccr.next()
                        for g in range(4):
                            psc, tpsc = bank()
                            P.op("tensor", lambda e, psc=psc, xa=xa, g=g: e.matmul(psc[:, 0:128], lhsT=xa[:, 16 + g, :], rhs=xa[:, 20 + g, :], start=True, stop=True), reads=[txa], writes=[tpsc])
                            cbs, tcbs = cbr.next()
                            P.op("scalar", lambda e, cbs=cbs, psc=psc: e.activation(out=cbs[:], in_=psc[:, 0:128], func=AF.Copy), reads=[tpsc], writes=[tcbs])
                            lt, tlt = ltr.next()
                            pds = [bank(), bank()]
                            for hh in range(8):
                                h = g * 8 + hh
                                pd, tpd = pds[hh // 4]
                                tgt = pd[:, (hh % 4) * 128:(hh % 4 + 1) * 128]
                                col = dr_ * 32 + h
                                P.op("tensor", lambda e, tgt=tgt, c=c, col=col, mtri=mtri: e.matmul(tgt, lhsT=adh[:, c, col:col + 1].to_broadcast([128, 128]), rhs=trib[:, mtri, :], start=True, stop=False), writes=[tpd])
                                P.op("tensor", lambda e, tgt=tgt, c=c, col=col, mtri=mtri: e.matmul(tgt, lhsT=adl[:, c, col:col + 1].to_broadcast([128, 128]), rhs=trib[:, mtri, :], start=False, stop=False), writes=[tpd])
                                P.op("tensor", lambda e, tgt=tgt, dr_=dr_: e.matmul(tgt, lhsT=identb[:], rhs=negb[:, dr_, :], start=False, stop=True), writes=[tpd])
                            for hh in range(8):
                                h = g * 8 + hh
                                pd, tpd = pds[hh // 4]
                                tgt = pd[:, (hh % 4) * 128:(hh % 4 + 1) * 128]
                                P.op("scalar", lambda e, lt=lt, hh=hh, tgt=tgt, c=c, h=h, dr_=dr_: e.activation(out=lt[:, hh, :], in_=tgt, func=AF.Exp, bias=biasd[dr_][:, c, h:h + 1], scale=(1.0 if dr_ == 0 else -1.0)), reads=[tpd], writes=[tlt], self_waw_ok=True)
                            mp, tmp_ = mpr.next()
                            P.op("vector", lambda e, mp=mp, lt=lt, cbs=cbs: e.tensor_tensor(out=mp[:], in0=lt[:], in1=cbs[:].unsqueeze(1).to_broadcast([128, 8, 128]), op=ALU.mult), reads=[tlt, tcbs], writes=[tmp_])
                            if dbg and dr_ == 1 and c == NT - 1 and g == 0:
                                P.dma("sync", lambda e, lt=lt: e.dma_start(out=dlt, in_=lt[:].rearrange("p a b -> p (a b)")), reads=[tlt], sem_tok=tdbg)
                                P.dma("sync", lambda e, mp=mp: e.dma_start(out=dmp, in_=mp[:].rearrange("p a b -> p (a b)")), reads=[tmp_], sem_tok=tdbg)
                                P.dma("sync", lambda e, cbs=cbs: e.dma_start(out=dcb, in_=cbs[:]), reads=[tcbs], sem_tok=tdbg)
                            psy, tpsy = bank()
                            for hh in range(8):
                                h = g * 8 + hh
                                P.op("tensor", lambda e, psy=psy, mp=mp, hh=hh, h=h, xtm=xtm, dr_=dr_: e.matmul(psy[:, hh * 64:(hh + 1) * 64], lhsT=mp[:, hh, :], rhs=xtm[:, h * 64:(h + 1) * 64], start=True, stop=(dr_ == 1), skip_group_check=True), reads=[tmp_, txtm], writes=[tpsy])
                                if dr_ == 0:
                                    P.op("tensor", lambda e, psy=psy, hh=hh, h=h, xtm=xtm: e.matmul(psy[:, hh * 64:(hh + 1) * 64], lhsT=DI[:, h, :], rhs=xtm[:, h * 64:(h + 1) * 64], start=False, stop=True, skip_group_check=True), reads=[txtm], writes=[tpsy])
                            pso, tpso = bank()
                            P.op("tensor", lambda e, pso=pso, xa=xa, g=g: e.matmul(pso[:], lhsT=xa[:, 20 + g, :], rhs=hbf[:, g * 512:(g + 1) * 512], start=True, stop=True), reads=[txa, thbf[g]], writes=[tpso])
                            yt, tyt = ytr.next()
                            P.op("vector", lambda e, yt=yt, pso=pso, c=c, g=g, dr_=dr_: e.tensor_tensor(out=yt[:].rearrange("p (h d) -> p h d", d=64), in0=pso[:].rearrange("p (h d) -> p h d", d=64), in1=esc[dr_][:, c, g * 8:(g + 1) * 8].unsqueeze(2).to_broadcast([128, 8, 64]), op=ALU.mult), reads=[tpso], writes=[tyt])
                            P.op("vector", lambda e, yacc=yacc, psy=psy, yt=yt, g=g: e.tensor_tensor(out=yacc[:, g * 512:(g + 1) * 512], in0=psy[:], in1=yt[:], op=ALU.add), reads=[tpsy, tyt], writes=[tyacc], self_waw_ok=True)
                            pss, tpss = bank()
                            P.op("tensor", lambda e, pss=pss, xtm=xtm, xw=xw, g=g: e.matmul(pss[:], lhsT=xtm[:, 2048 + g * 128:2048 + (g + 1) * 128], rhs=xw[:, g * 512:(g + 1) * 512], start=True, stop=True), reads=[txtm, txw], writes=[tpss])
                            hs = hst[:, g * 512:(g + 1) * 512]
                            P.op("vector", lambda e, hs=hs, c=c, g=g, dr_=dr_: e.tensor_tensor(out=hs.rearrange("p (h d) -> p h d", d=64), in0=hs.rearrange("p (h d) -> p h d", d=64), in1=dec[:, c, dr_ * 32 + g * 8:dr_ * 32 + (g + 1) * 8].unsqueeze(2).to_broadcast([128, 8, 64]), op=ALU.mult), reads=[thst[g]], writes=[thst[g]])
                            P.op("vector", lambda e, hs=hs, pss=pss: e.tensor_tensor(out=hs, in0=hs, in1=pss[:], op=ALU.add), reads=[thst[g], tpss], writes=[thst[g]])
                            P.op("gpsimd", lambda e, hs=hs, g=g: e.tensor_copy(out=hbf[:, g * 512:(g + 1) * 512], in_=hs), reads=[thst[g]], writes=[thbf[g]])
                        rows = slice(c * 128, (c + 1) * 128)
                        if dr_ == 1:
                            store(yb[rows, :], yacc[:], tyacc)
                        else:
                            ybl, tybl = yblr.next()
                            zt, tzt = ztr.next()
                            P.dma("sync", lambda e, ybl=ybl, rows=rows: e.dma_start(out=ybl[:], in_=yb[rows, :]), writes=[tybl], sem_tok=tybl)
                            P.dma("sync", lambda e, zt=zt, rows=rows: e.dma_start(out=zt[:], in_=zs[rows, :]), writes=[tzt], sem_tok=tzt)
                            P.op("gpsimd", lambda e, yacc=yacc, ybl=ybl: e.tensor_tensor(out=yacc[:], in0=yacc[:], in1=ybl[:], op=ALU.add), reads=[tyacc, tybl], writes=[tyacc])
                            P.op("gpsimd", lambda e, yacc=yacc, zt=zt: e.tensor_tensor(out=yacc[:], in0=yacc[:], in1=zt[:], op=ALU.mult), reads=[tyacc, tzt], writes=[tyacc])
                            store(yg[rows, :], yacc[:], tyacc)
                    P.barrier()


        def phase_attn(l):
            with contextlib.ExitStack() as pst:
                qr = Rot([(sb("qsb%d" % i, [128, T], BF16, pst), P.dtok("qsb%d" % i)) for i in range(2)])
                kr = Rot([(sb("ksb%d" % i, [128, T], BF16, pst), P.dtok("ksb%d" % i)) for i in range(2)])
                vr = Rot([(sb("vt%d" % i, [128, NT, 128], BF16, pst), P.dtok("vt%d" % i)) for i in range(2)])
                num = sb("num", [128, T], F32, pst)
                den = sb("den", [128, T], F32, pst)
                tnum, tden = P.tok("num"), P.tok("den")
                osb = sb("osb", [128, T], BF16, pst)
                tosb = P.dtok("osb")
                ptr = Rot([(sb("pT%d" % i, [128, 128], BF16, pst), P.tok("pT%d" % i)) for i in range(3)])
                for j in range(4):
                    for g in range(3):
                        h = 4 * g + j
                        d = ATTN_PATTERNS[g][1]
                        S = T // d
                        NK = S // 128
                        qs, tqs = qr.next()
                        ks, tks = kr.next()
                        vt, tvt = vr.next()
                        P.dma("sync", lambda e, qs=qs, h=h: e.dma_start(out=qs[:], in_=qT[h]), writes=[tqs], sem_tok=tqs)
                        P.dma("sync", lambda e, ks=ks, h=h: e.dma_start(out=ks[:], in_=kT[h]), writes=[tks], sem_tok=tks)
                        vv = vt[:].rearrange("a (r kt) e -> a r kt e", r=d)
                        for r in range(d):
                            P.dma("sync", lambda e, vv=vv, h=h, d=d, r=r: e.dma_start(out=vv[:, r], in_=vs[:, h * 128:(h + 1) * 128].rearrange("(kt a r) e -> a r kt e", a=128, r=d)[:, r]), writes=[tvt], sem_tok=tvt)
                        for r in range(d):
                            for m in range(NK + 1):
                                b0 = 64 if m == 0 else 0
                                b1 = 64 if m == NK else 128
                                nq = b1 - b0
                                i0 = 128 * m - 64 + b0
                                q0 = i0 * d + r
                                qsl = qs[:, q0:q0 + (nq - 1) * d + 1:d]
                                pso, tpso = bank()
                                psd, tpsd = bank()
                                kts = [kt for kt in (m - 1, m) if 0 <= kt < NK]
                                for idx, kt in enumerate(kts):
                                    typ = 0 if kt == m - 1 else 1
                                    pss, tpss = bank()
                                    k0 = 128 * kt * d + r
                                    ksl = ks[:, k0:k0 + 127 * d + 1:d]
                                    P.op("tensor", lambda e, pss=pss, ksl=ksl, qsl=qsl, nq=nq: e.matmul(pss[:, 0:nq], lhsT=ksl, rhs=qsl, start=True, stop=False), reads=[tqs, tks], writes=[tpss])
                                    P.op("tensor", lambda e, pss=pss, nq=nq, h=h, typ=typ, b0=b0, b1=b1: e.matmul(pss[:, 0:nq], lhsT=identf[:], rhs=abias[:, h * 2 + typ, b0:b1], start=False, stop=True), writes=[tpss])
                                    pt, tpt = ptr.next()
                                    P.op("scalar", lambda e, pt=pt, pss=pss, nq=nq: e.activation(out=pt[:, 0:nq], in_=pss[:, 0:nq], func=AF.Exp), reads=[tpss], writes=[tpt])
                                    fl = dict(start=(idx == 0), stop=(idx == len(kts) - 1))
                                    P.op("tensor", lambda e, pso=pso, vv=vv, r=r, kt=kt, pt=pt, nq=nq, fl=fl: e.matmul(pso[:, 0:nq], lhsT=vv[:, r, kt, :], rhs=pt[:, 0:nq], **fl), reads=[tvt, tpt], writes=[tpso])
                                    P.op("tensor", lambda e, psd=psd, pt=pt, nq=nq, fl=fl: e.matmul(psd[:, 0:nq], lhsT=onesb[:], rhs=pt[:, 0:nq], **fl), reads=[tpt], writes=[tpsd])
                                nsl = num[:, q0:q0 + (nq - 1) * d + 1:d]
                                dsl = den[:, q0:q0 + (nq - 1) * d + 1:d]
                                if g == 0:
                                    P.op("vector", lambda e, nsl=nsl, pso=pso, nq=nq: e.tensor_copy(out=nsl, in_=pso[:, 0:nq]), reads=[tpso], writes=[tnum], self_waw_ok=True)
                                    P.op("scalar", lambda e, dsl=dsl, psd=psd, nq=nq: e.activation(out=dsl, in_=psd[:, 0:nq], func=AF.Copy), reads=[tpsd], writes=[tden], self_waw_ok=True)
                                else:
                                    P.op("vector", lambda e, nsl=nsl, pso=pso, nq=nq: e.tensor_tensor(out=nsl, in0=nsl, in1=pso[:, 0:nq], op=ALU.add), reads=[tpso], writes=[tnum], self_waw_ok=True)
                                    P.op("vector", lambda e, dsl=dsl, psd=psd, nq=nq: e.tensor_tensor(out=dsl, in0=dsl, in1=psd[:, 0:nq], op=ALU.add), reads=[tpsd], writes=[tden], self_waw_ok=True)
                    P.op("vector", lambda e: e.reciprocal(out=den[:], in_=den[:]), reads=[tden], writes=[tden])
                    P.op("vector", lambda e: e.tensor_tensor(out=osb[:], in0=num[:], in1=den[:], op=ALU.mult), reads=[tnum, tden], writes=[tosb])
                    store(oT[j * 128:(j + 1) * 128, :], osb[:], tosb)
                P.barrier()

        def phase_conf(l):
            with contextlib.ExitStack() as pst:
                dwp = sb("dwp", [128, 16, 31], F32, pst)
                dwb = sb("dwb", [128, 16], F32, pst)
                tdw = P.dtok("dw")
                P.dma("sync", lambda e: e.dma_start(out=dwp[:], in_=p_dw[l]), writes=[tdw], sem_tok=tdw)
                P.dma("sync", lambda e: e.dma_start(out=dwb[:], in_=p_dwb[l]), writes=[tdw], sem_tok=tdw)
                cins = [(sb("ccin%d" % i, [128, T + 30], BF16, pst), P.dtok("ccin%d" % i)) for i in range(2)]
                for cin, tcin in cins:
                    P.op("vector", lambda e, cin=cin: e.memset(cin[:, 0:15], 0.0), writes=[tcin])
                    P.op("vector", lambda e, cin=cin: e.memset(cin[:, T + 15:T + 30], 0.0), writes=[tcin])
                cinrot = Rot(cins)
                dgr = Rot([(sb("dg%d" % i, [128, 31, 128], BF16, pst), P.tok("dg%d" % i)) for i in range(2)])
                str_ = Rot([(sb("cst%d" % i, [128, 512], BF16, pst), P.dtok("cst%d" % i)) for i in range(3)])
                for cc in range(16):
                    cin, tcin = cinrot.next()
                    P.dma("sync", lambda e, cin=cin, cc=cc: e.dma_start(out=cin[:, 15:15 + T], in_=uT[cc * 128:(cc + 1) * 128, :]), writes=[tcin], sem_tok=tcin)
                    dg, tdg = dgr.next()
                    for k in range(31):
                        eng = "vector" if k % 2 == 0 else "gpsimd"
                        P.op(eng, lambda e, dg=dg, k=k, cc=cc: e.tensor_scalar(out=dg[:, k, :], in0=identf[:], scalar1=dwp[:, cc, k:k + 1], scalar2=None, op0=ALU.mult), reads=[tdw], writes=[tdg], self_waw_ok=True)
                    P.op("vector", lambda e, dg=dg: e.tensor_copy(out=dg[:, 30, 0:1], in_=dg[:, 30, 0:1]), reads=[tdg], writes=[tdg])
                    for tt in range(T // 512):
                        ps, tps = bank()
                        for k in range(31):
                            P.op("tensor", lambda e, ps=ps, dg=dg, cin=cin, k=k, tt=tt: e.matmul(ps[:], lhsT=dg[:, k, :], rhs=cin[:, tt * 512 + k:tt * 512 + k + 512], start=(k == 0), stop=(k == 30)), reads=[tdg, tcin], writes=[tps])
                        stg, tst = str_.next()
                        P.op("scalar", lambda e, stg=stg, ps=ps, cc=cc: e.activation(out=stg[:], in_=ps[:], func=AF.Identity, bias=dwb[:, cc:cc + 1]), reads=[tps, tdw], writes=[tst])
                        store(ycT[cc * 128:(cc + 1) * 128, tt * 512:(tt + 1) * 512], stg[:], tst)
                P.barrier()
            with contextlib.ExitStack() as pst:
                lng = sb("lng", [128, 16], F32, pst)
                lnb = sb("lnb", [128, 16], F32, pst)
                tln = P.dtok("ln")
                P.dma("sync", lambda e: e.dma_start(out=lng[:], in_=p_lng[l]), writes=[tln], sem_tok=tln)
                P.dma("sync", lambda e: e.dma_start(out=lnb[:], in_=p_lnb[l]), writes=[tln], sem_tok=tln)
                ylr = Rot([(sb("yl%d" % i, [128, 16, 512], BF16, pst), P.dtok("yl%d" % i)) for i in range(2)])
                sqr = Rot([(sb("sqy%d" % i, [128, 16, 512], BF16, pst), P.tok("sqy%d" % i)) for i in range(1)])
                mur = Rot([(sb("mu%d" % i, [128, 3, 512], F32, pst), P.tok("mu%d" % i)) for i in range(2)])
                t1r = Rot([(sb("t1%d" % i, [128, 512], F32, pst), P.tok("t1%d" % i)) for i in range(3)])
                str_ = Rot([(sb("cst2%d" % i, [128, 512], BF16, pst), P.dtok("cst2%d" % i)) for i in range(3)])
                ycv = ycT.rearrange("(k p) t -> p k t", p=128)
                for tt in range(T // 512):
                    yl, tyl = ylr.next()
                    P.dma("sync", lambda e, yl=yl, tt=tt: e.dma_start(out=yl[:], in_=ycv[:, :, tt * 512:(tt + 1) * 512]), writes=[tyl], sem_tok=tyl)
                    sqy, tsq = sqr.next()
                    P.op("gpsimd", lambda e, sqy=sqy, yl=yl: e.tensor_tensor(out=sqy[:], in0=yl[:], in1=yl[:], op=ALU.mult), reads=[tyl], writes=[tsq])
                    ps1, tps1 = bank()
                    ps2, tps2 = bank()
                    for cc in range(16):
                        P.op("tensor", lambda e, ps1=ps1, yl=yl, cc=cc: e.matmul(ps1[:], lhsT=onesb[:], rhs=yl[:, cc, :], start=(cc == 0), stop=(cc == 15)), reads=[tyl], writes=[tps1])
                    for cc in range(16):
                        P.op("tensor", lambda e, ps2=ps2, sqy=sqy, cc=cc: e.matmul(ps2[:], lhsT=onesb[:], rhs=sqy[:, cc, :], start=(cc == 0), stop=(cc == 15)), reads=[tsq], writes=[tps2])
                    mu, tmu = mur.next()
                    P.op("scalar", lambda e, mu=mu, ps1=ps1: e.activation(out=mu[:, 0, :], in_=ps1[:], func=AF.Copy, scale=1.0 / D), reads=[tps1], writes=[tmu])
                    P.op("vector", lambda e, mu=mu: e.tensor_tensor(out=mu[:, 1, :], in0=mu[:, 0, :], in1=mu[:, 0, :], op=ALU.mult), reads=[tmu], writes=[tmu])
                    P.op("vector", lambda e, mu=mu, ps2=ps2: e.scalar_tensor_tensor(out=mu[:, 2, :], in0=ps2[:], scalar=1.0 / D, in1=mu[:, 1, :], op0=ALU.mult, op1=ALU.subtract), reads=[tmu, tps2], writes=[tmu])
                    P.op("scalar", lambda e, mu=mu: e.activation(out=mu[:, 2, :], in_=mu[:, 2, :], func=AF.Sqrt, bias=EPS), reads=[tmu], writes=[tmu])
                    P.op("vector", lambda e, mu=mu: e.reciprocal(out=mu[:, 2, :], in_=mu[:, 2, :]), reads=[tmu], writes=[tmu])
                    for cc in range(16):
                        t1, tt1 = t1r.next()
                        eng = "vector" if cc % 2 == 0 else "gpsimd"
                        P.op(eng, lambda e, t1=t1, yl=yl, mu=mu, cc=cc: e.tensor_tensor(out=t1[:], in0=yl[:, cc, :], in1=mu[:, 0, :], op=ALU.subtract), reads=[tyl, tmu], writes=[tt1])
                        P.op(eng, lambda e, t1=t1, mu=mu: e.tensor_tensor(out=t1[:], in0=t1[:], in1=mu[:, 2, :], op=ALU.mult), reads=[tt1, tmu], writes=[tt1])
                        stg, tst = str_.next()
                        P.op("scalar", lambda e, stg=stg, t1=t1, cc=cc: e.activation(out=stg[:], in_=t1[:], func=AF.Silu, bias=lnb[:, cc:cc + 1], scale=lng[:, cc:cc + 1]), reads=[tt1, tln], writes=[tst])
                        store(cT[cc * 128:(cc + 1) * 128, tt * 512:(tt + 1) * 512], stg[:], tst)
                P.barrier()

        def phase_B(l, xsrc):
            with contextlib.ExitStack() as pst:
                TB = min(1024, T)
                NTB = TB // 128
                ph = make_phase(pst, TB)
                actT, tact = ph["actT"], ph["tact"]
                gb, tgb = ph["gb"]
                load_bcast(gb[:], p_ng[l], tgb, D)
                mergedT = sb("mergedT", [128, KC, TB], BF16, pst)
                NTT = TB // 512
                tm = [P.tok("mer%d" % i) for i in range(NTT)]
                tmer_list = [tm[t // 4] for t in range(NTB)]
                glr = Rot([(sb("gl%d" % i, [128, 512], BF16, pst), P.dtok("gl%d" % i)) for i in range(3)])
                tactld = P.dtok("actld")
                tnone = P.tok("none")

                def branch(W, kcn, br, first, g0):
                    for nb in range(4):
                        wv, wtk = load_w(ph, W, kcn, 0, nb * 512, 512)
                        for j in range(4):
                            cb = nb * 4 + j
                            for tt in range(NTT):
                                gl, tgl = glr.next()
                                P.dma("sync", lambda e, gl=gl, cb=cb, tt=tt: e.dma_start(out=gl[:], in_=gT[br * D + cb * 128:br * D + (cb + 1) * 128, g0 + tt * 512:g0 + (tt + 1) * 512]), writes=[tgl], sem_tok=tgl)
                                ps, tps = mm_fm(wv, wtk, j, kcn, actT, tact, tt)
                                msl = mergedT[:, cb, tt * 512:(tt + 1) * 512]
                                if first:
                                    P.op("vector", lambda e, msl=msl, ps=ps, gl=gl: e.tensor_tensor(out=msl, in0=ps[:], in1=gl[:], op=ALU.mult), reads=[tps, tgl], writes=[tm[tt]], self_waw_ok=True)
                                else:
                                    tmpb, ttmp = ph["stf"].next()
                                    P.op("vector", lambda e, tmpb=tmpb, ps=ps, gl=gl: e.tensor_tensor(out=tmpb[:], in0=ps[:], in1=gl[:], op=ALU.mult), reads=[tps, tgl], writes=[ttmp])
                                    P.op("gpsimd", lambda e, msl=msl, tmpb=tmpb: e.tensor_tensor(out=msl, in0=msl, in1=tmpb[:], op=ALU.add), reads=[ttmp, tm[tt]], writes=[tm[tt]])

                for s in range(T // TB):
                    g0 = s * TB
                    norm_transpose(ph, yg, tnone, gb, tgb, g0, TB, actT, tact)
                    branch(w_ssd_o[l], 16, 0, True, g0)
                    for kc in range(4):
                        P.dma("sync", lambda e, kc=kc, g0=g0: e.dma_start(out=actT[:, kc, :], in_=oT[kc * 128:(kc + 1) * 128, g0:g0 + TB]), writes=tact, sem_tok=tactld)
                    branch(w_attn_o[l], 4, 1, False, g0)
                    for kc in range(16):
                        P.dma("sync", lambda e, kc=kc, g0=g0: e.dma_start(out=actT[:, kc, :], in_=cT[kc * 128:(kc + 1) * 128, g0:g0 + TB]), writes=tact, sem_tok=tactld)
                    branch(w_conv_o[l], 16, 2, False, g0)
                    for nb in range(4):
                        wv, wtk = load_w(ph, w_out[l], 16, 0, nb * 512, 512)
                        for t in range(NTB):
                            xo, txo = ph["stf"].next()
                            rows = slice(g0 + t * 128, g0 + (t + 1) * 128)
                            P.dma("sync", lambda e, xo=xo, rows=rows, nb=nb: e.dma_start(out=xo[:], in_=xsrc[rows, nb * 512:(nb + 1) * 512]), writes=[txo], sem_tok=txo)
                            ps, tps = mm_tm(wv, wtk, 512, 16, mergedT, tmer_list, t)
                            P.op("vector", lambda e, xo=xo, ps=ps: e.tensor_tensor(out=xo[:], in0=xo[:], in1=ps[:], op=ALU.add), reads=[tps, txo], writes=[txo])
                            store(xres[rows, nb * 512:(nb + 1) * 512], xo[:], txo)
                P.barrier()

        def phase_mlp(l, dst):
            with contextlib.ExitStack() as pst:
                TM = min(1024, T)
                NTM = TM // 128
                NTT = TM // 512
                HH = 4096
                ph = make_phase(pst, TM, nw=2, nx=2)
                actT, tact = ph["actT"], ph["tact"]
                gb, tgb = ph["gb"]
                load_bcast(gb[:], p_n2[l], tgb, D)
                aT = sb("aT", [128, 32, TM], BF16, pst)
                ta = [P.tok("aT%d" % i) for i in range(2)]
                rr = Rot([(sb("relu%d" % i, [128, 512], BF16, pst), P.tok("relu%d" % i)) for i in range(3)])
                tnone = P.tok("none")
                for ti in range(T // TM):
                    r0 = ti * TM
                    norm_transpose(ph, xres, tnone, gb, tgb, r0, TM, actT, tact)
                    tdr = [[P.tok("xd") for _ in range(4)] for _ in range(NTM)]
                    for hh in range(2):
                        for nb in range(8):
                            wv, wtk = load_w(ph, w_up[l], 16, 0, hh * HH + nb * 512, 512)
                            for j in range(4):
                                hc = nb * 4 + j
                                for tt in range(NTT):
                                    ps, tps = mm_fm(wv, wtk, j, 16, actT, tact, tt)
                                    rl, trl = rr.next()
                                    P.op("scalar", lambda e, rl=rl, ps=ps: e.activation(out=rl[:], in_=ps[:], func=AF.Relu), reads=[tps], writes=[trl])
                                    P.op("gpsimd", lambda e, rl=rl, hc=hc, tt=tt: e.tensor_tensor(out=aT[:, hc, tt * 512:(tt + 1) * 512], in0=rl[:], in1=rl[:], op=ALU.mult), reads=[trl], writes=[ta[hc // 16]], self_waw_ok=True)
                        for nb in range(4):
                            b8 = [bank() for _ in range(NTM)]
                            for kq in range(2):
                                wv, wtk = load_w(ph, w_down[l], 16, hh * HH + kq * 2048, nb * 512, 512)
                                for t in range(NTM):
                                    mm_tm(wv, wtk, 512, 16, aT, [ta[kq]] * NTM, t, pst=b8[t], first=(kq == 0), last=(kq == 1), kc0=kq * 16)
                            src = xres if hh == 0 else dst
                            for t in range(NTM):
                                ps, tps = b8[t]
                                xo, txo = ph["stf"].next()
                                rows = slice(r0 + t * 128, r0 + (t + 1) * 128)
                                P.dma("sync", lambda e, xo=xo, rows=rows, nb=nb, src=src: e.dma_start(out=xo[:], in_=src[rows, nb * 512:(nb + 1) * 512]), reads=[tdr[t][nb]], writes=[txo], sem_tok=txo)
                                P.op("vector", lambda e, xo=xo, ps=ps: e.tensor_tensor(out=xo[:], in0=xo[:], in1=ps[:], op=ALU.add), reads=[tps, txo], writes=[txo])
                                P.dma("sync", lambda e, xo=xo, rows=rows, nb=nb: e.dma_start(out=dst[rows, nb * 512:(nb + 1) * 512], in_=xo[:]), reads=[txo], writes=[tdr[t][nb]], sem_tok=txo)
                P.barrier()

        for l in range(depth):
            xsrc = x_in if l == 0 else xres
            for fn, args in [(phase_A, (l, xsrc)), (phase_ssd, (l,)), (phase_attn, (l,)), (phase_conf, (l,)),
                             (phase_B, (l, xsrc)), (phase_mlp, (l, out if l == depth - 1 else xres))]:
                if phases is not None and fn.__name__ not in phases:
                    continue
                P.begin_phase()
                fn(*args)
                P.end_phase()
        P.emit(block)
    return nc


_WNAMES = ["w_in", "w_ssd_o", "w_attn_o", "w_conv_o", "w_out", "w_mlp_up", "w_mlp_down"]


def run_cores(x_list, inputs, T, depth=DEPTH, dbg=False, phases=None):
    nc = build(T, depth=depth, dbg=dbg, phases=phases)
    consts = host_consts()
    lay = host_layout(inputs)
    common = {}
    for k in _WNAMES:
        common[k] = np.ascontiguousarray(inputs[k][:depth])
    for k, v in lay.items():
        common[k] = np.ascontiguousarray(v[:depth])
    common.update(consts)
    zeros = {k: np.zeros_like(v) for k, v in common.items()}
    in_maps = []
    for xb in x_list:
        if xb is None:
            m = dict(zeros)
            m["x"] = np.zeros((T, D), np.float32)
        else:
            m = dict(common)
            m["x"] = np.ascontiguousarray(xb)
        in_maps.append(m)
    res = run_bass_kernel_spmd(nc, in_maps, core_ids=list(range(len(x_list))))
    return res.results


def kernel(**inputs):
    x = np.asarray(inputs["x"], dtype=np.float32)
    B, T, _ = x.shape
    inp = {k: np.asarray(v, dtype=np.float32) for k, v in inputs.items()}
    cores = [0, 1, 4, 5]
    x_list = [None] * 8
    for b in range(B):
        x_list[cores[b]] = x[b]
    results = run_cores(x_list, inp, T)
    return np.stack([results[cores[b]]["out"] for b in range(B)], axis=0).astype(np.float32)
```

```python
import contextlib
import numpy as np
import concourse.bass as bass
import concourse.mybir as mybir
from concourse.bass_utils import run_bass_kernel_spmd

F32 = mybir.dt.float32
BF16 = mybir.dt.bfloat16
AF = mybir.ActivationFunctionType
ALU = mybir.AluOpType

ENGS = ["tensor", "vector", "scalar", "gpsimd", "sync"]
EPOCH = 30000

D = 2048
KC = 16
N_IN = 20032
C_Z, C_XBC, C_DT, C_Q, C_K, C_V, C_GLU, C_GATE = 0, 2048, 5120, 5184, 6720, 8256, 9792, 13888
EPS = 1e-6
DEPTH = 2


class Tok:
    __slots__ = ("name", "w", "r", "dsem")

    def __init__(self, name=""):
        self.name = name
        self.w = None
        self.r = {}
        self.dsem = None


class Prog:
    def __init__(self, nc, stack):
        self.nc = nc
        self.stack = stack
        self.q = {e: [] for e in ENGS}
        self.cnt = {e: 0 for e in ENGS}
        self.sems = {}
        self.waited = {e: {} for e in ENGS}
        self.nsem = 0
        self.dcount = {}
        self.free_dsems = {False: [], True: []}
        self.phase_sems = None

    def begin_phase(self):
        self.phase_sems = []

    def end_phase(self):
        for sw, k in self.phase_sems:
            self.free_dsems[sw].append(k)
        self.phase_sems = None

    def _sem(self, key):
        if key not in self.sems:
            self.sems[key] = self.stack.enter_context(self.nc.semaphore("s%d" % self.nsem))
            self.nsem += 1
        return self.sems[key]

    def tok(self, name=""):
        return Tok(name)

    def dtok(self, name="", sw=False):
        t = Tok(name)
        if self.free_dsems[sw]:
            t.dsem = self.free_dsems[sw].pop()
        else:
            t.dsem = ("d", self.nsem, name)
            self._sem(t.dsem)
            self.dcount[t.dsem] = 0
        if self.phase_sems is not None:
            self.phase_sems.append((sw, t.dsem))
        return t

    def _deps(self, eng, reads, writes, self_waw_ok):
        deps = {}

        def add(d):
            if d is None:
                return
            k, v = d
            if deps.get(k, 0) < v:
                deps[k] = v
        for t in reads:
            add(t.w)
        for t in writes:
            if t.w is not None:
                if not (self_waw_ok and t.w[0][0] == "e" and t.w[0][1] == eng):
                    add(t.w)
            for k, v in t.r.items():
                if k[0] == "e" and k[1] == eng:
                    continue
                add((k, v))
        out = []
        wd = self.waited[eng]
        for k, v in deps.items():
            if eng == "tensor" and k[0] == "e" and k[1] == "tensor":
                continue
            if wd.get(k, 0) >= v:
                continue
            wd[k] = v
            out.append((k, v))
        return out

    def _mark(self, comp, reads, writes):
        k, v = comp
        for t in reads:
            if t.r.get(k, 0) < v:
                t.r[k] = v
        for t in writes:
            t.w = comp
            t.r = {}

    def op(self, eng, fn, reads=(), writes=(), self_waw_ok=False):
        waits = self._deps(eng, reads, writes, self_waw_ok)
        self.cnt[eng] += 1
        ep, v = divmod(self.cnt[eng] - 1, EPOCH)
        key = ("e", eng, ep)
        comp = (key, v + 1)
        self._sem(key)
        self.q[eng].append((waits, fn, key, 1))
        self._mark(comp, reads, writes)
        return comp

    def dma(self, eng, fn, reads=(), writes=(), sem_tok=None):
        waits = self._deps(eng, reads, writes, False)
        key = sem_tok.dsem
        self.dcount[key] += 16
        comp = (key, self.dcount[key])
        self.q[eng].append((waits, fn, key, 16))
        self._mark(comp, reads, writes)
        return comp

    def barrier(self):
        allw = []
        for e in ENGS:
            if self.cnt[e] > 0:
                ep, v = divmod(self.cnt[e] - 1, EPOCH)
                allw.append((("e", e, ep), v + 1))
        for k, v in self.dcount.items():
            if v > 0:
                allw.append((k, v))
        for e in ENGS:
            wd = self.waited[e]
            waits = []
            for k, v in allw:
                if k[0] == "e" and k[1] == e:
                    continue
                if wd.get(k, 0) >= v:
                    continue
                wd[k] = v
                waits.append((k, v))
            if waits:
                self.q[e].append((waits, None, None, 0))

    def emit(self, block):
        fin = [(k, v) for k, v in self.dcount.items() if v > 0]
        sems = self.sems
        q = self.q

        def runner(e):
            def run(engine):
                for waits, fn, key, inc in q[e]:
                    for k, v in waits:
                        engine.wait_ge(sems[k], v)
                    if fn is None:
                        continue
                    ins = fn(engine)
                    ins.then_inc(sems[key], inc)
                if e == "sync":
                    for k, v in fin:
                        engine.wait_ge(sems[k], v)
            return run
        block.tensor(runner("tensor"))
        block.vector(runner("vector"))
        block.scalar(runner("scalar"))
        block.gpsimd(runner("gpsimd"))
        block.sync(runner("sync"))


class Rot:
    def __init__(self, items):
        self.items = items
        self.i = 0

    def next(self):
        it = self.items[self.i % len(self.items)]
        self.i += 1
        return it


ATTN_PATTERNS = ((128, 1), (512, 4), (2048, 16))


def host_consts():
    c = {}
    i = np.arange(128)
    lp, l = i[:, None], i[None, :]
    c["c_ident"] = np.eye(128, dtype=np.float32)
    tri = np.stack([(lp <= l), (lp > l), (lp < l), (lp >= l)], axis=1).astype(np.float32)
    c["c_tri"] = np.ascontiguousarray(tri)
    s, ll = i[:, None], i[None, :]
    neg = np.stack([np.where(ll < s, -30000.0, 0.0), np.where(ll > s, 30000.0, 0.0)], axis=1).astype(np.float32)
    c["c_neg"] = np.ascontiguousarray(neg)
    slopes = np.exp2(-8.0 * np.arange(1, 13, dtype=np.float64) / 12.0)
    a, b = i[:, None], i[None, :]
    ab = np.zeros((128, 12, 2, 128), np.float32)
    for h in range(12):
        d = ATTN_PATTERNS[h // 4][1]
        relA = a - b - 64
        relB = a - b + 64
        ab[:, h, 0, :] = np.where(a >= b, -slopes[h] * d * np.abs(relA), -30000.0)
        ab[:, h, 1, :] = np.where(a <= b, -slopes[h] * d * np.abs(relB), -30000.0)
    c["c_abias"] = ab.reshape(128, 24, 128)
    return c


def host_layout(inp):
    L = DEPTH
    o = {}
    o["p_cw"] = np.ascontiguousarray(inp["ssd_conv_w"].transpose(0, 2, 1).reshape(L, 24, 128, 5).transpose(0, 2, 1, 3))
    o["p_cb"] = np.ascontiguousarray(inp["ssd_conv_b"].reshape(L, 24, 128).transpose(0, 2, 1))
    o["p_dw"] = np.ascontiguousarray(inp["conv_dw_w"].transpose(0, 2, 1).reshape(L, 16, 128, 31).transpose(0, 2, 1, 3))
    o["p_dwb"] = np.ascontiguousarray(inp["conv_dw_b"].reshape(L, 16, 128).transpose(0, 2, 1))
    o["p_lng"] = np.ascontiguousarray(inp["conv_ln_g"].reshape(L, 16, 128).transpose(0, 2, 1))
    o["p_lnb"] = np.ascontiguousarray(inp["conv_ln_b"].reshape(L, 16, 128).transpose(0, 2, 1))
    o["p_dtb"] = np.ascontiguousarray(inp["ssd_dt_bias"].reshape(L, 1, 64))
    o["p_alog"] = np.ascontiguousarray(inp["ssd_a_log"].reshape(L, 1, 64))
    o["p_dsk"] = np.ascontiguousarray(inp["ssd_d"].reshape(L, 1, 32))
    o["p_qg"] = np.ascontiguousarray(inp["q_norm_g"].reshape(L, 128, 1))
    o["p_kg"] = np.ascontiguousarray(inp["k_norm_g"].reshape(L, 128, 1))
    for k in ["norm1_g", "norm2_g", "ssd_norm_g"]:
        o["p_" + k] = np.ascontiguousarray(inp[k].reshape(L, 1, D))
    return o


def build(T, depth=DEPTH, dbg=False, phases=None):
    nc = bass.Bass("TRN2", target_bir_lowering=False)
    TS = min(2048, T)
    NS = T // TS
    NT = T // 128
    NTS = TS // 128

    def din(name, shape, dt=F32):
        return nc.dram_tensor(name, list(shape), dt, kind="ExternalInput").ap()

    def dscr(name, shape, dt):
        return nc.dram_tensor(name, list(shape), dt, kind=("ExternalOutput" if dbg else "Internal")).ap()

    x_in = din("x", [T, D])
    w_in = din("w_in", [depth, D, N_IN])
    w_ssd_o = din("w_ssd_o", [depth, D, D])
    w_attn_o = din("w_attn_o", [depth, 512, D])
    w_conv_o = din("w_conv_o", [depth, D, D])
    w_out = din("w_out", [depth, D, D])
    w_up = din("w_mlp_up", [depth, D, 4 * D])
    w_down = din("w_mlp_down", [depth, 4 * D, D])
    p_cw = din("p_cw", [depth, 128, 24, 5])
    p_cb = din("p_cb", [depth, 128, 24])
    p_dw = din("p_dw", [depth, 128, 16, 31])
    p_dwb = din("p_dwb", [depth, 128, 16])
    p_lng = din("p_lng", [depth, 128, 16])
    p_lnb = din("p_lnb", [depth, 128, 16])
    p_dtb = din("p_dtb", [depth, 1, 64])
    p_alog = din("p_alog", [depth, 1, 64])
    p_dsk = din("p_dsk", [depth, 1, 32])
    p_qg = din("p_qg", [depth, 128, 1])
    p_kg = din("p_kg", [depth, 128, 1])
    p_n1 = din("p_norm1_g", [depth, 1, D])
    p_n2 = din("p_norm2_g", [depth, 1, D])
    p_ng = din("p_ssd_norm_g", [depth, 1, D])
    c_ident = din("c_ident", [128, 128])
    c_tri = din("c_tri", [128, 4, 128])
    c_neg = din("c_neg", [128, 2, 128])
    c_abias = din("c_abias", [128, 24, 128])
    out = nc.dram_tensor("out", [T, D], F32, kind="ExternalOutput").ap()

    xres = dscr("xres", [T, D], F32)
    zs = dscr("zs", [T, D], BF16)
    xbc_pre = dscr("xbc_pre", [3072, T], BF16)
    xbcT = dscr("xbcT", [3072, T], BF16)
    dts = dscr("dts", [T, 64], F32)
    qT = dscr("qT", [12, 128, T], BF16)
    kT = dscr("kT", [12, 128, T], BF16)
    vs = dscr("vs", [T, 1536], BF16)
    uT = dscr("uT", [D, T], BF16)
    gT = dscr("gT", [3 * D, T], BF16)
    yb = dscr("yb", [T, D], F32)
    yg = dscr("yg", [T, D], F32)
    oT = dscr("oT", [512, T], BF16)
    ycT = dscr("ycT", [D, T], BF16)
    cT = dscr("cT", [D, T], BF16)

    with contextlib.ExitStack() as st:
        P = Prog(nc, st)

        uniq = [0]

        def sb(name, shape, dt, stack=st):
            uniq[0] += 1
            return stack.enter_context(nc.sbuf_tensor("%s_%d" % (name, uniq[0]), list(shape), dt))

        pbanks = []
        for i in range(8):
            pbanks.append((st.enter_context(nc.psum_tensor("pb%d" % i, [128, 512], F32)), P.tok("pb%d" % i)))
        prot = Rot(pbanks)
        bank = prot.next

        identf = sb("identf", [128, 128], F32)
        identb = sb("identb", [128, 128], BF16)
        onesb = sb("onesb", [128, 128], BF16)
        onesf = sb("onesf", [128, 128], F32)
        tri = sb("tri", [128, 4, 128], F32)
        negb = sb("negb", [128, 2, 128], BF16)
        trib = sb("trib", [128, 4, 128], BF16)
        abias = sb("abias", [128, 24, 128], F32)
        tconst = P.dtok("const")
        tconst2 = P.dtok("const2", sw=True)

        block = st.enter_context(nc.Block())

        P.dma("sync", lambda e: e.dma_start(out=identf[:], in_=c_ident[:]), writes=[tconst], sem_tok=tconst)
        P.dma("sync", lambda e: e.dma_start(out=tri[:], in_=c_tri[:]), writes=[tconst], sem_tok=tconst)
        P.dma("sync", lambda e: e.dma_start(out=abias[:], in_=c_abias[:]), writes=[tconst], sem_tok=tconst)
        P.dma("gpsimd", lambda e: e.dma_start(out=negb[:], in_=c_neg[:]), writes=[tconst2], sem_tok=tconst2)
        P.dma("gpsimd", lambda e: e.dma_start(out=identb[:], in_=c_ident[:]), writes=[tconst2], sem_tok=tconst2)
        P.dma("gpsimd", lambda e: e.dma_start(out=trib[:], in_=c_tri[:]), writes=[tconst2], sem_tok=tconst2)
        P.op("vector", lambda e: e.memset(onesb[:], 1.0), writes=[tconst])
        P.op("vector", lambda e: e.memset(onesf[:], 1.0), writes=[tconst])
        P.barrier()

        def rms_rstd(eng_ss, ss, rstd, tss, n):
            P.op("scalar", lambda e: e.activation(out=rstd, in_=ss, func=AF.Sqrt, bias=EPS, scale=1.0 / n), reads=[tss], writes=[tss])
            P.op("vector", lambda e: e.reciprocal(out=rstd, in_=rstd), reads=[tss], writes=[tss])

        def norm_transpose(ph, src, tsrc, gb, tgb, row0, ntok, actT, tact):
            for t in range(ntok // 128):
                xt, txt = ph["xrot"].next()
                r0 = row0 + t * 128
                P.dma("sync", lambda e, xt=xt, r0=r0: e.dma_start(out=xt[:], in_=src[r0:r0 + 128, :]), reads=[tsrc], writes=[txt], sem_tok=txt)
                xn, txn = ph["xnrot"].next()
                ss, tss = ph["ssrot"].next()
                P.op("scalar", lambda e, xt=xt, xn=xn, ss=ss: e.activation(out=xn[:], in_=xt[:], func=AF.Square, accum_out=ss[:, 0:1]), reads=[txt], writes=[txn, tss])
                rstd = ss[:, 1:2]
                rms_rstd("vector", ss[:, 0:1], rstd, tss, D)
                P.op("vector", lambda e, xt=xt, xn=xn, rstd=rstd: e.scalar_tensor_tensor(out=xn[:], in0=xt[:], scalar=rstd, in1=gb[:], op0=ALU.mult, op1=ALU.mult), reads=[txt, tss, tgb], writes=[txn])
                for half in range(2):
                    ps, tps = bank()
                    psb = ps[:].bitcast(BF16)
                    for k in range(8):
                        kc = half * 8 + k
                        P.op("tensor", lambda e, psb=psb, xn=xn, k=k, kc=kc: e.transpose(psb[:, k * 128:(k + 1) * 128], xn[:, kc * 128:(kc + 1) * 128], identb[:]), reads=[txn], writes=[tps])
                    dst = actT[:, half * 8:half * 8 + 8, t * 128:(t + 1) * 128]
                    srcp = psb[:, 0:1024].rearrange("p (k c) -> p k c", c=128)
                    if half == 0:
                        P.op("scalar", lambda e, dst=dst, srcp=srcp: e.activation(out=dst, in_=srcp, func=AF.Copy), reads=[tps], writes=[tact[t]])
                    else:
                        P.op("vector", lambda e, dst=dst, srcp=srcp: e.tensor_copy(out=dst, in_=srcp), reads=[tps], writes=[tact[t]])

        def load_w(ph, W2d, kcn, r0, c0, cw):
            buf, tk = ph["wrot"].next()
            src = W2d[r0:r0 + kcn * 128, c0:c0 + cw].rearrange("(kc p) n -> p kc n", p=128)
            dst = buf[:, 0:kcn * cw].rearrange("p (kc c) -> p kc c", c=cw)
            P.dma("gpsimd", lambda e: e.dma_start(out=dst, in_=src), writes=[tk], sem_tok=tk)
            return dst, tk

        def mm_fm(wv, wtk, j, kcn, actT, tact, tt, ntok=512):
            ps, tps = bank()
            rd = [wtk] + tact[(tt * 512) // 128:(tt * 512 + ntok) // 128]
            for kc in range(kcn):
                P.op("tensor", lambda e, ps=ps, kc=kc: e.matmul(ps[:, 0:ntok], lhsT=wv[:, kc, j * 128:(j + 1) * 128], rhs=actT[:, kc, tt * 512:tt * 512 + ntok], start=(kc == 0), stop=(kc == kcn - 1)), reads=rd, writes=[tps])
            return ps, tps

        def mm_tm(wv, wtk, cw, kcn, actT, tact, t, pst=None, first=True, last=True, kc0=0):
            ps, tps = pst if pst is not None else bank()
            rd = [wtk, tact[t]]
            for kc in range(kcn):
                P.op("tensor", lambda e, ps=ps, kc=kc: e.matmul(ps[:, 0:cw], lhsT=actT[:, kc0 + kc, t * 128:(t + 1) * 128], rhs=wv[:, kc, 0:cw], start=(first and kc == 0), stop=(last and kc == kcn - 1)), reads=rd, writes=[tps])
            return ps, tps

        def load_bcast(dst, src_row, tk, n):
            P.dma("sync", lambda e: e.dma_start(out=dst, in_=src_row.to_broadcast([128, n])), writes=[tk], sem_tok=tk)

        def make_phase(pst, TSUB, nw=3, nx=2, wsize=8192):
            ph = {}
            ph["actT"] = sb("actT", [128, KC, TSUB], BF16, pst)
            ph["tact"] = [P.tok("act%d" % i) for i in range(TSUB // 128)]
            ph["wrot"] = Rot([(sb("wb%d" % i, [128, wsize], BF16, pst), P.dtok("wb%d" % i, sw=True)) for i in range(nw)])
            ph["xrot"] = Rot([(sb("xt%d" % i, [128, D], F32, pst), P.dtok("xt%d" % i)) for i in range(nx)])
            ph["xnrot"] = Rot([(sb("xn%d" % i, [128, D], BF16, pst), P.tok("xn%d" % i)) for i in range(2)])
            ph["ssrot"] = Rot([(sb("ss%d" % i, [128, 2], F32, pst), P.tok("ss%d" % i)) for i in range(4)])
            ph["gb"] = (sb("gb", [128, D], F32, pst), P.dtok("gb"))
            ph["stb"] = Rot([(sb("stb%d" % i, [128, 512], BF16, pst), P.dtok("stb%d" % i)) for i in range(4)])
            ph["stf"] = Rot([(sb("stf%d" % i, [128, 512], F32, pst), P.dtok("stf%d" % i)) for i in range(4)])
            return ph

        dumps = {}

        def dump(name, ap, shape, tk):
            if not dbg:
                return
            dt_ = nc.dram_tensor("d_" + name, list(shape), F32, kind="ExternalOutput").ap()
            tkd = P.dtok("dump")
            P.dma("gpsimd", lambda e: e.dma_start(out=dt_, in_=ap), reads=[tk], sem_tok=tkd)

        def store(dst, src, tsrc):
            P.dma("sync", lambda e: e.dma_start(out=dst, in_=src), reads=[tsrc], sem_tok=tsrc)

        def phase_A(l, xsrc):
            with contextlib.ExitStack() as pst:
                ph = make_phase(pst, TS)
                actT, tact = ph["actT"], ph["tact"]
                gb, tgb = ph["gb"]
                load_bcast(gb[:], p_n1[l], tgb, D)
                qg = sb("qg", [128, 2], F32, pst)
                tqg = P.dtok("qg")
                P.dma("sync", lambda e: e.dma_start(out=qg[:, 0:1], in_=p_qg[l]), writes=[tqg], sem_tok=tqg)
                P.dma("sync", lambda e: e.dma_start(out=qg[:, 1:2], in_=p_kg[l]), writes=[tqg], sem_tok=tqg)
                P.op("vector", lambda e: e.tensor_scalar(out=qg[:, 1:2], in0=qg[:, 1:2], scalar1=float(np.sqrt(128.0)), scalar2=None, op0=ALU.mult), reads=[tqg], writes=[tqg])
                sq = [(sb("sq%d" % i, [128, 512], BF16, pst), P.tok("sq%d" % i)) for i in range(2)]
                sqrot = Rot(sq)
                W = w_in[l]
                tnone = P.tok("none")
                for s in range(NS):
                    g0 = s * TS
                    norm_transpose(ph, xsrc, tnone, gb, tgb, g0, TS, actT, tact)
                    NTT = TS // 512
                    for nb in range(4):
                        wv, wtk = load_w(ph, W, 16, 0, C_Z + nb * 512, 512)
                        for t in range(NTS):
                            ps, tps = mm_tm(wv, wtk, 512, 16, actT, tact, t)
                            stg, tst = ph["stb"].next()
                            P.op("scalar", lambda e, stg=stg, ps=ps: e.activation(out=stg[:], in_=ps[:], func=AF.Silu), reads=[tps], writes=[tst])
                            store(zs[g0 + t * 128:g0 + (t + 1) * 128, nb * 512:(nb + 1) * 512], stg[:], tst)
                    for nb in range(6):
                        wv, wtk = load_w(ph, W, 16, 0, C_XBC + nb * 512, 512)
                        for j in range(4):
                            for tt in range(NTT):
                                ps, tps = mm_fm(wv, wtk, j, 16, actT, tact, tt)
                                stg, tst = ph["stb"].next()
                                P.op("vector", lambda e, stg=stg, ps=ps: e.tensor_copy(out=stg[:], in_=ps[:]), reads=[tps], writes=[tst])
                                c0 = nb * 512 + j * 128
                                store(xbc_pre[c0:c0 + 128, g0 + tt * 512:g0 + (tt + 1) * 512], stg[:], tst)
                    wv, wtk = load_w(ph, W, 16, 0, C_DT, 64)
                    for t in range(NTS):
                        ps, tps = mm_tm(wv, wtk, 64, 16, actT, tact, t)
                        stg, tst = ph["stf"].next()
                        P.op("vector", lambda e, stg=stg, ps=ps: e.tensor_copy(out=stg[:, 0:64], in_=ps[:, 0:64]), reads=[tps], writes=[tst])
                        store(dts[g0 + t * 128:g0 + (t + 1) * 128, :], stg[:, 0:64], tst)
                    for qk in range(2):
                        dstT = qT if qk == 0 else kT
                        for nb in range(3):
                            wv, wtk = load_w(ph, W, 16, 0, (C_Q if qk == 0 else C_K) + nb * 512, 512)
                            for j in range(4):
                                h = nb * 4 + j
                                for tt in range(NTT):
                                    ps, tps = mm_fm(wv, wtk, j, 16, actT, tact, tt)
                                    sqb, tsq = sqrot.next()
                                    P.op("scalar", lambda e, sqb=sqb, ps=ps: e.activation(out=sqb[:], in_=ps[:], func=AF.Square), reads=[tps], writes=[tsq])
                                    ps2, tps2 = bank()
                                    P.op("tensor", lambda e, ps2=ps2, sqb=sqb: e.matmul(ps2[:], lhsT=onesb[:], rhs=sqb[:], start=True, stop=True), reads=[tsq], writes=[tps2])
                                    rr, trr = ph["stf"].next()
                                    P.op("scalar", lambda e, rr=rr, ps2=ps2: e.activation(out=rr[:], in_=ps2[:], func=AF.Sqrt, bias=128.0 * EPS), reads=[tps2], writes=[trr])
                                    P.op("vector", lambda e, rr=rr: e.reciprocal(out=rr[:], in_=rr[:]), reads=[trr], writes=[trr])
                                    stg, tst = ph["stb"].next()
                                    P.op("vector", lambda e, stg=stg, ps=ps, rr=rr, qk=qk: e.scalar_tensor_tensor(out=stg[:], in0=ps[:], scalar=qg[:, qk:qk + 1], in1=rr[:], op0=ALU.mult, op1=ALU.mult), reads=[tps, trr, tqg], writes=[tst])
                                    store(dstT[h, :, g0 + tt * 512:g0 + (tt + 1) * 512], stg[:], tst)
                    for nb in range(3):
                        wv, wtk = load_w(ph, W, 16, 0, C_V + nb * 512, 512)
                        for t in range(NTS):
                            ps, tps = mm_tm(wv, wtk, 512, 16, actT, tact, t)
                            stg, tst = ph["stb"].next()
                            P.op("scalar", lambda e, stg=stg, ps=ps: e.activation(out=stg[:], in_=ps[:], func=AF.Copy), reads=[tps], writes=[tst])
                            store(vs[g0 + t * 128:g0 + (t + 1) * 128, nb * 512:(nb + 1) * 512], stg[:], tst)
                    for nb in range(4):
                        wa, wta = load_w(ph, W, 16, 0, C_GLU + nb * 512, 512)
                        wg, wtg = load_w(ph, W, 16, 0, C_GLU + D + nb * 512, 512)
                        for j in range(4):
                            for tt in range(NTT):
                                psg, tpsg = mm_fm(wg, wtg, j, 16, actT, tact, tt)
                                sg, tsg = ph["stf"].next()
                                P.op("scalar", lambda e, sg=sg, psg=psg: e.activation(out=sg[:], in_=psg[:], func=AF.Sigmoid), reads=[tpsg], writes=[tsg])
                                psa, tpsa = mm_fm(wa, wta, j, 16, actT, tact, tt)
                                stg, tst = ph["stb"].next()
                                P.op("vector", lambda e, stg=stg, psa=psa, sg=sg: e.tensor_tensor(out=stg[:], in0=psa[:], in1=sg[:], op=ALU.mult), reads=[tpsa, tsg], writes=[tst])
                                c0 = nb * 512 + j * 128
                                store(uT[c0:c0 + 128, g0 + tt * 512:g0 + (tt + 1) * 512], stg[:], tst)
                    for nb in range(12):
                        wv, wtk = load_w(ph, W, 16, 0, C_GATE + nb * 512, 512)
                        for j in range(4):
                            for tt in range(NTT):
                                ps, tps = mm_fm(wv, wtk, j, 16, actT, tact, tt)
                                stg, tst = ph["stb"].next()
                                P.op("scalar", lambda e, stg=stg, ps=ps: e.activation(out=stg[:], in_=ps[:], func=AF.Sigmoid), reads=[tps], writes=[tst])
                                c0 = nb * 512 + j * 128
                                store(gT[c0:c0 + 128, g0 + tt * 512:g0 + (tt + 1) * 512], stg[:], tst)
                P.barrier()

        def phase_ssd(l):
            with contextlib.ExitStack() as pst:
                cw = sb("cw", [128, 24, 5], F32, pst)
                cbv = sb("cbv", [128, 24], F32, pst)
                tcw = P.dtok("cw")
                P.dma("sync", lambda e: e.dma_start(out=cw[:], in_=p_cw[l]), writes=[tcw], sem_tok=tcw)
                P.dma("sync", lambda e: e.dma_start(out=cbv[:], in_=p_cb[l]), writes=[tcw], sem_tok=tcw)
                cins = [(sb("cin%d" % i, [128, T + 4], BF16, pst), P.dtok("cin%d" % i)) for i in range(2)]
                for cin, tcin in cins:
                    P.op("vector", lambda e, cin=cin: e.memset(cin[:, 0:2], 0.0), writes=[tcin])
                    P.op("vector", lambda e, cin=cin: e.memset(cin[:, T + 2:T + 4], 0.0), writes=[tcin])
                cinrot = Rot(cins)
                accrot = Rot([(sb("cacc%d" % i, [128, T], F32, pst), P.tok("cacc%d" % i)) for i in range(2)])
                xcrot = Rot([(sb("xc%d" % i, [128, T], BF16, pst), P.dtok("xc%d" % i)) for i in range(2)])
                for c in range(24):
                    cin, tcin = cinrot.next()
                    P.dma("sync", lambda e, cin=cin, c=c: e.dma_start(out=cin[:, 2:2 + T], in_=xbc_pre[c * 128:(c + 1) * 128, :]), writes=[tcin], sem_tok=tcin)
                    acc, tacc = accrot.next()
                    veng = "vector"
                    P.op(veng, lambda e, acc=acc, cin=cin, c=c: e.tensor_scalar(out=acc[:], in0=cin[:, 0:T], scalar1=cw[:, c, 0:1], scalar2=cbv[:, c:c + 1], op0=ALU.mult, op1=ALU.add), reads=[tcin, tcw], writes=[tacc])
                    for k in range(1, 5):
                        P.op(veng, lambda e, acc=acc, cin=cin, c=c, k=k: e.scalar_tensor_tensor(out=acc[:], in0=cin[:, k:k + T], scalar=cw[:, c, k:k + 1], in1=acc[:], op0=ALU.mult, op1=ALU.add), reads=[tcin, tacc], writes=[tacc])
                    xc, txc = xcrot.next()
                    P.op("scalar", lambda e, xc=xc, acc=acc: e.activation(out=xc[:], in_=acc[:], func=AF.Silu), reads=[tacc], writes=[txc])
                    store(xbcT[c * 128:(c + 1) * 128, :], xc[:], txc)
                P.barrier()
            with contextlib.ExitStack() as pst:
                adt = sb("adt", [128, NT, 64], F32, pst)
                dec = sb("dec", [128, NT, 64], F32, pst)
                adh = sb("adh", [128, NT, 64], BF16, pst)
                adl = sb("adl", [128, NT, 64], BF16, pst)
                biasd = [sb("biasd%d" % i, [128, NT, 32], F32, pst) for i in range(2)]
                wgt = [sb("wgt%d" % i, [128, NT, 32], F32, pst) for i in range(2)]
                esc = [sb("esc%d" % i, [128, NT, 32], F32, pst) for i in range(2)]
                dskb = sb("dskb", [128, 32], F32, pst)
                DI = sb("DI", [128, 32, 128], BF16, pst)
                tprep = P.dtok("prep")
                with contextlib.ExitStack() as pst2:
                    dtr = sb("dtr", [128, NT, 64], F32, pst2)
                    dtv = sb("dtv", [128, NT, 64], F32, pst2)
                    lndt = sb("lndt", [128, NT, 64], F32, pst2)
                    cs = [sb("cs%d" % i, [128, NT, 64], F32, pst2) for i in range(4)]
                    dtb = sb("dtb", [128, 64], F32, pst2)
                    alg = sb("alg", [128, 64], F32, pst2)
                    P.dma("sync", lambda e: e.dma_start(out=dtr[:], in_=dts.rearrange("(c p) j -> p c j", p=128)), writes=[tprep], sem_tok=tprep)
                    load_bcast(dtb[:], p_dtb[l], tprep, 64)
                    load_bcast(alg[:], p_alog[l], tprep, 64)
                    load_bcast(dskb[:], p_dsk[l], tprep, 32)
                    tp = [tprep]
                    P.op("vector", lambda e: e.tensor_tensor(out=dtr[:], in0=dtr[:], in1=dtb[:].unsqueeze(1).to_broadcast([128, NT, 64]), op=ALU.add), reads=tp, writes=tp)
                    P.op("scalar", lambda e: e.activation(out=dtr[:], in_=dtr[:], func=AF.Exp), reads=tp, writes=tp)
                    P.op("scalar", lambda e: e.activation(out=dtv[:], in_=dtr[:], func=AF.Ln, bias=1.0), reads=tp, writes=tp)
                    sm = cs[0]
                    mk = cs[1]
                    P.op("vector", lambda e: e.tensor_scalar(out=sm[:], in0=dtr[:], scalar1=-0.25, scalar2=1.0 / 3.0, op0=ALU.mult, op1=ALU.add), reads=tp, writes=tp)
                    P.op("vector", lambda e: e.tensor_tensor(out=sm[:], in0=sm[:], in1=dtr[:], op=ALU.mult), reads=tp, writes=tp)
                    P.op("vector", lambda e: e.tensor_scalar(out=sm[:], in0=sm[:], scalar1=-0.5, scalar2=None, op0=ALU.add), reads=tp, writes=tp)
                    P.op("vector", lambda e: e.tensor_tensor(out=sm[:], in0=sm[:], in1=dtr[:], op=ALU.mult), reads=tp, writes=tp)
                    P.op("vector", lambda e: e.tensor_scalar(out=sm[:], in0=sm[:], scalar1=1.0, scalar2=None, op0=ALU.add), reads=tp, writes=tp)
                    P.op("vector", lambda e: e.tensor_tensor(out=sm[:], in0=sm[:], in1=dtr[:], op=ALU.mult), reads=tp, writes=tp)
                    P.op("vector", lambda e: e.tensor_scalar(out=mk[:], in0=dtr[:], scalar1=0.1, scalar2=None, op0=ALU.is_lt), reads=tp, writes=tp)
                    P.op("vector", lambda e: e.tensor_tensor(out=sm[:], in0=sm[:], in1=dtv[:], op=ALU.subtract), reads=tp, writes=tp)
                    P.op("vector", lambda e: e.tensor_tensor(out=sm[:], in0=sm[:], in1=mk[:], op=ALU.mult), reads=tp, writes=tp)
                    P.op("vector", lambda e: e.tensor_tensor(out=dtv[:], in0=dtv[:], in1=sm[:], op=ALU.add), reads=tp, writes=tp)
                    dump("dt", dtv[:], [128, NT, 64], tprep)
                    P.op("scalar", lambda e: e.activation(out=lndt[:], in_=dtv[:], func=AF.Ln), reads=tp, writes=tp)
                    P.op("scalar", lambda e: e.activation(out=alg[:], in_=alg[:], func=AF.Exp), reads=tp, writes=tp)
                    P.op("vector", lambda e: e.scalar_tensor_tensor(out=adt[:], in0=dtv[:], scalar=-1.0, in1=alg[:].unsqueeze(1).to_broadcast([128, NT, 64]), op0=ALU.mult, op1=ALU.mult), reads=tp, writes=tp)
                    P.op("vector", lambda e: e.tensor_copy(out=adh[:], in_=adt[:]), reads=tp, writes=tp)
                    P.op("vector", lambda e: e.tensor_tensor(out=dtr[:], in0=adt[:], in1=adh[:], op=ALU.subtract), reads=tp, writes=tp)
                    P.op("vector", lambda e: e.tensor_copy(out=adl[:], in_=dtr[:]), reads=tp, writes=tp)
                    adf = adt[:].rearrange("p c j -> p (c j)")
                    npc = (NT * 64 + 511) // 512
                    for m in range(5):
                        dstt = (cs[m] if m < 4 else dec)[:].rearrange("p c j -> p (c j)")
                        for pc in range(npc):
                            n0 = pc * 512
                            n1 = min(NT * 64, n0 + 512)
                            ps, tps = bank()
                            lh = tri[:, m, :] if m < 4 else onesf[:]
                            P.op("tensor", lambda e, ps=ps, lh=lh, n0=n0, n1=n1: e.matmul(ps[:, 0:n1 - n0], lhsT=lh, rhs=adf[:, n0:n1], start=True, stop=True), reads=tp + [tconst], writes=[tps])
                            if m < 4:
                                P.op("scalar", lambda e, ps=ps, dstt=dstt, n0=n0, n1=n1: e.activation(out=dstt[:, n0:n1], in_=ps[:, 0:n1 - n0], func=AF.Copy), reads=[tps], writes=tp)
                            else:
                                P.op("scalar", lambda e, ps=ps, dstt=dstt, n0=n0, n1=n1: e.activation(out=dstt[:, n0:n1], in_=ps[:, 0:n1 - n0], func=AF.Exp), reads=[tps], writes=tp)
                    P.op("vector", lambda e: e.tensor_tensor(out=biasd[0][:], in0=lndt[:, :, 0:32], in1=cs[0][:, :, 0:32], op=ALU.subtract), reads=tp, writes=tp)
                    P.op("vector", lambda e: e.tensor_tensor(out=wgt[0][:], in0=cs[1][:, :, 0:32], in1=lndt[:, :, 0:32], op=ALU.add), reads=tp, writes=tp)
                    P.op("scalar", lambda e: e.activation(out=wgt[0][:], in_=wgt[0][:], func=AF.Exp), reads=tp, writes=tp)
                    P.op("scalar", lambda e: e.activation(out=esc[0][:], in_=cs[0][:, :, 0:32], func=AF.Exp), reads=tp, writes=tp)
                    P.op("vector", lambda e: e.tensor_tensor(out=biasd[1][:], in0=cs[2][:, :, 32:64], in1=lndt[:, :, 32:64], op=ALU.add), reads=tp, writes=tp)
                    P.op("scalar", lambda e: e.activation(out=wgt[1][:], in_=biasd[1][:], func=AF.Exp), reads=tp, writes=tp)
                    P.op("scalar", lambda e: e.activation(out=esc[1][:], in_=cs[3][:, :, 32:64], func=AF.Exp), reads=tp, writes=tp)
                    dump("lndt", lndt[:], [128, NT, 64], tprep)
                    dump("adt", adt[:], [128, NT, 64], tprep)
                    dump("dec", dec[:], [128, NT, 64], tprep)
                    dump("bias1", biasd[1][:], [128, NT, 32], tprep)
                    dump("wgt1", wgt[1][:], [128, NT, 32], tprep)
                    dump("esc1", esc[1][:], [128, NT, 32], tprep)
                    dump("cs2", cs[2][:], [128, NT, 64], tprep)
                    for h in range(32):
                        P.op("vector", lambda e, h=h: e.tensor_scalar(out=DI[:, h, :], in0=identf[:], scalar1=dskb[:, h:h + 1], scalar2=None, op0=ALU.mult), reads=tp, writes=tp)
                    P.barrier()
                xcl = Rot([(sb("xcl%d" % i, [128, 24, 128], BF16, pst), P.dtok("xcl%d" % i)) for i in range(2)])
                xtmr = Rot([(sb("xtm%d" % i, [128, 2560], BF16, pst), P.tok("xtm%d" % i)) for i in range(2)])
                xwr = Rot([(sb("xw%d" % i, [128, 2048], BF16, pst), P.tok("xw%d" % i)) for i in range(2)])
                ltr = Rot([(sb("lt%d" % i, [128, 8, 128], BF16, pst), P.tok("lt%d" % i)) for i in range(2)])
                mpr = Rot([(sb("mp%d" % i, [128, 8, 128], BF16, pst), P.tok("mp%d" % i)) for i in range(2)])
                cbr = Rot([(sb("cbs%d" % i, [128, 128], F32, pst), P.tok("cbs%d" % i)) for i in range(2)])
                hst = sb("hst", [128, 2048], F32, pst)
                hbf = sb("hbf", [128, 2048], BF16, pst)
                thst = [P.tok("hst%d" % g) for g in range(4)]
                thbf = [P.tok("hbf%d" % g) for g in range(4)]
                ytr = Rot([(sb("ytmp%d" % i, [128, 512], F32, pst), P.tok("ytmp%d" % i)) for i in range(2)])
                yaccr = Rot([(sb("yacc%d" % i, [128, 2048], F32, pst), P.dtok("yacc%d" % i)) for i in range(2)])
                yblr = Rot([(sb("ybl%d" % i, [128, 2048], F32, pst), P.dtok("ybl%d" % i)) for i in range(1)])
                ztr = Rot([(sb("zt%d" % i, [128, 2048], BF16, pst), P.dtok("zt%d" % i)) for i in range(1)])
                xbv = xbcT.rearrange("(k p) t -> p k t", p=128)
                if dbg:
                    dlt = nc.dram_tensor("d_lt", [128, 1024], BF16, kind="ExternalOutput").ap()
                    dmp = nc.dram_tensor("d_mp", [128, 1024], BF16, kind="ExternalOutput").ap()
                    dcb = nc.dram_tensor("d_cb", [128, 128], F32, kind="ExternalOutput").ap()
                    tdbg = P.dtok("dbgst")
                for dr_ in (1, 0):
                    P.op("vector", lambda e: e.memset(hst[:], 0.0), writes=thst)
                    P.op("vector", lambda e: e.memset(hbf[:], 0.0), writes=thbf)
                    order = range(NT) if dr_ == 0 else range(NT - 1, -1, -1)
                    mtri = 0 if dr_ == 0 else 2
                    for c in order:
                        xa, txa = xcl.next()
                        P.dma("sync", lambda e, xa=xa, c=c: e.dma_start(out=xa[:], in_=xbv[:, :, c * 128:(c + 1) * 128]), writes=[txa], sem_tok=txa)
                        xtm, txtm = xtmr.next()
                        for part in range(3):
                            ps, tps = bank()
                            psb = ps[:].bitcast(BF16)
                            nk = 8 if part < 2 else 4
                            for k in range(nk):
                                P.op("tensor", lambda e, psb=psb, xa=xa, k=k, part=part: e.transpose(psb[:, k * 128:(k + 1) * 128], xa[:, part * 8 + k, :], identb[:]), reads=[txa], writes=[tps])
                            if part == 1:
                                P.op("vector", lambda e, xtm=xtm, psb=psb, part=part, nk=nk: e.tensor_copy(out=xtm[:, part * 1024:part * 1024 + nk * 128], in_=psb[:, 0:nk * 128]), reads=[tps], writes=[txtm])
                            else:
                                P.op("scalar", lambda e, xtm=xtm, psb=psb, part=part, nk=nk: e.activation(out=xtm[:, part * 1024:part * 1024 + nk * 128], in_=psb[:, 0:nk * 128], func=AF.Copy), reads=[tps], writes=[txtm])
                        xw, txw = xwr.next()
                        P.op("gpsimd", lambda e, xw=xw, xtm=xtm, c=c, dr_=dr_: e.tensor_tensor(out=xw[:].rearrange("p (h d) -> p h d", d=64), in0=xtm[:, 0:2048].rearrange("p (h d) -> p h d", d=64), in1=wgt[dr_][:, c, :].unsqueeze(2).to_broadcast([128, 32, 64]), op=ALU.mult), reads=[txtm], writes=[txw])
                        yacc, tyacc = yaccr.next()
                        for g in range(4):
                            psc, tpsc = bank()
                            P.op("tensor", lambda e, psc=psc, xa=xa, g=g: e.matmul(psc[:, 0:128], lhsT=xa[:, 16 + g, :], rhs=xa[:, 20 + g, :], start=True, stop=True), reads=[txa], writes=[tpsc])
                            cbs, tcbs = cbr.next()
                            P.op("scalar", lambda e, cbs=cbs, psc=psc: e.activation(out=cbs[:], in_=psc[:, 0:128], func=AF.Copy), reads=[tpsc], writes=[tcbs])
                            lt, tlt = ltr.next()
                            pds = [bank(), bank()]
                            for hh in range(8):
                                h = g * 8 + hh
                                pd, tpd = pds[hh // 4]
                                tgt = pd[:, (hh % 4) * 128:(hh % 4 + 1) * 128]
                                col = dr_ * 32 + h
                                P.op("tensor", lambda e, tgt=tgt, c=c, col=col, mtri=mtri: e.matmul(tgt, lhsT=adh[:, c, col:col + 1].to_broadcast([128, 128]), rhs=trib[:, mtri, :], start=True, stop=False), writes=[tpd])
                                P.op("tensor", lambda e, tgt=tgt, c=c, col=col, mtri=mtri: e.matmul(tgt, lhsT=adl[:, c, col:col + 1].to_broadcast([128, 128]), rhs=trib[:, mtri, :], start=False, stop=False), writes=[tpd])
                                P.op("tensor", lambda e, tgt=tgt, dr_=dr_: e.matmul(tgt, lhsT=identb[:], rhs=negb[:, dr_, :], start=False, stop=True), writes=[tpd])
                            for hh in range(8):
                                h = g * 8 + hh
                                pd, tpd = pds[hh // 4]
                                tgt = pd[:, (hh % 4) * 128:(hh % 4 + 1) * 128]
                                P.op("scalar", lambda e, lt=lt, hh=hh, tgt=tgt, c=c, h=h, dr_=dr_: e.activation(out=lt[:, hh, :], in_=tgt, func=AF.Exp, bias=biasd[dr_][:, c, h:h + 1], scale=(1.0 if dr_ == 0 else -1.0)), reads=[tpd], writes=[tlt], self_waw_ok=True)
                            mp, tmp_ = mpr.next()
                            P.op("vector", lambda e, mp=mp, lt=lt, cbs=cbs: e.tensor_tensor(out=mp[:], in0=lt[:], in1=cbs[:].unsqueeze(1).to_broadcast([128, 8, 128]), op=ALU.mult), reads=[tlt, tcbs], writes=[tmp_])
                            if dbg and dr_ == 1 and c == NT - 1 and g == 0:
                                P.dma("sync", lambda e, lt=lt: e.dma_start(out=dlt, in_=lt[:].rearrange("p a b -> p (a b)")), reads=[tlt], sem_tok=tdbg)
                                P.dma("sync", lambda e, mp=mp: e.dma_start(out=dmp, in_=mp[:].rearrange("p a b -> p (a b)")), reads=[tmp_], sem_tok=tdbg)
                                P.dma("sync", lambda e, cbs=cbs: e.dma_start(out=dcb, in_=cbs[:]), reads=[tcbs], sem_tok=tdbg)
                            psy, tpsy = bank()
                            for hh in range(8):
                                h = g * 8 + hh
                                P.op("tensor", lambda e, psy=psy, mp=mp, hh=hh, h=h, xtm=xtm, dr_=dr_: e.matmul(psy[:, hh * 64:(hh + 1) * 64], lhsT=mp[:, hh, :], rhs=xtm[:, h * 64:(h + 1) * 64], start=True, stop=(dr_ == 1), skip_group_check=True), reads=[tmp_, txtm], writes=[tpsy])
                                if dr_ == 0:
                                    P.op("tensor", lambda e, psy=psy, hh=hh, h=h, xtm=xtm: e.matmul(psy[:, hh * 64:(hh + 1) * 64], lhsT=DI[:, h, :], rhs=xtm[:, h * 64:(h + 1) * 64], start=False, stop=True, skip_group_check=True), reads=[txtm], writes=[tpsy])
                            pso, tpso = bank()
                            P.op("tensor", lambda e, pso=pso, xa=xa, g=g: e.matmul(pso[:], lhsT=xa[:, 20 + g, :], rhs=hbf[:, g * 512:(g + 1) * 512], start=True, stop=True), reads=[txa, thbf[g]], writes=[tpso])
                            yt, tyt = ytr.next()
                            P.op("vector", lambda e, yt=yt, pso=pso, c=c, g=g, dr_=dr_: e.tensor_tensor(out=yt[:].rearrange("p (h d) -> p h d", d=64), in0=pso[:].rearrange("p (h d) -> p h d", d=64), in1=esc[dr_][:, c, g * 8:(g + 1) * 8].unsqueeze(2).to_broadcast([128, 8, 64]), op=ALU.mult), reads=[tpso], writes=[tyt])
                            P.op("vector", lambda e, yacc=yacc, psy=psy, yt=yt, g=g: e.tensor_tensor(out=yacc[:, g * 512:(g + 1) * 512], in0=psy[:], in1=yt[:], op=ALU.add), reads=[tpsy, tyt], writes=[tyacc], self_waw_ok=True)
                            pss, tpss = bank()
                            P.op("tensor", lambda e, pss=pss, xtm=xtm, xw=xw, g=g: e.matmul(pss[:], lhsT=xtm[:, 2048 + g * 128:2048 + (g + 1) * 128], rhs=xw[:, g * 512:(g + 1) * 512], start=True, stop=True), reads=[txtm, txw], writes=[tpss])
                            hs = hst[:, g * 512:(g + 1) * 512]
                            P.op("vector", lambda e, hs=hs, c=c, g=g, dr_=dr_: e.tensor_tensor(out=hs.rearrange("p (h d) -> p h d", d=64), in0=hs.rearrange("p (h d) -> p h d", d=64), in1=dec[:, c, dr_ * 32 + g * 8:dr_ * 32 + (g + 1) * 8].unsqueeze(2).to_broadcast([128, 8, 64]), op=ALU.mult), reads=[thst[g]], writes=[thst[g]])
                            P.op("vector", lambda e, hs=hs, pss=pss: e.tensor_tensor(out=hs, in0=hs, in1=pss[:], op=ALU.add), reads=[thst[g], tpss], writes=[thst[g]])
                            P.op("gpsimd", lambda e, hs=hs, g=g: e.tensor_copy(out=hbf[:, g * 512:(g + 1) * 512], in_=hs), reads=[thst[g]], writes=[thbf[g]])
                        rows = slice(c * 128, (c + 1) * 128)
                        if dr_ == 1:
                            store(yb[rows, :], yacc[:], tyacc)
                        else:
                            ybl, tybl = yblr.next()
                            zt, tzt = ztr.next()
                            P.dma("sync", lambda e, ybl=ybl, rows=rows: e.dma_start(out=ybl[:], in_=yb[rows, :]), writes=[tybl], sem_tok=tybl)
                            P.dma("sync", lambda e, zt=zt, rows=rows: e.dma_start(out=zt[:], in_=zs[rows, :]), writes=[tzt], sem_tok=tzt)
                            P.op("gpsimd", lambda e, yacc=yacc, ybl=ybl: e.tensor_tensor(out=yacc[:], in0=yacc[:], in1=ybl[:], op=ALU.add), reads=[tyacc, tybl], writes=[tyacc])
                            P.op("gpsimd", lambda e, yacc=yacc, zt=zt: e.tensor_tensor(out=yacc[:], in0=yacc[:], in1=zt[:], op=ALU.mult), reads=[tyacc, tzt], writes=[tyacc])
                            store(yg[rows, :], yacc[:], tyacc)
                    P.barrier()


        def phase_attn(l):
            with contextlib.ExitStack() as pst:
                qr = Rot([(sb("qsb%d" % i, [128, T], BF16, pst), P.dtok("qsb%d" % i)) for i in range(2)])
                kr = Rot([(sb("ksb%d" % i, [128, T], BF16, pst), P.dtok("ksb%d" % i)) for i in range(2)])
                vr = Rot([(sb("vt%d" % i, [128, NT, 128], BF16, pst), P.dtok("vt%d" % i)) for i in range(2)])
                num = sb("num", [128, T], F32, pst)
                den = sb("den", [128, T], F32, pst)
                tnum, tden = P.tok("num"), P.tok("den")
                osb = sb("osb", [128, T], BF16, pst)
                tosb = P.dtok("osb")
                ptr = Rot([(sb("pT%d" % i, [128, 128], BF16, pst), P.tok("pT%d" % i)) for i in range(3)])
                for j in range(4):
                    for g in range(3):
                        h = 4 * g + j
                        d = ATTN_PATTERNS[g][1]
                        S = T // d
                        NK = S // 128
                        qs, tqs = qr.next()
                        ks, tks = kr.next()
                        vt, tvt = vr.next()
                        P.dma("sync", lambda e, qs=qs, h=h: e.dma_start(out=qs[:], in_=qT[h]), writes=[tqs], sem_tok=tqs)
                        P.dma("sync", lambda e, ks=ks, h=h: e.dma_start(out=ks[:], in_=kT[h]), writes=[tks], sem_tok=tks)
                        vv = vt[:].rearrange("a (r kt) e -> a r kt e", r=d)
                        for r in range(d):
                            P.dma("sync", lambda e, vv=vv, h=h, d=d, r=r: e.dma_start(out=vv[:, r], in_=vs[:, h * 128:(h + 1) * 128].rearrange("(kt a r) e -> a r kt e", a=128, r=d)[:, r]), writes=[tvt], sem_tok=tvt)
                        for r in range(d):
                            for m in range(NK + 1):
                                b0 = 64 if m == 0 else 0
                                b1 = 64 if m == NK else 128
                                nq = b1 - b0
                                i0 = 128 * m - 64 + b0
                                q0 = i0 * d + r
                                qsl = qs[:, q0:q0 + (nq - 1) * d + 1:d]
                                pso, tpso = bank()
                                psd, tpsd = bank()
                                kts = [kt for kt in (m - 1, m) if 0 <= kt < NK]
                                for idx, kt in enumerate(kts):
                                    typ = 0 if kt == m - 1 else 1
                                    pss, tpss = bank()
                                    k0 = 128 * kt * d + r
                                    ksl = ks[:, k0:k0 + 127 * d + 1:d]
                                    P.op("tensor", lambda e, pss=pss, ksl=ksl, qsl=qsl, nq=nq: e.matmul(pss[:, 0:nq], lhsT=ksl, rhs=qsl, start=True, stop=False), reads=[tqs, tks], writes=[tpss])
                                    P.op("tensor", lambda e, pss=pss, nq=nq, h=h, typ=typ, b0=b0, b1=b1: e.matmul(pss[:, 0:nq], lhsT=identf[:], rhs=abias[:, h * 2 + typ, b0:b1], start=False, stop=True), writes=[tpss])
                                    pt, tpt = ptr.next()
                                    P.op("scalar", lambda e, pt=pt, pss=pss, nq=nq: e.activation(out=pt[:, 0:nq], in_=pss[:, 0:nq], func=AF.Exp), reads=[tpss], writes=[tpt])
                                    fl = dict(start=(idx == 0), stop=(idx == len(kts) - 1))
                                    P.op("tensor", lambda e, pso=pso, vv=vv, r=r, kt=kt, pt=pt, nq=nq, fl=fl: e.matmul(pso[:, 0:nq], lhsT=vv[:, r, kt, :], rhs=pt[:, 0:nq], **fl), reads=[tvt, tpt], writes=[tpso])
                                    P.op("tensor", lambda e, psd=psd, pt=pt, nq=nq, fl=fl: e.matmul(psd[:, 0:nq], lhsT=onesb[:], rhs=pt[:, 0:nq], **fl), reads=[tpt], writes=[tpsd])
                                nsl = num[:, q0:q0 + (nq - 1) * d + 1:d]
                                dsl = den[:, q0:q0 + (nq - 1) * d + 1:d]
                                if g == 0:
                                    P.op("vector", lambda e, nsl=nsl, pso=pso, nq=nq: e.tensor_copy(out=nsl, in_=pso[:, 0:nq]), reads=[tpso], writes=[tnum], self_waw_ok=True)
                                    P.op("scalar", lambda e, dsl=dsl, psd=psd, nq=nq: e.activation(out=dsl, in_=psd[:, 0:nq], func=AF.Copy), reads=[tpsd], writes=[tden], self_waw_ok=True)
                                else:
                                    P.op("vector", lambda e, nsl=nsl, pso=pso, nq=nq: e.tensor_tensor(out=nsl, in0=nsl, in1=pso[:, 0:nq], op=ALU.add), reads=[tpso], writes=[tnum], self_waw_ok=True)
                                    P.op("vector", lambda e, dsl=dsl, psd=psd, nq=nq: e.tensor_tensor(out=dsl, in0=dsl, in1=psd[:, 0:nq], op=ALU.add), reads=[tpsd], writes=[tden], self_waw_ok=True)
                    P.op("vector", lambda e: e.reciprocal(out=den[:], in_=den[:]), reads=[tden], writes=[tden])
                    P.op("vector", lambda e: e.tensor_tensor(out=osb[:], in0=num[:], in1=den[:], op=ALU.mult), reads=[tnum, tden], writes=[tosb])
                    store(oT[j * 128:(j + 1) * 128, :], osb[:], tosb)
                P.barrier()

        def phase_conf(l):
            with contextlib.ExitStack() as pst:
                dwp = sb("dwp", [128, 16, 31], F32, pst)
                dwb = sb("dwb", [128, 16], F32, pst)
                tdw = P.dtok("dw")
                P.dma("sync", lambda e: e.dma_start(out=dwp[:], in_=p_dw[l]), writes=[tdw], sem_tok=tdw)
                P.dma("sync", lambda e: e.dma_start(out=dwb[:], in_=p_dwb[l]), writes=[tdw], sem_tok=tdw)
                cins = [(sb("ccin%d" % i, [128, T + 30], BF16, pst), P.dtok("ccin%d" % i)) for i in range(2)]
                for cin, tcin in cins:
                    P.op("vector", lambda e, cin=cin: e.memset(cin[:, 0:15], 0.0), writes=[tcin])
                    P.op("vector", lambda e, cin=cin: e.memset(cin[:, T + 15:T + 30], 0.0), writes=[tcin])
                cinrot = Rot(cins)
                dgr = Rot([(sb("dg%d" % i, [128, 31, 128], BF16, pst), P.tok("dg%d" % i)) for i in range(2)])
                str_ = Rot([(sb("cst%d" % i, [128, 512], BF16, pst), P.dtok("cst%d" % i)) for i in range(3)])
                for cc in range(16):
                    cin, tcin = cinrot.next()
                    P.dma("sync", lambda e, cin=cin, cc=cc: e.dma_start(out=cin[:, 15:15 + T], in_=uT[cc * 128:(cc + 1) * 128, :]), writes=[tcin], sem_tok=tcin)
                    dg, tdg = dgr.next()
                    for k in range(31):
                        eng = "vector" if k % 2 == 0 else "gpsimd"
                        P.op(eng, lambda e, dg=dg, k=k, cc=cc: e.tensor_scalar(out=dg[:, k, :], in0=identf[:], scalar1=dwp[:, cc, k:k + 1], scalar2=None, op0=ALU.mult), reads=[tdw], writes=[tdg], self_waw_ok=True)
                    P.op("vector", lambda e, dg=dg: e.tensor_copy(out=dg[:, 30, 0:1], in_=dg[:, 30, 0:1]), reads=[tdg], writes=[tdg])
                    for tt in range(T // 512):
                        ps, tps = bank()
                        for k in range(31):
                            P.op("tensor", lambda e, ps=ps, dg=dg, cin=cin, k=k, tt=tt: e.matmul(ps[:], lhsT=dg[:, k, :], rhs=cin[:, tt * 512 + k:tt * 512 + k + 512], start=(k == 0), stop=(k == 30)), reads=[tdg, tcin], writes=[tps])
                        stg, tst = str_.next()
                        P.op("scalar", lambda e, stg=stg, ps=ps, cc=cc: e.activation(out=stg[:], in_=ps[:], func=AF.Identity, bias=dwb[:, cc:cc + 1]), reads=[tps, tdw], writes=[tst])
                        store(ycT[cc * 128:(cc + 1) * 128, tt * 512:(tt + 1) * 512], stg[:], tst)
                P.barrier()
            with contextlib.ExitStack() as pst:
                lng = sb("lng", [128, 16], F32, pst)
                lnb = sb("lnb", [128, 16], F32, pst)
                tln = P.dtok("ln")
                P.dma("sync", lambda e: e.dma_start(out=lng[:], in_=p_lng[l]), writes=[tln], sem_tok=tln)
                P.dma("sync", lambda e: e.dma_start(out=lnb[:], in_=p_lnb[l]), writes=[tln], sem_tok=tln)
                ylr = Rot([(sb("yl%d" % i, [128, 16, 512], BF16, pst), P.dtok("yl%d" % i)) for i in range(2)])
                sqr = Rot([(sb("sqy%d" % i, [128, 16, 512], BF16, pst), P.tok("sqy%d" % i)) for i in range(1)])
                mur = Rot([(sb("mu%d" % i, [128, 3, 512], F32, pst), P.tok("mu%d" % i)) for i in range(2)])
                t1r = Rot([(sb("t1%d" % i, [128, 512], F32, pst), P.tok("t1%d" % i)) for i in range(3)])
                str_ = Rot([(sb("cst2%d" % i, [128, 512], BF16, pst), P.dtok("cst2%d" % i)) for i in range(3)])
                ycv = ycT.rearrange("(k p) t -> p k t", p=128)
                for tt in range(T // 512):
                    yl, tyl = ylr.next()
                    P.dma("sync", lambda e, yl=yl, tt=tt: e.dma_start(out=yl[:], in_=ycv[:, :, tt * 512:(tt + 1) * 512]), writes=[tyl], sem_tok=tyl)
                    sqy, tsq = sqr.next()
                    P.op("gpsimd", lambda e, sqy=sqy, yl=yl: e.tensor_tensor(out=sqy[:], in0=yl[:], in1=yl[:], op=ALU.mult), reads=[tyl], writes=[tsq])
                    ps1, tps1 = bank()
                    ps2, tps2 = bank()
                    for cc in range(16):
                        P.op("tensor", lambda e, ps1=ps1, yl=yl, cc=cc: e.matmul(ps1[:], lhsT=onesb[:], rhs=yl[:, cc, :], start=(cc == 0), stop=(cc == 15)), reads=[tyl], writes=[tps1])
                    for cc in range(16):
                        P.op("tensor", lambda e, ps2=ps2, sqy=sqy, cc=cc: e.matmul(ps2[:], lhsT=onesb[:], rhs=sqy[:, cc, :], start=(cc == 0), stop=(cc == 15)), reads=[tsq], writes=[tps2])
                    mu, tmu = mur.next()
                    P.op("scalar", lambda e, mu=mu, ps1=ps1: e.activation(out=mu[:, 0, :], in_=ps1[:], func=AF.Copy, scale=1.0 / D), reads=[tps1], writes=[tmu])
                    P.op("vector", lambda e, mu=mu: e.tensor_tensor(out=mu[:, 1, :], in0=mu[:, 0, :], in1=mu[:, 0, :], op=ALU.mult), reads=[tmu], writes=[tmu])
                    P.op("vector", lambda e, mu=mu, ps2=ps2: e.scalar_tensor_tensor(out=mu[:, 2, :], in0=ps2[:], scalar=1.0 / D, in1=mu[:, 1, :], op0=ALU.mult, op1=ALU.subtract), reads=[tmu, tps2], writes=[tmu])
                    P.op("scalar", lambda e, mu=mu: e.activation(out=mu[:, 2, :], in_=mu[:, 2, :], func=AF.Sqrt, bias=EPS), reads=[tmu], writes=[tmu])
                    P.op("vector", lambda e, mu=mu: e.reciprocal(out=mu[:, 2, :], in_=mu[:, 2, :]), reads=[tmu], writes=[tmu])
                    for cc in range(16):
                        t1, tt1 = t1r.next()
                        eng = "vector" if cc % 2 == 0 else "gpsimd"
                        P.op(eng, lambda e, t1=t1, yl=yl, mu=mu, cc=cc: e.tensor_tensor(out=t1[:], in0=yl[:, cc, :], in1=mu[:, 0, :], op=ALU.subtract), reads=[tyl, tmu], writes=[tt1])
                        P.op(eng, lambda e, t1=t1, mu=mu: e.tensor_tensor(out=t1[:], in0=t1[:], in1=mu[:, 2, :], op=ALU.mult), reads=[tt1, tmu], writes=[tt1])
                        stg, tst = str_.next()
                        P.op("scalar", lambda e, stg=stg, t1=t1, cc=cc: e.activation(out=stg[:], in_=t1[:], func=AF.Silu, bias=lnb[:, cc:cc + 1], scale=lng[:, cc:cc + 1]), reads=[tt1, tln], writes=[tst])
                        store(cT[cc * 128:(cc + 1) * 128, tt * 512:(tt + 1) * 512], stg[:], tst)
                P.barrier()

        def phase_B(l, xsrc):
            with contextlib.ExitStack() as pst:
                TB = min(1024, T)
                NTB = TB // 128
                CW = 256
                NJ = CW // 128
                ph = make_phase(pst, TB, nw=6, wsize=16 * CW)
                actT, tact = ph["actT"], ph["tact"]
                gb, tgb = ph["gb"]
                load_bcast(gb[:], p_ng[l], tgb, D)
                mergedT = sb("mergedT", [128, KC, TB], BF16, pst)
                NTT = TB // 512
                tm = [P.tok("mer%d" % i) for i in range(NTT)]
                tmer_list = [tm[t // 4] for t in range(NTB)]
                glr = Rot([(sb("gl%d" % i, [128, 512], BF16, pst), P.dtok("gl%d" % i)) for i in range(3)])
                tactld = P.dtok("actld")
                tnone = P.tok("none")

                def branch(W, kcn, br, first, g0):
                    for nb in range(D // CW):
                        wv, wtk = load_w(ph, W, kcn, 0, nb * CW, CW)
                        for j in range(NJ):
                            cb = nb * NJ + j
                            for tt in range(NTT):
                                gl, tgl = glr.next()
                                P.dma("sync", lambda e, gl=gl, cb=cb, tt=tt: e.dma_start(out=gl[:], in_=gT[br * D + cb * 128:br * D + (cb + 1) * 128, g0 + tt * 512:g0 + (tt + 1) * 512]), writes=[tgl], sem_tok=tgl)
                                ps, tps = mm_fm(wv, wtk, j, kcn, actT, tact, tt)
                                msl = mergedT[:, cb, tt * 512:(tt + 1) * 512]
                                if first:
                                    P.op("vector", lambda e, msl=msl, ps=ps, gl=gl: e.tensor_tensor(out=msl, in0=ps[:], in1=gl[:], op=ALU.mult), reads=[tps, tgl], writes=[tm[tt]], self_waw_ok=True)
                                else:
                                    tmpb, ttmp = ph["stf"].next()
                                    P.op("vector", lambda e, tmpb=tmpb, ps=ps, gl=gl: e.tensor_tensor(out=tmpb[:], in0=ps[:], in1=gl[:], op=ALU.mult), reads=[tps, tgl], writes=[ttmp])
                                    P.op("vector", lambda e, msl=msl, tmpb=tmpb: e.tensor_tensor(out=msl, in0=msl, in1=tmpb[:], op=ALU.add), reads=[ttmp], writes=[tm[tt]], self_waw_ok=True)

                for s in range(T // TB):
                    g0 = s * TB
                    norm_transpose(ph, yg, tnone, gb, tgb, g0, TB, actT, tact)
                    branch(w_ssd_o[l], 16, 0, True, g0)
                    for kc in range(4):
                        P.dma("sync", lambda e, kc=kc, g0=g0: e.dma_start(out=actT[:, kc, :], in_=oT[kc * 128:(kc + 1) * 128, g0:g0 + TB]), writes=tact, sem_tok=tactld)
                    branch(w_attn_o[l], 4, 1, False, g0)
                    for kc in range(16):
                        P.dma("sync", lambda e, kc=kc, g0=g0: e.dma_start(out=actT[:, kc, :], in_=cT[kc * 128:(kc + 1) * 128, g0:g0 + TB]), writes=tact, sem_tok=tactld)
                    branch(w_conv_o[l], 16, 2, False, g0)
                    for nb in range(D // CW):
                        wv, wtk = load_w(ph, w_out[l], 16, 0, nb * CW, CW)
                        for t in range(NTB):
                            xo, txo = ph["stf"].next()
                            rows = slice(g0 + t * 128, g0 + (t + 1) * 128)
                            P.dma("sync", lambda e, xo=xo, rows=rows, nb=nb: e.dma_start(out=xo[:, 0:CW], in_=xsrc[rows, nb * CW:(nb + 1) * CW]), writes=[txo], sem_tok=txo)
                            ps, tps = mm_tm(wv, wtk, CW, 16, mergedT, tmer_list, t)
                            P.op("vector", lambda e, xo=xo, ps=ps: e.tensor_tensor(out=xo[:, 0:CW], in0=xo[:, 0:CW], in1=ps[:, 0:CW], op=ALU.add), reads=[tps, txo], writes=[txo])
                            store(xres[rows, nb * CW:(nb + 1) * CW], xo[:, 0:CW], txo)
                P.barrier()

        def phase_mlp(l, dst):
            with contextlib.ExitStack() as pst:
                TM = min(1024, T)
                NTM = TM // 128
                NTT = TM // 512
                HH = 4096
                CW = 256
                NJ = CW // 128
                ph = make_phase(pst, TM, nw=4, nx=2, wsize=16 * CW)
                actT, tact = ph["actT"], ph["tact"]
                gb, tgb = ph["gb"]
                load_bcast(gb[:], p_n2[l], tgb, D)
                aT = sb("aT", [128, 32, TM], BF16, pst)
                ta = [P.tok("aT%d" % i) for i in range(2)]
                rr = Rot([(sb("relu%d" % i, [128, 512], BF16, pst), P.tok("relu%d" % i)) for i in range(3)])
                tnone = P.tok("none")
                for ti in range(T // TM):
                    r0 = ti * TM
                    norm_transpose(ph, xres, tnone, gb, tgb, r0, TM, actT, tact)
                    tdr = [[P.tok("xd") for _ in range(D // CW)] for _ in range(NTM)]
                    for hh in range(2):
                        for nb in range(HH // CW):
                            wv, wtk = load_w(ph, w_up[l], 16, 0, hh * HH + nb * CW, CW)
                            for j in range(NJ):
                                hc = nb * NJ + j
                                for tt in range(NTT):
                                    ps, tps = mm_fm(wv, wtk, j, 16, actT, tact, tt)
                                    rl, trl = rr.next()
                                    P.op("scalar", lambda e, rl=rl, ps=ps: e.activation(out=rl[:], in_=ps[:], func=AF.Relu), reads=[tps], writes=[trl])
                                    P.op("vector", lambda e, rl=rl, hc=hc, tt=tt: e.tensor_tensor(out=aT[:, hc, tt * 512:(tt + 1) * 512], in0=rl[:], in1=rl[:], op=ALU.mult), reads=[trl], writes=[ta[hc // 16]], self_waw_ok=True)
                        for nb in range(D // CW):
                            b8 = [bank() for _ in range(NTM)]
                            for kq in range(2):
                                wv, wtk = load_w(ph, w_down[l], 16, hh * HH + kq * 2048, nb * CW, CW)
                                for t in range(NTM):
                                    mm_tm(wv, wtk, CW, 16, aT, [ta[kq]] * NTM, t, pst=b8[t], first=(kq == 0), last=(kq == 1), kc0=kq * 16)
                            src = xres if hh == 0 else dst
                            for t in range(NTM):
                                ps, tps = b8[t]
                                xo, txo = ph["stf"].next()
                                rows = slice(r0 + t * 128, r0 + (t + 1) * 128)
                                P.dma("sync", lambda e, xo=xo, rows=rows, nb=nb, src=src: e.dma_start(out=xo[:, 0:CW], in_=src[rows, nb * CW:(nb + 1) * CW]), reads=[tdr[t][nb]], writes=[txo], sem_tok=txo)
                                P.op("vector", lambda e, xo=xo, ps=ps: e.tensor_tensor(out=xo[:, 0:CW], in0=xo[:, 0:CW], in1=ps[:, 0:CW], op=ALU.add), reads=[tps, txo], writes=[txo])
                                P.dma("sync", lambda e, xo=xo, rows=rows, nb=nb: e.dma_start(out=dst[rows, nb * CW:(nb + 1) * CW], in_=xo[:, 0:CW]), reads=[txo], writes=[tdr[t][nb]], sem_tok=txo)
                P.barrier()

        for l in range(depth):
            xsrc = x_in if l == 0 else xres
            for fn, args in [(phase_A, (l, xsrc)), (phase_ssd, (l,)), (phase_attn, (l,)), (phase_conf, (l,)),
                             (phase_B, (l, xsrc)), (phase_mlp, (l, out if l == depth - 1 else xres))]:
                if phases is not None and fn.__name__ not in phases:
                    continue
                P.begin_phase()
                fn(*args)
                P.end_phase()
        P.emit(block)
    return nc


_WNAMES = ["w_in", "w_ssd_o", "w_attn_o", "w_conv_o", "w_out", "w_mlp_up", "w_mlp_down"]


def run_cores(x_list, inputs, T, depth=DEPTH, dbg=False, phases=None):
    nc = build(T, depth=depth, dbg=dbg, phases=phases)
    consts = host_consts()
    lay = host_layout(inputs)
    common = {}
    for k in _WNAMES:
        common[k] = np.ascontiguousarray(inputs[k][:depth])
    for k, v in lay.items():
        common[k] = np.ascontiguousarray(v[:depth])
    common.update(consts)
    zeros = {k: np.zeros_like(v) for k, v in common.items()}
    in_maps = []
    for xb in x_list:
        if xb is None:
            m = dict(zeros)
            m["x"] = np.zeros((T, D), np.float32)
        else:
            m = dict(common)
            m["x"] = np.ascontiguousarray(xb)
        in_maps.append(m)
    res = run_bass_kernel_spmd(nc, in_maps, core_ids=list(range(len(x_list))))
    return res.results


def kernel(**inputs):
    x = np.asarray(inputs["x"], dtype=np.float32)
    B, T, _ = x.shape
    inp = {k: np.asarray(v, dtype=np.float32) for k, v in inputs.items()}
    cores = [0, 1, 4, 5]
    x_list = [None] * 8
    for b in range(B):
        x_list[cores[b]] = x[b]
    results = run_cores(x_list, inp, T)
    return np.stack([results[cores[b]]["out"] for b in range(B)], axis=0).astype(np.float32)
```

```python
import contextlib
import numpy as np
import concourse.bass as bass
import concourse.mybir as mybir
from concourse.bass_utils import run_bass_kernel_spmd

F32 = mybir.dt.float32
BF16 = mybir.dt.bfloat16
AF = mybir.ActivationFunctionType
ALU = mybir.AluOpType

ENGS = ["tensor", "vector", "scalar", "gpsimd", "sync"]
EPOCH = 30000

D = 2048
KC = 16
N_IN = 20032
C_Z, C_XBC, C_DT, C_Q, C_K, C_V, C_GLU, C_GATE = 0, 2048, 5120, 5184, 6720, 8256, 9792, 13888
EPS = 1e-6
DEPTH = 2


class Tok:
    __slots__ = ("name", "w", "r", "dsem")

    def __init__(self, name=""):
        self.name = name
        self.w = None
        self.r = {}
        self.dsem = None


class Prog:
    def __init__(self, nc, stack):
        self.nc = nc
        self.stack = stack
        self.q = {e: [] for e in ENGS}
        self.cnt = {e: 0 for e in ENGS}
        self.sems = {}
        self.waited = {e: {} for e in ENGS}
        self.nsem = 0
        self.dcount = {}
        self.free_dsems = {False: [], True: []}
        self.phase_sems = None

    def begin_phase(self):
        self.phase_sems = []

    def end_phase(self):
        for sw, k in self.phase_sems:
            self.free_dsems[sw].append(k)
        self.phase_sems = None

    def _sem(self, key):
        if key not in self.sems:
            self.sems[key] = self.stack.enter_context(self.nc.semaphore("s%d" % self.nsem))
            self.nsem += 1
        return self.sems[key]

    def tok(self, name=""):
        return Tok(name)

    def dtok(self, name="", sw=False):
        t = Tok(name)
        if self.free_dsems[sw]:
            t.dsem = self.free_dsems[sw].pop()
        else:
            t.dsem = ("d", self.nsem, name)
            self._sem(t.dsem)
            self.dcount[t.dsem] = 0
        if self.phase_sems is not None:
            self.phase_sems.append((sw, t.dsem))
        return t

    def _deps(self, eng, reads, writes, self_waw_ok):
        deps = {}

        def add(d):
            if d is None:
                return
            k, v = d
            if deps.get(k, 0) < v:
                deps[k] = v
        for t in reads:
            add(t.w)
        for t in writes:
            if t.w is not None:
                if not (self_waw_ok and t.w[0][0] == "e" and t.w[0][1] == eng):
                    add(t.w)
            for k, v in t.r.items():
                if k[0] == "e" and k[1] == eng:
                    continue
                add((k, v))
        out = []
        wd = self.waited[eng]
        for k, v in deps.items():
            if eng == "tensor" and k[0] == "e" and k[1] == "tensor":
                continue
            if wd.get(k, 0) >= v:
                continue
            wd[k] = v
            out.append((k, v))
        return out

    def _mark(self, comp, reads, writes):
        k, v = comp
        for t in reads:
            if t.r.get(k, 0) < v:
                t.r[k] = v
        for t in writes:
            t.w = comp
            t.r = {}

    def op(self, eng, fn, reads=(), writes=(), self_waw_ok=False):
        waits = self._deps(eng, reads, writes, self_waw_ok)
        self.cnt[eng] += 1
        ep, v = divmod(self.cnt[eng] - 1, EPOCH)
        key = ("e", eng, ep)
        comp = (key, v + 1)
        self._sem(key)
        self.q[eng].append((waits, fn, key, 1))
        self._mark(comp, reads, writes)
        return comp

    def dma(self, eng, fn, reads=(), writes=(), sem_tok=None):
        waits = self._deps(eng, reads, writes, False)
        key = sem_tok.dsem
        self.dcount[key] += 16
        comp = (key, self.dcount[key])
        self.q[eng].append((waits, fn, key, 16))
        self._mark(comp, reads, writes)
        return comp

    def barrier(self):
        allw = []
        for e in ENGS:
            if self.cnt[e] > 0:
                ep, v = divmod(self.cnt[e] - 1, EPOCH)
                allw.append((("e", e, ep), v + 1))
        for k, v in self.dcount.items():
            if v > 0:
                allw.append((k, v))
        for e in ENGS:
            wd = self.waited[e]
            waits = []
            for k, v in allw:
                if k[0] == "e" and k[1] == e:
                    continue
                if wd.get(k, 0) >= v:
                    continue
                wd[k] = v
                waits.append((k, v))
            if waits:
                self.q[e].append((waits, None, None, 0))

    def emit(self, block):
        fin = [(k, v) for k, v in self.dcount.items() if v > 0]
        sems = self.sems
        q = self.q

        def runner(e):
            def run(engine):
                for waits, fn, key, inc in q[e]:
                    for k, v in waits:
                        engine.wait_ge(sems[k], v)
                    if fn is None:
                        continue
                    ins = fn(engine)
                    ins.then_inc(sems[key], inc)
                if e == "sync":
                    for k, v in fin:
                        engine.wait_ge(sems[k], v)
            return run
        block.tensor(runner("tensor"))
        block.vector(runner("vector"))
        block.scalar(runner("scalar"))
        block.gpsimd(runner("gpsimd"))
        block.sync(runner("sync"))


class Rot:
    def __init__(self, items):
        self.items = items
        self.i = 0

    def next(self):
        it = self.items[self.i % len(self.items)]
        self.i += 1
        return it


ATTN_PATTERNS = ((128, 1), (512, 4), (2048, 16))


def host_consts():
    c = {}
    i = np.arange(128)
    lp, l = i[:, None], i[None, :]
    c["c_ident"] = np.eye(128, dtype=np.float32)
    tri = np.stack([(lp <= l), (lp > l), (lp < l), (lp >= l)], axis=1).astype(np.float32)
    c["c_tri"] = np.ascontiguousarray(tri)
    s, ll = i[:, None], i[None, :]
    neg = np.stack([np.where(ll < s, -30000.0, 0.0), np.where(ll > s, 30000.0, 0.0)], axis=1).astype(np.float32)
    c["c_neg"] = np.ascontiguousarray(neg)
    slopes = np.exp2(-8.0 * np.arange(1, 13, dtype=np.float64) / 12.0)
    a, b = i[:, None], i[None, :]
    ab = np.zeros((128, 12, 2, 128), np.float32)
    for h in range(12):
        d = ATTN_PATTERNS[h // 4][1]
        relA = a - b - 64
        relB = a - b + 64
        ab[:, h, 0, :] = np.where(a >= b, -slopes[h] * d * np.abs(relA), -30000.0)
        ab[:, h, 1, :] = np.where(a <= b, -slopes[h] * d * np.abs(relB), -30000.0)
    c["c_abias"] = ab.reshape(128, 24, 128)
    return c


def host_layout(inp):
    L = DEPTH
    o = {}
    o["p_cw"] = np.ascontiguousarray(inp["ssd_conv_w"].transpose(0, 2, 1).reshape(L, 24, 128, 5).transpose(0, 2, 1, 3))
    o["p_cb"] = np.ascontiguousarray(inp["ssd_conv_b"].reshape(L, 24, 128).transpose(0, 2, 1))
    o["p_dw"] = np.ascontiguousarray(inp["conv_dw_w"].transpose(0, 2, 1).reshape(L, 16, 128, 31).transpose(0, 2, 1, 3))
    o["p_dwb"] = np.ascontiguousarray(inp["conv_dw_b"].reshape(L, 16, 128).transpose(0, 2, 1))
    o["p_lng"] = np.ascontiguousarray(inp["conv_ln_g"].reshape(L, 16, 128).transpose(0, 2, 1))
    o["p_lnb"] = np.ascontiguousarray(inp["conv_ln_b"].reshape(L, 16, 128).transpose(0, 2, 1))
    o["p_dtb"] = np.ascontiguousarray(inp["ssd_dt_bias"].reshape(L, 1, 64))
    o["p_alog"] = np.ascontiguousarray(inp["ssd_a_log"].reshape(L, 1, 64))
    o["p_dsk"] = np.ascontiguousarray(inp["ssd_d"].reshape(L, 1, 32))
    o["p_qg"] = np.ascontiguousarray(inp["q_norm_g"].reshape(L, 128, 1))
    o["p_kg"] = np.ascontiguousarray(inp["k_norm_g"].reshape(L, 128, 1))
    for k in ["norm1_g", "norm2_g", "ssd_norm_g"]:
        o["p_" + k] = np.ascontiguousarray(inp[k].reshape(L, 1, D))
    return o


def build(T, depth=DEPTH, dbg=False, phases=None):
    nc = bass.Bass("TRN2", target_bir_lowering=False)
    TS = min(2048, T)
    NS = T // TS
    NT = T // 128
    NTS = TS // 128

    def din(name, shape, dt=F32):
        return nc.dram_tensor(name, list(shape), dt, kind="ExternalInput").ap()

    def dscr(name, shape, dt):
        return nc.dram_tensor(name, list(shape), dt, kind=("ExternalOutput" if dbg else "Internal")).ap()

    x_in = din("x", [T, D])
    w_in = din("w_in", [depth, D, N_IN])
    w_ssd_o = din("w_ssd_o", [depth, D, D])
    w_attn_o = din("w_attn_o", [depth, 512, D])
    w_conv_o = din("w_conv_o", [depth, D, D])
    w_out = din("w_out", [depth, D, D])
    w_up = din("w_mlp_up", [depth, D, 4 * D])
    w_down = din("w_mlp_down", [depth, 4 * D, D])
    p_cw = din("p_cw", [depth, 128, 24, 5])
    p_cb = din("p_cb", [depth, 128, 24])
    p_dw = din("p_dw", [depth, 128, 16, 31])
    p_dwb = din("p_dwb", [depth, 128, 16])
    p_lng = din("p_lng", [depth, 128, 16])
    p_lnb = din("p_lnb", [depth, 128, 16])
    p_dtb = din("p_dtb", [depth, 1, 64])
    p_alog = din("p_alog", [depth, 1, 64])
    p_dsk = din("p_dsk", [depth, 1, 32])
    p_qg = din("p_qg", [depth, 128, 1])
    p_kg = din("p_kg", [depth, 128, 1])
    p_n1 = din("p_norm1_g", [depth, 1, D])
    p_n2 = din("p_norm2_g", [depth, 1, D])
    p_ng = din("p_ssd_norm_g", [depth, 1, D])
    c_ident = din("c_ident", [128, 128])
    c_tri = din("c_tri", [128, 4, 128])
    c_neg = din("c_neg", [128, 2, 128])
    c_abias = din("c_abias", [128, 24, 128])
    out = nc.dram_tensor("out", [T, D], F32, kind="ExternalOutput").ap()

    xres = dscr("xres", [T, D], F32)
    zs = dscr("zs", [T, D], BF16)
    xbc_pre = dscr("xbc_pre", [3072, T], BF16)
    xbcT = dscr("xbcT", [3072, T], BF16)
    dts = dscr("dts", [T, 64], F32)
    qT = dscr("qT", [12, 128, T], BF16)
    kT = dscr("kT", [12, 128, T], BF16)
    vs = dscr("vs", [T, 1536], BF16)
    uT = dscr("uT", [D, T], BF16)
    gT = dscr("gT", [3 * D, T], BF16)
    yb = dscr("yb", [T, D], F32)
    yg = dscr("yg", [T, D], F32)
    oT = dscr("oT", [512, T], BF16)
    ycT = dscr("ycT", [D, T], BF16)
    cT = dscr("cT", [D, T], BF16)

    with contextlib.ExitStack() as st:
        P = Prog(nc, st)

        uniq = [0]

        def sb(name, shape, dt, stack=st):
            uniq[0] += 1
            return stack.enter_context(nc.sbuf_tensor("%s_%d" % (name, uniq[0]), list(shape), dt))

        pbanks = []
        for i in range(8):
            pbanks.append((st.enter_context(nc.psum_tensor("pb%d" % i, [128, 512], F32)), P.tok("pb%d" % i)))
        prot = Rot(pbanks)
        bank = prot.next

        identf = sb("identf", [128, 128], F32)
        identb = sb("identb", [128, 128], BF16)
        onesb = sb("onesb", [128, 128], BF16)
        onesf = sb("onesf", [128, 128], F32)
        tri = sb("tri", [128, 4, 128], F32)
        negb = sb("negb", [128, 2, 128], BF16)
        trib = sb("trib", [128, 4, 128], BF16)
        abias = sb("abias", [128, 24, 128], F32)
        tconst = P.dtok("const")
        tconst2 = P.dtok("const2", sw=True)

        block = st.enter_context(nc.Block())

        P.dma("sync", lambda e: e.dma_start(out=identf[:], in_=c_ident[:]), writes=[tconst], sem_tok=tconst)
        P.dma("sync", lambda e: e.dma_start(out=tri[:], in_=c_tri[:]), writes=[tconst], sem_tok=tconst)
        P.dma("sync", lambda e: e.dma_start(out=abias[:], in_=c_abias[:]), writes=[tconst], sem_tok=tconst)
        P.dma("gpsimd", lambda e: e.dma_start(out=negb[:], in_=c_neg[:]), writes=[tconst2], sem_tok=tconst2)
        P.dma("gpsimd", lambda e: e.dma_start(out=identb[:], in_=c_ident[:]), writes=[tconst2], sem_tok=tconst2)
        P.dma("gpsimd", lambda e: e.dma_start(out=trib[:], in_=c_tri[:]), writes=[tconst2], sem_tok=tconst2)
        P.op("vector", lambda e: e.memset(onesb[:], 1.0), writes=[tconst])
        P.op("vector", lambda e: e.memset(onesf[:], 1.0), writes=[tconst])
        P.barrier()

        def rms_rstd(eng_ss, ss, rstd, tss, n):
            P.op("scalar", lambda e: e.activation(out=rstd, in_=ss, func=AF.Sqrt, bias=EPS, scale=1.0 / n), reads=[tss], writes=[tss])
            P.op("vector", lambda e: e.reciprocal(out=rstd, in_=rstd), reads=[tss], writes=[tss])

        def norm_transpose(ph, src, tsrc, gb, tgb, row0, ntok, actT, tact):
            for t in range(ntok // 128):
                xt, txt = ph["xrot"].next()
                r0 = row0 + t * 128
                P.dma("sync", lambda e, xt=xt, r0=r0: e.dma_start(out=xt[:], in_=src[r0:r0 + 128, :]), reads=[tsrc], writes=[txt], sem_tok=txt)
                xn, txn = ph["xnrot"].next()
                ss, tss = ph["ssrot"].next()
                P.op("scalar", lambda e, xt=xt, xn=xn, ss=ss: e.activation(out=xn[:], in_=xt[:], func=AF.Square, accum_out=ss[:, 0:1]), reads=[txt], writes=[txn, tss])
                rstd = ss[:, 1:2]
                rms_rstd("vector", ss[:, 0:1], rstd, tss, D)
                P.op("vector", lambda e, xt=xt, xn=xn, rstd=rstd: e.scalar_tensor_tensor(out=xn[:], in0=xt[:], scalar=rstd, in1=gb[:], op0=ALU.mult, op1=ALU.mult), reads=[txt, tss, tgb], writes=[txn])
                for half in range(2):
                    ps, tps = bank()
                    psb = ps[:].bitcast(BF16)
                    for k in range(8):
                        kc = half * 8 + k
                        P.op("tensor", lambda e, psb=psb, xn=xn, k=k, kc=kc: e.transpose(psb[:, k * 128:(k + 1) * 128], xn[:, kc * 128:(kc + 1) * 128], identb[:]), reads=[txn], writes=[tps])
                    dst = actT[:, half * 8:half * 8 + 8, t * 128:(t + 1) * 128]
                    srcp = psb[:, 0:1024].rearrange("p (k c) -> p k c", c=128)
                    if half == 0:
                        P.op("scalar", lambda e, dst=dst, srcp=srcp: e.activation(out=dst, in_=srcp, func=AF.Copy), reads=[tps], writes=[tact[t]])
                    else:
                        P.op("vector", lambda e, dst=dst, srcp=srcp: e.tensor_copy(out=dst, in_=srcp), reads=[tps], writes=[tact[t]])

        def load_w(ph, W2d, kcn, r0, c0, cw):
            buf, tk = ph["wrot"].next()
            src = W2d[r0:r0 + kcn * 128, c0:c0 + cw].rearrange("(kc p) n -> p kc n", p=128)
            dst = buf[:, 0:kcn * cw].rearrange("p (kc c) -> p kc c", c=cw)
            P.dma("gpsimd", lambda e: e.dma_start(out=dst, in_=src), writes=[tk], sem_tok=tk)
            return dst, tk

        def mm_fm(wv, wtk, j, kcn, actT, tact, tt, ntok=512):
            ps, tps = bank()
            rd = [wtk] + tact[(tt * 512) // 128:(tt * 512 + ntok) // 128]
            for kc in range(kcn):
                P.op("tensor", lambda e, ps=ps, kc=kc: e.matmul(ps[:, 0:ntok], lhsT=wv[:, kc, j * 128:(j + 1) * 128], rhs=actT[:, kc, tt * 512:tt * 512 + ntok], start=(kc == 0), stop=(kc == kcn - 1)), reads=rd, writes=[tps])
            return ps, tps

        def mm_tm(wv, wtk, cw, kcn, actT, tact, t, pst=None, first=True, last=True, kc0=0):
            ps, tps = pst if pst is not None else bank()
            rd = [wtk, tact[t]]
            for kc in range(kcn):
                P.op("tensor", lambda e, ps=ps, kc=kc: e.matmul(ps[:, 0:cw], lhsT=actT[:, kc0 + kc, t * 128:(t + 1) * 128], rhs=wv[:, kc, 0:cw], start=(first and kc == 0), stop=(last and kc == kcn - 1)), reads=rd, writes=[tps])
            return ps, tps

        def load_bcast(dst, src_row, tk, n):
            P.dma("sync", lambda e: e.dma_start(out=dst, in_=src_row.to_broadcast([128, n])), writes=[tk], sem_tok=tk)

        def make_phase(pst, TSUB, nw=3, nx=2, wsize=8192):
            ph = {}
            ph["actT"] = sb("actT", [128, KC, TSUB], BF16, pst)
            ph["tact"] = [P.tok("act%d" % i) for i in range(TSUB // 128)]
            ph["wrot"] = Rot([(sb("wb%d" % i, [128, wsize], BF16, pst), P.dtok("wb%d" % i, sw=True)) for i in range(nw)])
            ph["xrot"] = Rot([(sb("xt%d" % i, [128, D], F32, pst), P.dtok("xt%d" % i)) for i in range(nx)])
            ph["xnrot"] = Rot([(sb("xn%d" % i, [128, D], BF16, pst), P.tok("xn%d" % i)) for i in range(2)])
            ph["ssrot"] = Rot([(sb("ss%d" % i, [128, 2], F32, pst), P.tok("ss%d" % i)) for i in range(4)])
            ph["gb"] = (sb("gb", [128, D], F32, pst), P.dtok("gb"))
            ph["stb"] = Rot([(sb("stb%d" % i, [128, 512], BF16, pst), P.dtok("stb%d" % i)) for i in range(4)])
            ph["stf"] = Rot([(sb("stf%d" % i, [128, 512], F32, pst), P.dtok("stf%d" % i)) for i in range(4)])
            return ph

        dumps = {}

        def dump(name, ap, shape, tk):
            if not dbg:
                return
            dt_ = nc.dram_tensor("d_" + name, list(shape), F32, kind="ExternalOutput").ap()
            tkd = P.dtok("dump")
            P.dma("gpsimd", lambda e: e.dma_start(out=dt_, in_=ap), reads=[tk], sem_tok=tkd)

        def store(dst, src, tsrc):
            P.dma("sync", lambda e: e.dma_start(out=dst, in_=src), reads=[tsrc], sem_tok=tsrc)

        def phase_A(l, xsrc):
            with contextlib.ExitStack() as pst:
                ph = make_phase(pst, TS)
                actT, tact = ph["actT"], ph["tact"]
                gb, tgb = ph["gb"]
                load_bcast(gb[:], p_n1[l], tgb, D)
                qg = sb("qg", [128, 2], F32, pst)
                tqg = P.dtok("qg")
                P.dma("sync", lambda e: e.dma_start(out=qg[:, 0:1], in_=p_qg[l]), writes=[tqg], sem_tok=tqg)
                P.dma("sync", lambda e: e.dma_start(out=qg[:, 1:2], in_=p_kg[l]), writes=[tqg], sem_tok=tqg)
                P.op("vector", lambda e: e.tensor_scalar(out=qg[:, 1:2], in0=qg[:, 1:2], scalar1=float(np.sqrt(128.0)), scalar2=None, op0=ALU.mult), reads=[tqg], writes=[tqg])
                sq = [(sb("sq%d" % i, [128, 512], BF16, pst), P.tok("sq%d" % i)) for i in range(2)]
                sqrot = Rot(sq)
                W = w_in[l]
                tnone = P.tok("none")
                for s in range(NS):
                    g0 = s * TS
                    norm_transpose(ph, xsrc, tnone, gb, tgb, g0, TS, actT, tact)
                    NTT = TS // 512
                    for nb in range(4):
                        wv, wtk = load_w(ph, W, 16, 0, C_Z + nb * 512, 512)
                        for t in range(NTS):
                            ps, tps = mm_tm(wv, wtk, 512, 16, actT, tact, t)
                            stg, tst = ph["stb"].next()
                            P.op("scalar", lambda e, stg=stg, ps=ps: e.activation(out=stg[:], in_=ps[:], func=AF.Silu), reads=[tps], writes=[tst])
                            store(zs[g0 + t * 128:g0 + (t + 1) * 128, nb * 512:(nb + 1) * 512], stg[:], tst)
                    for nb in range(6):
                        wv, wtk = load_w(ph, W, 16, 0, C_XBC + nb * 512, 512)
                        for j in range(4):
                            for tt in range(NTT):
                                ps, tps = mm_fm(wv, wtk, j, 16, actT, tact, tt)
                                stg, tst = ph["stb"].next()
                                P.op("vector", lambda e, stg=stg, ps=ps: e.tensor_copy(out=stg[:], in_=ps[:]), reads=[tps], writes=[tst])
                                c0 = nb * 512 + j * 128
                                store(xbc_pre[c0:c0 + 128, g0 + tt * 512:g0 + (tt + 1) * 512], stg[:], tst)
                    wv, wtk = load_w(ph, W, 16, 0, C_DT, 64)
                    for t in range(NTS):
                        ps, tps = mm_tm(wv, wtk, 64, 16, actT, tact, t)
                        stg, tst = ph["stf"].next()
                        P.op("vector", lambda e, stg=stg, ps=ps: e.tensor_copy(out=stg[:, 0:64], in_=ps[:, 0:64]), reads=[tps], writes=[tst])
                        store(dts[g0 + t * 128:g0 + (t + 1) * 128, :], stg[:, 0:64], tst)
                    for qk in range(2):
                        dstT = qT if qk == 0 else kT
                        for nb in range(3):
                            wv, wtk = load_w(ph, W, 16, 0, (C_Q if qk == 0 else C_K) + nb * 512, 512)
                            for j in range(4):
                                h = nb * 4 + j
                                for tt in range(NTT):
                                    ps, tps = mm_fm(wv, wtk, j, 16, actT, tact, tt)
                                    sqb, tsq = sqrot.next()
                                    P.op("scalar", lambda e, sqb=sqb, ps=ps: e.activation(out=sqb[:], in_=ps[:], func=AF.Square), reads=[tps], writes=[tsq])
                                    ps2, tps2 = bank()
                                    P.op("tensor", lambda e, ps2=ps2, sqb=sqb: e.matmul(ps2[:], lhsT=onesb[:], rhs=sqb[:], start=True, stop=True), reads=[tsq], writes=[tps2])
                                    rr, trr = ph["stf"].next()
                                    P.op("scalar", lambda e, rr=rr, ps2=ps2: e.activation(out=rr[:], in_=ps2[:], func=AF.Sqrt, bias=128.0 * EPS), reads=[tps2], writes=[trr])
                                    P.op("vector", lambda e, rr=rr: e.reciprocal(out=rr[:], in_=rr[:]), reads=[trr], writes=[trr])
                                    stg, tst = ph["stb"].next()
                                    P.op("vector", lambda e, stg=stg, ps=ps, rr=rr, qk=qk: e.scalar_tensor_tensor(out=stg[:], in0=ps[:], scalar=qg[:, qk:qk + 1], in1=rr[:], op0=ALU.mult, op1=ALU.mult), reads=[tps, trr, tqg], writes=[tst])
                                    store(dstT[h, :, g0 + tt * 512:g0 + (tt + 1) * 512], stg[:], tst)
                    for nb in range(3):
                        wv, wtk = load_w(ph, W, 16, 0, C_V + nb * 512, 512)
                        for t in range(NTS):
                            ps, tps = mm_tm(wv, wtk, 512, 16, actT, tact, t)
                            stg, tst = ph["stb"].next()
                            P.op("scalar", lambda e, stg=stg, ps=ps: e.activation(out=stg[:], in_=ps[:], func=AF.Copy), reads=[tps], writes=[tst])
                            store(vs[g0 + t * 128:g0 + (t + 1) * 128, nb * 512:(nb + 1) * 512], stg[:], tst)
                    for nb in range(4):
                        wa, wta = load_w(ph, W, 16, 0, C_GLU + nb * 512, 512)
                        wg, wtg = load_w(ph, W, 16, 0, C_GLU + D + nb * 512, 512)
                        for j in range(4):
                            for tt in range(NTT):
                                psg, tpsg = mm_fm(wg, wtg, j, 16, actT, tact, tt)
                                sg, tsg = ph["stf"].next()
                                P.op("scalar", lambda e, sg=sg, psg=psg: e.activation(out=sg[:], in_=psg[:], func=AF.Sigmoid), reads=[tpsg], writes=[tsg])
                                psa, tpsa = mm_fm(wa, wta, j, 16, actT, tact, tt)
                                stg, tst = ph["stb"].next()
                                P.op("vector", lambda e, stg=stg, psa=psa, sg=sg: e.tensor_tensor(out=stg[:], in0=psa[:], in1=sg[:], op=ALU.mult), reads=[tpsa, tsg], writes=[tst])
                                c0 = nb * 512 + j * 128
                                store(uT[c0:c0 + 128, g0 + tt * 512:g0 + (tt + 1) * 512], stg[:], tst)
                    for nb in range(12):
                        wv, wtk = load_w(ph, W, 16, 0, C_GATE + nb * 512, 512)
                        for j in range(4):
                            for tt in range(NTT):
                                ps, tps = mm_fm(wv, wtk, j, 16, actT, tact, tt)
                                stg, tst = ph["stb"].next()
                                P.op("scalar", lambda e, stg=stg, ps=ps: e.activation(out=stg[:], in_=ps[:], func=AF.Sigmoid), reads=[tps], writes=[tst])
                                c0 = nb * 512 + j * 128
                                store(gT[c0:c0 + 128, g0 + tt * 512:g0 + (tt + 1) * 512], stg[:], tst)
                P.barrier()

        def phase_ssd(l):
            with contextlib.ExitStack() as pst:
                cw = sb("cw", [128, 24, 5], F32, pst)
                cbv = sb("cbv", [128, 24], F32, pst)
                tcw = P.dtok("cw")
                P.dma("sync", lambda e: e.dma_start(out=cw[:], in_=p_cw[l]), writes=[tcw], sem_tok=tcw)
                P.dma("sync", lambda e: e.dma_start(out=cbv[:], in_=p_cb[l]), writes=[tcw], sem_tok=tcw)
                cins = [(sb("cin%d" % i, [128, T + 4], BF16, pst), P.dtok("cin%d" % i)) for i in range(2)]
                for cin, tcin in cins:
                    P.op("vector", lambda e, cin=cin: e.memset(cin[:, 0:2], 0.0), writes=[tcin])
                    P.op("vector", lambda e, cin=cin: e.memset(cin[:, T + 2:T + 4], 0.0), writes=[tcin])
                cinrot = Rot(cins)
                dgr = Rot([(sb("sdg%d" % i, [128, 5, 128], BF16, pst), P.tok("sdg%d" % i)) for i in range(2)])
                sstr = Rot([(sb("sst%d" % i, [128, 512], BF16, pst), P.dtok("sst%d" % i)) for i in range(4)])
                for c in range(24):
                    cin, tcin = cinrot.next()
                    P.dma("sync", lambda e, cin=cin, c=c: e.dma_start(out=cin[:, 2:2 + T], in_=xbc_pre[c * 128:(c + 1) * 128, :]), writes=[tcin], sem_tok=tcin)
                    dg, tdg = dgr.next()
                    for k in range(5):
                        P.op("vector", lambda e, dg=dg, k=k, c=c: e.tensor_scalar(out=dg[:, k, :], in0=identf[:], scalar1=cw[:, c, k:k + 1], scalar2=None, op0=ALU.mult), reads=[tcw], writes=[tdg], self_waw_ok=True)
                    for tt in range(T // 512):
                        ps, tps = bank()
                        for k in range(5):
                            P.op("tensor", lambda e, ps=ps, dg=dg, cin=cin, k=k, tt=tt: e.matmul(ps[:], lhsT=dg[:, k, :], rhs=cin[:, tt * 512 + k:tt * 512 + k + 512], start=(k == 0), stop=(k == 4)), reads=[tdg, tcin], writes=[tps])
                        stg, tst = sstr.next()
                        P.op("scalar", lambda e, stg=stg, ps=ps, c=c: e.activation(out=stg[:], in_=ps[:], func=AF.Silu, bias=cbv[:, c:c + 1]), reads=[tps, tcw], writes=[tst])
                        store(xbcT[c * 128:(c + 1) * 128, tt * 512:(tt + 1) * 512], stg[:], tst)
                P.barrier()
            with contextlib.ExitStack() as pst:
                adt = sb("adt", [128, NT, 64], F32, pst)
                dec = sb("dec", [128, NT, 64], F32, pst)
                adh = sb("adh", [128, NT, 64], BF16, pst)
                adl = sb("adl", [128, NT, 64], BF16, pst)
                biasd = [sb("biasd%d" % i, [128, NT, 32], F32, pst) for i in range(2)]
                wgt = [sb("wgt%d" % i, [128, NT, 32], F32, pst) for i in range(2)]
                esc = [sb("esc%d" % i, [128, NT, 32], F32, pst) for i in range(2)]
                dskb = sb("dskb", [128, 32], F32, pst)
                DI = sb("DI", [128, 32, 128], BF16, pst)
                tprep = P.dtok("prep")
                with contextlib.ExitStack() as pst2:
                    dtr = sb("dtr", [128, NT, 64], F32, pst2)
                    dtv = sb("dtv", [128, NT, 64], F32, pst2)
                    lndt = sb("lndt", [128, NT, 64], F32, pst2)
                    cs = [sb("cs%d" % i, [128, NT, 64], F32, pst2) for i in range(4)]
                    dtb = sb("dtb", [128, 64], F32, pst2)
                    alg = sb("alg", [128, 64], F32, pst2)
                    P.dma("sync", lambda e: e.dma_start(out=dtr[:], in_=dts.rearrange("(c p) j -> p c j", p=128)), writes=[tprep], sem_tok=tprep)
                    load_bcast(dtb[:], p_dtb[l], tprep, 64)
                    load_bcast(alg[:], p_alog[l], tprep, 64)
                    load_bcast(dskb[:], p_dsk[l], tprep, 32)
                    tp = [tprep]
                    P.op("vector", lambda e: e.tensor_tensor(out=dtr[:], in0=dtr[:], in1=dtb[:].unsqueeze(1).to_broadcast([128, NT, 64]), op=ALU.add), reads=tp, writes=tp)
                    P.op("scalar", lambda e: e.activation(out=dtr[:], in_=dtr[:], func=AF.Exp), reads=tp, writes=tp)
                    P.op("scalar", lambda e: e.activation(out=dtv[:], in_=dtr[:], func=AF.Ln, bias=1.0), reads=tp, writes=tp)
                    sm = cs[0]
                    mk = cs[1]
                    P.op("vector", lambda e: e.tensor_scalar(out=sm[:], in0=dtr[:], scalar1=-0.25, scalar2=1.0 / 3.0, op0=ALU.mult, op1=ALU.add), reads=tp, writes=tp)
                    P.op("vector", lambda e: e.tensor_tensor(out=sm[:], in0=sm[:], in1=dtr[:], op=ALU.mult), reads=tp, writes=tp)
                    P.op("vector", lambda e: e.tensor_scalar(out=sm[:], in0=sm[:], scalar1=-0.5, scalar2=None, op0=ALU.add), reads=tp, writes=tp)
                    P.op("vector", lambda e: e.tensor_tensor(out=sm[:], in0=sm[:], in1=dtr[:], op=ALU.mult), reads=tp, writes=tp)
                    P.op("vector", lambda e: e.tensor_scalar(out=sm[:], in0=sm[:], scalar1=1.0, scalar2=None, op0=ALU.add), reads=tp, writes=tp)
                    P.op("vector", lambda e: e.tensor_tensor(out=sm[:], in0=sm[:], in1=dtr[:], op=ALU.mult), reads=tp, writes=tp)
                    P.op("vector", lambda e: e.tensor_scalar(out=mk[:], in0=dtr[:], scalar1=0.1, scalar2=None, op0=ALU.is_lt), reads=tp, writes=tp)
                    P.op("vector", lambda e: e.tensor_tensor(out=sm[:], in0=sm[:], in1=dtv[:], op=ALU.subtract), reads=tp, writes=tp)
                    P.op("vector", lambda e: e.tensor_tensor(out=sm[:], in0=sm[:], in1=mk[:], op=ALU.mult), reads=tp, writes=tp)
                    P.op("vector", lambda e: e.tensor_tensor(out=dtv[:], in0=dtv[:], in1=sm[:], op=ALU.add), reads=tp, writes=tp)
                    dump("dt", dtv[:], [128, NT, 64], tprep)
                    P.op("scalar", lambda e: e.activation(out=lndt[:], in_=dtv[:], func=AF.Ln), reads=tp, writes=tp)
                    P.op("scalar", lambda e: e.activation(out=alg[:], in_=alg[:], func=AF.Exp), reads=tp, writes=tp)
                    P.op("vector", lambda e: e.scalar_tensor_tensor(out=adt[:], in0=dtv[:], scalar=-1.0, in1=alg[:].unsqueeze(1).to_broadcast([128, NT, 64]), op0=ALU.mult, op1=ALU.mult), reads=tp, writes=tp)
                    P.op("vector", lambda e: e.tensor_copy(out=adh[:], in_=adt[:]), reads=tp, writes=tp)
                    P.op("vector", lambda e: e.tensor_tensor(out=dtr[:], in0=adt[:], in1=adh[:], op=ALU.subtract), reads=tp, writes=tp)
                    P.op("vector", lambda e: e.tensor_copy(out=adl[:], in_=dtr[:]), reads=tp, writes=tp)
                    adf = adt[:].rearrange("p c j -> p (c j)")
                    npc = (NT * 64 + 511) // 512
                    for m in range(5):
                        dstt = (cs[m] if m < 4 else dec)[:].rearrange("p c j -> p (c j)")
                        for pc in range(npc):
                            n0 = pc * 512
                            n1 = min(NT * 64, n0 + 512)
                            ps, tps = bank()
                            lh = tri[:, m, :] if m < 4 else onesf[:]
                            P.op("tensor", lambda e, ps=ps, lh=lh, n0=n0, n1=n1: e.matmul(ps[:, 0:n1 - n0], lhsT=lh, rhs=adf[:, n0:n1], start=True, stop=True), reads=tp + [tconst], writes=[tps])
                            if m < 4:
                                P.op("scalar", lambda e, ps=ps, dstt=dstt, n0=n0, n1=n1: e.activation(out=dstt[:, n0:n1], in_=ps[:, 0:n1 - n0], func=AF.Copy), reads=[tps], writes=tp)
                            else:
                                P.op("scalar", lambda e, ps=ps, dstt=dstt, n0=n0, n1=n1: e.activation(out=dstt[:, n0:n1], in_=ps[:, 0:n1 - n0], func=AF.Exp), reads=[tps], writes=tp)
                    P.op("vector", lambda e: e.tensor_tensor(out=biasd[0][:], in0=lndt[:, :, 0:32], in1=cs[0][:, :, 0:32], op=ALU.subtract), reads=tp, writes=tp)
                    P.op("vector", lambda e: e.tensor_tensor(out=wgt[0][:], in0=cs[1][:, :, 0:32], in1=lndt[:, :, 0:32], op=ALU.add), reads=tp, writes=tp)
                    P.op("scalar", lambda e: e.activation(out=wgt[0][:], in_=wgt[0][:], func=AF.Exp), reads=tp, writes=tp)
                    P.op("scalar", lambda e: e.activation(out=esc[0][:], in_=cs[0][:, :, 0:32], func=AF.Exp), reads=tp, writes=tp)
                    P.op("vector", lambda e: e.tensor_tensor(out=biasd[1][:], in0=cs[2][:, :, 32:64], in1=lndt[:, :, 32:64], op=ALU.add), reads=tp, writes=tp)
                    P.op("scalar", lambda e: e.activation(out=wgt[1][:], in_=biasd[1][:], func=AF.Exp), reads=tp, writes=tp)
                    P.op("scalar", lambda e: e.activation(out=esc[1][:], in_=cs[3][:, :, 32:64], func=AF.Exp), reads=tp, writes=tp)
                    dump("lndt", lndt[:], [128, NT, 64], tprep)
                    dump("adt", adt[:], [128, NT, 64], tprep)
                    dump("dec", dec[:], [128, NT, 64], tprep)
                    dump("bias1", biasd[1][:], [128, NT, 32], tprep)
                    dump("wgt1", wgt[1][:], [128, NT, 32], tprep)
                    dump("esc1", esc[1][:], [128, NT, 32], tprep)
                    dump("cs2", cs[2][:], [128, NT, 64], tprep)
                    for h in range(32):
                        P.op("vector", lambda e, h=h: e.tensor_scalar(out=DI[:, h, :], in0=identf[:], scalar1=dskb[:, h:h + 1], scalar2=None, op0=ALU.mult), reads=tp, writes=tp)
                    P.barrier()
                xcl = Rot([(sb("xcl%d" % i, [128, 24, 128], BF16, pst), P.dtok("xcl%d" % i)) for i in range(2)])
                xtmr = Rot([(sb("xtm%d" % i, [128, 2560], BF16, pst), P.tok("xtm%d" % i)) for i in range(2)])
                xwr = Rot([(sb("xw%d" % i, [128, 2048], BF16, pst), P.tok("xw%d" % i)) for i in range(2)])
                ltr = Rot([(sb("lt%d" % i, [128, 8, 128], BF16, pst), P.tok("lt%d" % i)) for i in range(2)])
                mpr = Rot([(sb("mp%d" % i, [128, 8, 128], BF16, pst), P.tok("mp%d" % i)) for i in range(2)])
                cbr = Rot([(sb("cbs%d" % i, [128, 128], F32, pst), P.tok("cbs%d" % i)) for i in range(2)])
                hst = sb("hst", [128, 2048], F32, pst)
                hbf = sb("hbf", [128, 2048], BF16, pst)
                thst = [P.tok("hst%d" % g) for g in range(4)]
                thbf = [P.tok("hbf%d" % g) for g in range(4)]
                ytr = Rot([(sb("ytmp%d" % i, [128, 512], F32, pst), P.tok("ytmp%d" % i)) for i in range(2)])
                yaccr = Rot([(sb("yacc%d" % i, [128, 2048], F32, pst), P.dtok("yacc%d" % i)) for i in range(2)])
                yblr = Rot([(sb("ybl%d" % i, [128, 2048], F32, pst), P.dtok("ybl%d" % i)) for i in range(1)])
                ztr = Rot([(sb("zt%d" % i, [128, 2048], BF16, pst), P.dtok("zt%d" % i)) for i in range(1)])
                xbv = xbcT.rearrange("(k p) t -> p k t", p=128)
                if dbg:
                    dlt = nc.dram_tensor("d_lt", [128, 1024], BF16, kind="ExternalOutput").ap()
                    dmp = nc.dram_tensor("d_mp", [128, 1024], BF16, kind="ExternalOutput").ap()
                    dcb = nc.dram_tensor("d_cb", [128, 128], F32, kind="ExternalOutput").ap()
                    tdbg = P.dtok("dbgst")
                for dr_ in (1, 0):
                    P.op("vector", lambda e: e.memset(hst[:], 0.0), writes=thst)
                    P.op("vector", lambda e: e.memset(hbf[:], 0.0), writes=thbf)
                    order = range(NT) if dr_ == 0 else range(NT - 1, -1, -1)
                    mtri = 0 if dr_ == 0 else 2
                    for c in order:
                        xa, txa = xcl.next()
                        P.dma("sync", lambda e, xa=xa, c=c: e.dma_start(out=xa[:], in_=xbv[:, :, c * 128:(c + 1) * 128]), writes=[txa], sem_tok=txa)
                        xtm, txtm = xtmr.next()
                        for part in range(3):
                            ps, tps = bank()
                            psb = ps[:].bitcast(BF16)
                            nk = 8 if part < 2 else 4
                            for k in range(nk):
                                P.op("tensor", lambda e, psb=psb, xa=xa, k=k, part=part: e.transpose(psb[:, k * 128:(k + 1) * 128], xa[:, part * 8 + k, :], identb[:]), reads=[txa], writes=[tps])
                            if part == 1:
                                P.op("vector", lambda e, xtm=xtm, psb=psb, part=part, nk=nk: e.tensor_copy(out=xtm[:, part * 1024:part * 1024 + nk * 128], in_=psb[:, 0:nk * 128]), reads=[tps], writes=[txtm])
                            else:
                                P.op("scalar", lambda e, xtm=xtm, psb=psb, part=part, nk=nk: e.activation(out=xtm[:, part * 1024:part * 1024 + nk * 128], in_=psb[:, 0:nk * 128], func=AF.Copy), reads=[tps], writes=[txtm])
                        xw, txw = xwr.next()
                        P.op("gpsimd", lambda e, xw=xw, xtm=xtm, c=c, dr_=dr_: e.tensor_tensor(out=xw[:].rearrange("p (h d) -> p h d", d=64), in0=xtm[:, 0:2048].rearrange("p (h d) -> p h d", d=64), in1=wgt[dr_][:, c, :].unsqueeze(2).to_broadcast([128, 32, 64]), op=ALU.mult), reads=[txtm], writes=[txw])
                        yacc, tyacc = yaccr.next()
                        for g in range(4):
                            psc, tpsc = bank()
                            P.op("tensor", lambda e, psc=psc, xa=xa, g=g: e.matmul(psc[:, 0:128], lhsT=xa[:, 16 + g, :], rhs=xa[:, 20 + g, :], start=True, stop=True), reads=[txa], writes=[tpsc])
                            cbs, tcbs = cbr.next()
                            P.op("scalar", lambda e, cbs=cbs, psc=psc: e.activation(out=cbs[:], in_=psc[:, 0:128], func=AF.Copy), reads=[tpsc], writes=[tcbs])
                            lt, tlt = ltr.next()
                            pds = [bank(), bank()]
                            for hh in range(8):
                                h = g * 8 + hh
                                pd, tpd = pds[hh // 4]
                                tgt = pd[:, (hh % 4) * 128:(hh % 4 + 1) * 128]
                                col = dr_ * 32 + h
                                P.op("tensor", lambda e, tgt=tgt, c=c, col=col, mtri=mtri: e.matmul(tgt, lhsT=adh[:, c, col:col + 1].to_broadcast([128, 128]), rhs=trib[:, mtri, :], start=True, stop=False), writes=[tpd])
                                P.op("tensor", lambda e, tgt=tgt, c=c, col=col, mtri=mtri: e.matmul(tgt, lhsT=adl[:, c, col:col + 1].to_broadcast([128, 128]), rhs=trib[:, mtri, :], start=False, stop=False), writes=[tpd])
                                P.op("tensor", lambda e, tgt=tgt, dr_=dr_: e.matmul(tgt, lhsT=identb[:], rhs=negb[:, dr_, :], start=False, stop=True), writes=[tpd])
                            for hh in range(8):
                                h = g * 8 + hh
                                pd, tpd = pds[hh // 4]
                                tgt = pd[:, (hh % 4) * 128:(hh % 4 + 1) * 128]
                                P.op("scalar", lambda e, lt=lt, hh=hh, tgt=tgt, c=c, h=h, dr_=dr_: e.activation(out=lt[:, hh, :], in_=tgt, func=AF.Exp, bias=biasd[dr_][:, c, h:h + 1], scale=(1.0 if dr_ == 0 else -1.0)), reads=[tpd], writes=[tlt], self_waw_ok=True)
                            mp, tmp_ = mpr.next()
                            P.op("vector", lambda e, mp=mp, lt=lt, cbs=cbs: e.tensor_tensor(out=mp[:], in0=lt[:], in1=cbs[:].unsqueeze(1).to_broadcast([128, 8, 128]), op=ALU.mult), reads=[tlt, tcbs], writes=[tmp_])
                            if dbg and dr_ == 1 and c == NT - 1 and g == 0:
                                P.dma("sync", lambda e, lt=lt: e.dma_start(out=dlt, in_=lt[:].rearrange("p a b -> p (a b)")), reads=[tlt], sem_tok=tdbg)
                                P.dma("sync", lambda e, mp=mp: e.dma_start(out=dmp, in_=mp[:].rearrange("p a b -> p (a b)")), reads=[tmp_], sem_tok=tdbg)
                                P.dma("sync", lambda e, cbs=cbs: e.dma_start(out=dcb, in_=cbs[:]), reads=[tcbs], sem_tok=tdbg)
                            psy, tpsy = bank()
                            for hh in range(8):
                                h = g * 8 + hh
                                P.op("tensor", lambda e, psy=psy, mp=mp, hh=hh, h=h, xtm=xtm, dr_=dr_: e.matmul(psy[:, hh * 64:(hh + 1) * 64], lhsT=mp[:, hh, :], rhs=xtm[:, h * 64:(h + 1) * 64], start=True, stop=(dr_ == 1), skip_group_check=True), reads=[tmp_, txtm], writes=[tpsy])
                                if dr_ == 0:
                                    P.op("tensor", lambda e, psy=psy, hh=hh, h=h, xtm=xtm: e.matmul(psy[:, hh * 64:(hh + 1) * 64], lhsT=DI[:, h, :], rhs=xtm[:, h * 64:(h + 1) * 64], start=False, stop=True, skip_group_check=True), reads=[txtm], writes=[tpsy])
                            pso, tpso = bank()
                            P.op("tensor", lambda e, pso=pso, xa=xa, g=g: e.matmul(pso[:], lhsT=xa[:, 20 + g, :], rhs=hbf[:, g * 512:(g + 1) * 512], start=True, stop=True), reads=[txa, thbf[g]], writes=[tpso])
                            yt, tyt = ytr.next()
                            P.op("vector", lambda e, yt=yt, pso=pso, c=c, g=g, dr_=dr_: e.tensor_tensor(out=yt[:].rearrange("p (h d) -> p h d", d=64), in0=pso[:].rearrange("p (h d) -> p h d", d=64), in1=esc[dr_][:, c, g * 8:(g + 1) * 8].unsqueeze(2).to_broadcast([128, 8, 64]), op=ALU.mult), reads=[tpso], writes=[tyt])
                            P.op("vector", lambda e, yacc=yacc, psy=psy, yt=yt, g=g: e.tensor_tensor(out=yacc[:, g * 512:(g + 1) * 512], in0=psy[:], in1=yt[:], op=ALU.add), reads=[tpsy, tyt], writes=[tyacc], self_waw_ok=True)
                            pss, tpss = bank()
                            P.op("tensor", lambda e, pss=pss, xtm=xtm, xw=xw, g=g: e.matmul(pss[:], lhsT=xtm[:, 2048 + g * 128:2048 + (g + 1) * 128], rhs=xw[:, g * 512:(g + 1) * 512], start=True, stop=True), reads=[txtm, txw], writes=[tpss])
                            hs = hst[:, g * 512:(g + 1) * 512]
                            P.op("vector", lambda e, hs=hs, c=c, g=g, dr_=dr_: e.tensor_tensor(out=hs.rearrange("p (h d) -> p h d", d=64), in0=hs.rearrange("p (h d) -> p h d", d=64), in1=dec[:, c, dr_ * 32 + g * 8:dr_ * 32 + (g + 1) * 8].unsqueeze(2).to_broadcast([128, 8, 64]), op=ALU.mult), reads=[thst[g]], writes=[thst[g]])
                            P.op("vector", lambda e, hs=hs, pss=pss: e.tensor_tensor(out=hs, in0=hs, in1=pss[:], op=ALU.add), reads=[thst[g], tpss], writes=[thst[g]])
                            P.op("gpsimd", lambda e, hs=hs, g=g: e.tensor_copy(out=hbf[:, g * 512:(g + 1) * 512], in_=hs), reads=[thst[g]], writes=[thbf[g]])
                        rows = slice(c * 128, (c + 1) * 128)
                        if dr_ == 1:
                            store(yb[rows, :], yacc[:], tyacc)
                        else:
                            ybl, tybl = yblr.next()
                            zt, tzt = ztr.next()
                            P.dma("sync", lambda e, ybl=ybl, rows=rows: e.dma_start(out=ybl[:], in_=yb[rows, :]), writes=[tybl], sem_tok=tybl)
                            P.dma("sync", lambda e, zt=zt, rows=rows: e.dma_start(out=zt[:], in_=zs[rows, :]), writes=[tzt], sem_tok=tzt)
                            P.op("gpsimd", lambda e, yacc=yacc, ybl=ybl: e.tensor_tensor(out=yacc[:], in0=yacc[:], in1=ybl[:], op=ALU.add), reads=[tyacc, tybl], writes=[tyacc])
                            P.op("gpsimd", lambda e, yacc=yacc, zt=zt: e.tensor_tensor(out=yacc[:], in0=yacc[:], in1=zt[:], op=ALU.mult), reads=[tyacc, tzt], writes=[tyacc])
                            store(yg[rows, :], yacc[:], tyacc)
                    P.barrier()


        def phase_attn(l):
            with contextlib.ExitStack() as pst:
                qr = Rot([(sb("qsb%d" % i, [128, T], BF16, pst), P.dtok("qsb%d" % i)) for i in range(2)])
                kr = Rot([(sb("ksb%d" % i, [128, T], BF16, pst), P.dtok("ksb%d" % i)) for i in range(2)])
                vr = Rot([(sb("vt%d" % i, [128, NT, 128], BF16, pst), P.dtok("vt%d" % i)) for i in range(2)])
                num = sb("num", [128, T], F32, pst)
                den = sb("den", [128, T], F32, pst)
                tnum, tden = P.tok("num"), P.tok("den")
                osb = sb("osb", [128, T], BF16, pst)
                tosb = P.dtok("osb")
                ptr = Rot([(sb("pT%d" % i, [128, 128], BF16, pst), P.tok("pT%d" % i)) for i in range(3)])
                for j in range(4):
                    for g in range(3):
                        h = 4 * g + j
                        d = ATTN_PATTERNS[g][1]
                        S = T // d
                        NK = S // 128
                        qs, tqs = qr.next()
                        ks, tks = kr.next()
                        vt, tvt = vr.next()
                        P.dma("sync", lambda e, qs=qs, h=h: e.dma_start(out=qs[:], in_=qT[h]), writes=[tqs], sem_tok=tqs)
                        P.dma("sync", lambda e, ks=ks, h=h: e.dma_start(out=ks[:], in_=kT[h]), writes=[tks], sem_tok=tks)
                        vv = vt[:].rearrange("a (r kt) e -> a r kt e", r=d)
                        for r in range(d):
                            P.dma("sync", lambda e, vv=vv, h=h, d=d, r=r: e.dma_start(out=vv[:, r], in_=vs[:, h * 128:(h + 1) * 128].rearrange("(kt a r) e -> a r kt e", a=128, r=d)[:, r]), writes=[tvt], sem_tok=tvt)
                        for r in range(d):
                            for m in range(NK + 1):
                                b0 = 64 if m == 0 else 0
                                b1 = 64 if m == NK else 128
                                nq = b1 - b0
                                i0 = 128 * m - 64 + b0
                                q0 = i0 * d + r
                                qsl = qs[:, q0:q0 + (nq - 1) * d + 1:d]
                                pso, tpso = bank()
                                psd, tpsd = bank()
                                kts = [kt for kt in (m - 1, m) if 0 <= kt < NK]
                                for idx, kt in enumerate(kts):
                                    typ = 0 if kt == m - 1 else 1
                                    pss, tpss = bank()
                                    k0 = 128 * kt * d + r
                                    ksl = ks[:, k0:k0 + 127 * d + 1:d]
                                    P.op("tensor", lambda e, pss=pss, ksl=ksl, qsl=qsl, nq=nq: e.matmul(pss[:, 0:nq], lhsT=ksl, rhs=qsl, start=True, stop=False), reads=[tqs, tks], writes=[tpss])
                                    P.op("tensor", lambda e, pss=pss, nq=nq, h=h, typ=typ, b0=b0, b1=b1: e.matmul(pss[:, 0:nq], lhsT=identf[:], rhs=abias[:, h * 2 + typ, b0:b1], start=False, stop=True), writes=[tpss])
                                    pt, tpt = ptr.next()
                                    P.op("scalar", lambda e, pt=pt, pss=pss, nq=nq: e.activation(out=pt[:, 0:nq], in_=pss[:, 0:nq], func=AF.Exp), reads=[tpss], writes=[tpt])
                                    fl = dict(start=(idx == 0), stop=(idx == len(kts) - 1))
                                    P.op("tensor", lambda e, pso=pso, vv=vv, r=r, kt=kt, pt=pt, nq=nq, fl=fl: e.matmul(pso[:, 0:nq], lhsT=vv[:, r, kt, :], rhs=pt[:, 0:nq], **fl), reads=[tvt, tpt], writes=[tpso])
                                    P.op("tensor", lambda e, psd=psd, pt=pt, nq=nq, fl=fl: e.matmul(psd[:, 0:nq], lhsT=onesb[:], rhs=pt[:, 0:nq], **fl), reads=[tpt], writes=[tpsd])
                                nsl = num[:, q0:q0 + (nq - 1) * d + 1:d]
                                dsl = den[:, q0:q0 + (nq - 1) * d + 1:d]
                                if g == 0:
                                    P.op("vector", lambda e, nsl=nsl, pso=pso, nq=nq: e.tensor_copy(out=nsl, in_=pso[:, 0:nq]), reads=[tpso], writes=[tnum], self_waw_ok=True)
                                    P.op("scalar", lambda e, dsl=dsl, psd=psd, nq=nq: e.activation(out=dsl, in_=psd[:, 0:nq], func=AF.Copy), reads=[tpsd], writes=[tden], self_waw_ok=True)
                                else:
                                    P.op("vector", lambda e, nsl=nsl, pso=pso, nq=nq: e.tensor_tensor(out=nsl, in0=nsl, in1=pso[:, 0:nq], op=ALU.add), reads=[tpso], writes=[tnum], self_waw_ok=True)
                                    P.op("vector", lambda e, dsl=dsl, psd=psd, nq=nq: e.tensor_tensor(out=dsl, in0=dsl, in1=psd[:, 0:nq], op=ALU.add), reads=[tpsd], writes=[tden], self_waw_ok=True)
                    P.op("vector", lambda e: e.reciprocal(out=den[:], in_=den[:]), reads=[tden], writes=[tden])
                    P.op("vector", lambda e: e.tensor_tensor(out=osb[:], in0=num[:], in1=den[:], op=ALU.mult), reads=[tnum, tden], writes=[tosb])
                    store(oT[j * 128:(j + 1) * 128, :], osb[:], tosb)
                P.barrier()

        def phase_conf(l):
            with contextlib.ExitStack() as pst:
                dwp = sb("dwp", [128, 16, 31], F32, pst)
                dwb = sb("dwb", [128, 16], F32, pst)
                tdw = P.dtok("dw")
                P.dma("sync", lambda e: e.dma_start(out=dwp[:], in_=p_dw[l]), writes=[tdw], sem_tok=tdw)
                P.dma("sync", lambda e: e.dma_start(out=dwb[:], in_=p_dwb[l]), writes=[tdw], sem_tok=tdw)
                cins = [(sb("ccin%d" % i, [128, T + 30], BF16, pst), P.dtok("ccin%d" % i)) for i in range(2)]
                for cin, tcin in cins:
                    P.op("vector", lambda e, cin=cin: e.memset(cin[:, 0:15], 0.0), writes=[tcin])
                    P.op("vector", lambda e, cin=cin: e.memset(cin[:, T + 15:T + 30], 0.0), writes=[tcin])
                cinrot = Rot(cins)
                dgr = Rot([(sb("dg%d" % i, [128, 31, 128], BF16, pst), P.tok("dg%d" % i)) for i in range(2)])
                str_ = Rot([(sb("cst%d" % i, [128, 512], BF16, pst), P.dtok("cst%d" % i)) for i in range(3)])
                for cc in range(16):
                    cin, tcin = cinrot.next()
                    P.dma("sync", lambda e, cin=cin, cc=cc: e.dma_start(out=cin[:, 15:15 + T], in_=uT[cc * 128:(cc + 1) * 128, :]), writes=[tcin], sem_tok=tcin)
                    dg, tdg = dgr.next()
                    for k in range(31):
                        eng = "vector" if k % 2 == 0 else "gpsimd"
                        P.op(eng, lambda e, dg=dg, k=k, cc=cc: e.tensor_scalar(out=dg[:, k, :], in0=identf[:], scalar1=dwp[:, cc, k:k + 1], scalar2=None, op0=ALU.mult), reads=[tdw], writes=[tdg], self_waw_ok=True)
                    P.op("vector", lambda e, dg=dg: e.tensor_copy(out=dg[:, 30, 0:1], in_=dg[:, 30, 0:1]), reads=[tdg], writes=[tdg])
                    for tt in range(T // 512):
                        ps, tps = bank()
                        for k in range(31):
                            P.op("tensor", lambda e, ps=ps, dg=dg, cin=cin, k=k, tt=tt: e.matmul(ps[:], lhsT=dg[:, k, :], rhs=cin[:, tt * 512 + k:tt * 512 + k + 512], start=(k == 0), stop=(k == 30)), reads=[tdg, tcin], writes=[tps])
                        stg, tst = str_.next()
                        P.op("scalar", lambda e, stg=stg, ps=ps, cc=cc: e.activation(out=stg[:], in_=ps[:], func=AF.Identity, bias=dwb[:, cc:cc + 1]), reads=[tps, tdw], writes=[tst])
                        store(ycT[cc * 128:(cc + 1) * 128, tt * 512:(tt + 1) * 512], stg[:], tst)
                P.barrier()
            with contextlib.ExitStack() as pst:
                lng = sb("lng", [128, 16], F32, pst)
                lnb = sb("lnb", [128, 16], F32, pst)
                tln = P.dtok("ln")
                P.dma("sync", lambda e: e.dma_start(out=lng[:], in_=p_lng[l]), writes=[tln], sem_tok=tln)
                P.dma("sync", lambda e: e.dma_start(out=lnb[:], in_=p_lnb[l]), writes=[tln], sem_tok=tln)
                ylr = Rot([(sb("yl%d" % i, [128, 16, 512], BF16, pst), P.dtok("yl%d" % i)) for i in range(2)])
                sqr = Rot([(sb("sqy%d" % i, [128, 16, 512], BF16, pst), P.tok("sqy%d" % i)) for i in range(1)])
                mur = Rot([(sb("mu%d" % i, [128, 3, 512], F32, pst), P.tok("mu%d" % i)) for i in range(2)])
                t1r = Rot([(sb("t1%d" % i, [128, 512], F32, pst), P.tok("t1%d" % i)) for i in range(3)])
                str_ = Rot([(sb("cst2%d" % i, [128, 512], BF16, pst), P.dtok("cst2%d" % i)) for i in range(3)])
                ycv = ycT.rearrange("(k p) t -> p k t", p=128)
                for tt in range(T // 512):
                    yl, tyl = ylr.next()
                    P.dma("sync", lambda e, yl=yl, tt=tt: e.dma_start(out=yl[:], in_=ycv[:, :, tt * 512:(tt + 1) * 512]), writes=[tyl], sem_tok=tyl)
                    sqy, tsq = sqr.next()
                    P.op("gpsimd", lambda e, sqy=sqy, yl=yl: e.tensor_tensor(out=sqy[:], in0=yl[:], in1=yl[:], op=ALU.mult), reads=[tyl], writes=[tsq])
                    ps1, tps1 = bank()
                    ps2, tps2 = bank()
                    for cc in range(16):
                        P.op("tensor", lambda e, ps1=ps1, yl=yl, cc=cc: e.matmul(ps1[:], lhsT=onesb[:], rhs=yl[:, cc, :], start=(cc == 0), stop=(cc == 15)), reads=[tyl], writes=[tps1])
                    for cc in range(16):
                        P.op("tensor", lambda e, ps2=ps2, sqy=sqy, cc=cc: e.matmul(ps2[:], lhsT=onesb[:], rhs=sqy[:, cc, :], start=(cc == 0), stop=(cc == 15)), reads=[tsq], writes=[tps2])
                    mu, tmu = mur.next()
                    P.op("scalar", lambda e, mu=mu, ps1=ps1: e.activation(out=mu[:, 0, :], in_=ps1[:], func=AF.Copy, scale=1.0 / D), reads=[tps1], writes=[tmu])
                    P.op("vector", lambda e, mu=mu: e.tensor_tensor(out=mu[:, 1, :], in0=mu[:, 0, :], in1=mu[:, 0, :], op=ALU.mult), reads=[tmu], writes=[tmu])
                    P.op("vector", lambda e, mu=mu, ps2=ps2: e.scalar_tensor_tensor(out=mu[:, 2, :], in0=ps2[:], scalar=1.0 / D, in1=mu[:, 1, :], op0=ALU.mult, op1=ALU.subtract), reads=[tmu, tps2], writes=[tmu])
                    P.op("scalar", lambda e, mu=mu: e.activation(out=mu[:, 2, :], in_=mu[:, 2, :], func=AF.Sqrt, bias=EPS), reads=[tmu], writes=[tmu])
                    P.op("vector", lambda e, mu=mu: e.reciprocal(out=mu[:, 2, :], in_=mu[:, 2, :]), reads=[tmu], writes=[tmu])
                    for cc in range(16):
                        t1, tt1 = t1r.next()
                        eng = "vector" if cc % 2 == 0 else "gpsimd"
                        P.op(eng, lambda e, t1=t1, yl=yl, mu=mu, cc=cc: e.tensor_tensor(out=t1[:], in0=yl[:, cc, :], in1=mu[:, 0, :], op=ALU.subtract), reads=[tyl, tmu], writes=[tt1])
                        P.op(eng, lambda e, t1=t1, mu=mu: e.tensor_tensor(out=t1[:], in0=t1[:], in1=mu[:, 2, :], op=ALU.mult), reads=[tt1, tmu], writes=[tt1])
                        stg, tst = str_.next()
                        P.op("scalar", lambda e, stg=stg, t1=t1, cc=cc: e.activation(out=stg[:], in_=t1[:], func=AF.Silu, bias=lnb[:, cc:cc + 1], scale=lng[:, cc:cc + 1]), reads=[tt1, tln], writes=[tst])
                        store(cT[cc * 128:(cc + 1) * 128, tt * 512:(tt + 1) * 512], stg[:], tst)
                P.barrier()

        def phase_B(l, xsrc):
            with contextlib.ExitStack() as pst:
                TB = min(1024, T)
                NTB = TB // 128
                CW = 256
                NJ = CW // 128
                ph = make_phase(pst, TB, nw=6, wsize=16 * CW)
                actT, tact = ph["actT"], ph["tact"]
                gb, tgb = ph["gb"]
                load_bcast(gb[:], p_ng[l], tgb, D)
                mergedT = sb("mergedT", [128, KC, TB], BF16, pst)
                NTT = TB // 512
                tm = [P.tok("mer%d" % i) for i in range(NTT)]
                tmer_list = [tm[t // 4] for t in range(NTB)]
                glr = Rot([(sb("gl%d" % i, [128, 512], BF16, pst), P.dtok("gl%d" % i)) for i in range(3)])
                tactld = P.dtok("actld")
                tnone = P.tok("none")

                def branch(W, kcn, br, first, g0):
                    for nb in range(D // CW):
                        wv, wtk = load_w(ph, W, kcn, 0, nb * CW, CW)
                        for j in range(NJ):
                            cb = nb * NJ + j
                            for tt in range(NTT):
                                gl, tgl = glr.next()
                                P.dma("sync", lambda e, gl=gl, cb=cb, tt=tt: e.dma_start(out=gl[:], in_=gT[br * D + cb * 128:br * D + (cb + 1) * 128, g0 + tt * 512:g0 + (tt + 1) * 512]), writes=[tgl], sem_tok=tgl)
                                ps, tps = mm_fm(wv, wtk, j, kcn, actT, tact, tt)
                                msl = mergedT[:, cb, tt * 512:(tt + 1) * 512]
                                if first:
                                    P.op("vector", lambda e, msl=msl, ps=ps, gl=gl: e.tensor_tensor(out=msl, in0=ps[:], in1=gl[:], op=ALU.mult), reads=[tps, tgl], writes=[tm[tt]], self_waw_ok=True)
                                else:
                                    tmpb, ttmp = ph["stf"].next()
                                    P.op("vector", lambda e, tmpb=tmpb, ps=ps, gl=gl: e.tensor_tensor(out=tmpb[:], in0=ps[:], in1=gl[:], op=ALU.mult), reads=[tps, tgl], writes=[ttmp])
                                    P.op("vector", lambda e, msl=msl, tmpb=tmpb: e.tensor_tensor(out=msl, in0=msl, in1=tmpb[:], op=ALU.add), reads=[ttmp], writes=[tm[tt]], self_waw_ok=True)

                for s in range(T // TB):
                    g0 = s * TB
                    norm_transpose(ph, yg, tnone, gb, tgb, g0, TB, actT, tact)
                    branch(w_ssd_o[l], 16, 0, True, g0)
                    for kc in range(4):
                        P.dma("sync", lambda e, kc=kc, g0=g0: e.dma_start(out=actT[:, kc, :], in_=oT[kc * 128:(kc + 1) * 128, g0:g0 + TB]), writes=tact, sem_tok=tactld)
                    branch(w_attn_o[l], 4, 1, False, g0)
                    for kc in range(16):
                        P.dma("sync", lambda e, kc=kc, g0=g0: e.dma_start(out=actT[:, kc, :], in_=cT[kc * 128:(kc + 1) * 128, g0:g0 + TB]), writes=tact, sem_tok=tactld)
                    branch(w_conv_o[l], 16, 2, False, g0)
                    for nb in range(D // CW):
                        wv, wtk = load_w(ph, w_out[l], 16, 0, nb * CW, CW)
                        for t in range(NTB):
                            xo, txo = ph["stf"].next()
                            rows = slice(g0 + t * 128, g0 + (t + 1) * 128)
                            P.dma("sync", lambda e, xo=xo, rows=rows, nb=nb: e.dma_start(out=xo[:, 0:CW], in_=xsrc[rows, nb * CW:(nb + 1) * CW]), writes=[txo], sem_tok=txo)
                            ps, tps = mm_tm(wv, wtk, CW, 16, mergedT, tmer_list, t)
                            P.op("vector", lambda e, xo=xo, ps=ps: e.tensor_tensor(out=xo[:, 0:CW], in0=xo[:, 0:CW], in1=ps[:, 0:CW], op=ALU.add), reads=[tps, txo], writes=[txo])
                            store(xres[rows, nb * CW:(nb + 1) * CW], xo[:, 0:CW], txo)
                P.barrier()

        def phase_mlp(l, dst):
            with contextlib.ExitStack() as pst:
                TM = min(1024, T)
                NTM = TM // 128
                NTT = TM // 512
                HH = 4096
                CW = 256
                NJ = CW // 128
                ph = make_phase(pst, TM, nw=4, nx=2, wsize=16 * CW)
                actT, tact = ph["actT"], ph["tact"]
                gb, tgb = ph["gb"]
                load_bcast(gb[:], p_n2[l], tgb, D)
                aT = sb("aT", [128, 32, TM], BF16, pst)
                ta = [P.tok("aT%d" % i) for i in range(2)]
                rr = Rot([(sb("relu%d" % i, [128, 512], BF16, pst), P.tok("relu%d" % i)) for i in range(3)])
                tnone = P.tok("none")
                for ti in range(T // TM):
                    r0 = ti * TM
                    norm_transpose(ph, xres, tnone, gb, tgb, r0, TM, actT, tact)
                    tdr = [[P.tok("xd") for _ in range(D // CW)] for _ in range(NTM)]
                    for hh in range(2):
                        for nb in range(HH // CW):
                            wv, wtk = load_w(ph, w_up[l], 16, 0, hh * HH + nb * CW, CW)
                            for j in range(NJ):
                                hc = nb * NJ + j
                                for tt in range(NTT):
                                    ps, tps = mm_fm(wv, wtk, j, 16, actT, tact, tt)
                                    rl, trl = rr.next()
                                    P.op("scalar", lambda e, rl=rl, ps=ps: e.activation(out=rl[:], in_=ps[:], func=AF.Relu), reads=[tps], writes=[trl])
                                    P.op("vector", lambda e, rl=rl, hc=hc, tt=tt: e.tensor_tensor(out=aT[:, hc, tt * 512:(tt + 1) * 512], in0=rl[:], in1=rl[:], op=ALU.mult), reads=[trl], writes=[ta[hc // 16]], self_waw_ok=True)
                        for nb in range(D // CW):
                            b8 = [bank() for _ in range(NTM)]
                            for kq in range(2):
                                wv, wtk = load_w(ph, w_down[l], 16, hh * HH + kq * 2048, nb * CW, CW)
                                for t in range(NTM):
                                    mm_tm(wv, wtk, CW, 16, aT, [ta[kq]] * NTM, t, pst=b8[t], first=(kq == 0), last=(kq == 1), kc0=kq * 16)
                            src = xres if hh == 0 else dst
                            for t in range(NTM):
                                ps, tps = b8[t]
                                xo, txo = ph["stf"].next()
                                rows = slice(r0 + t * 128, r0 + (t + 1) * 128)
                                P.dma("sync", lambda e, xo=xo, rows=rows, nb=nb, src=src: e.dma_start(out=xo[:, 0:CW], in_=src[rows, nb * CW:(nb + 1) * CW]), reads=[tdr[t][nb]], writes=[txo], sem_tok=txo)
                                P.op("vector", lambda e, xo=xo, ps=ps: e.tensor_tensor(out=xo[:, 0:CW], in0=xo[:, 0:CW], in1=ps[:, 0:CW], op=ALU.add), reads=[tps, txo], writes=[txo])
                                P.dma("sync", lambda e, xo=xo, rows=rows, nb=nb: e.dma_start(out=dst[rows, nb * CW:(nb + 1) * CW], in_=xo[:, 0:CW]), reads=[txo], writes=[tdr[t][nb]], sem_tok=txo)
                P.barrier()

        for l in range(depth):
            xsrc = x_in if l == 0 else xres
            for fn, args in [(phase_A, (l, xsrc)), (phase_ssd, (l,)), (phase_attn, (l,)), (phase_conf, (l,)),
                             (phase_B, (l, xsrc)), (phase_mlp, (l, out if l == depth - 1 else xres))]:
                if phases is not None and fn.__name__ not in phases:
                    continue
                P.begin_phase()
                fn(*args)
                P.end_phase()
        P.emit(block)
    return nc


_WNAMES = ["w_in", "w_ssd_o", "w_attn_o", "w_conv_o", "w_out", "w_mlp_up", "w_mlp_down"]


def run_cores(x_list, inputs, T, depth=DEPTH, dbg=False, phases=None):
    nc = build(T, depth=depth, dbg=dbg, phases=phases)
    consts = host_consts()
    lay = host_layout(inputs)
    common = {}
    for k in _WNAMES:
        common[k] = np.ascontiguousarray(inputs[k][:depth])
    for k, v in lay.items():
        common[k] = np.ascontiguousarray(v[:depth])
    common.update(consts)
    zeros = {k: np.zeros_like(v) for k, v in common.items()}
    in_maps = []
    for xb in x_list:
        if xb is None:
            m = dict(zeros)
            m["x"] = np.zeros((T, D), np.float32)
        else:
            m = dict(common)
            m["x"] = np.ascontiguousarray(xb)
        in_maps.append(m)
    res = run_bass_kernel_spmd(nc, in_maps, core_ids=list(range(len(x_list))))
    return res.results


def kernel(**inputs):
    x = np.asarray(inputs["x"], dtype=np.float32)
    B, T, _ = x.shape
    inp = {k: np.asarray(v, dtype=np.float32) for k, v in inputs.items()}
    cores = [0, 1, 4, 5]
    x_list = [None] * 8
    for b in range(B):
        x_list[cores[b]] = x[b]
    results = run_cores(x_list, inp, T)
    return np.stack([results[cores[b]]["out"] for b in range(B)], axis=0).astype(np.float32)
```

```python
import contextlib
import numpy as np
import concourse.bass as bass
import concourse.mybir as mybir
from concourse.bass_utils import run_bass_kernel_spmd

F32 = mybir.dt.float32
BF16 = mybir.dt.bfloat16
AF = mybir.ActivationFunctionType
ALU = mybir.AluOpType

ENGS = ["tensor", "vector", "scalar", "gpsimd", "sync"]
EPOCH = 30000

D = 2048
KC = 16
N_IN = 20032
C_Z, C_XBC, C_DT, C_Q, C_K, C_V, C_GLU, C_GATE = 0, 2048, 5120, 5184, 6720, 8256, 9792, 13888
EPS = 1e-6
DEPTH = 2


class Tok:
    __slots__ = ("name", "w", "r", "dsem")

    def __init__(self, name=""):
        self.name = name
        self.w = None
        self.r = {}
        self.dsem = None


class Prog:
    def __init__(self, nc, stack):
        self.nc = nc
        self.stack = stack
        self.q = {e: [] for e in ENGS}
        self.cnt = {e: 0 for e in ENGS}
        self.sems = {}
        self.waited = {e: {} for e in ENGS}
        self.nsem = 0
        self.dcount = {}
        self.free_dsems = {False: [], True: []}
        self.phase_sems = None

    def begin_phase(self):
        self.phase_sems = []

    def end_phase(self):
        for sw, k in self.phase_sems:
            self.free_dsems[sw].append(k)
        self.phase_sems = None

    def _sem(self, key):
        if key not in self.sems:
            self.sems[key] = self.stack.enter_context(self.nc.semaphore("s%d" % self.nsem))
            self.nsem += 1
        return self.sems[key]

    def tok(self, name=""):
        return Tok(name)

    def dtok(self, name="", sw=False):
        t = Tok(name)
        if self.free_dsems[sw]:
            t.dsem = self.free_dsems[sw].pop()
        else:
            t.dsem = ("d", self.nsem, name)
            self._sem(t.dsem)
            self.dcount[t.dsem] = 0
        if self.phase_sems is not None:
            self.phase_sems.append((sw, t.dsem))
        return t

    def _deps(self, eng, reads, writes, self_waw_ok):
        deps = {}

        def add(d):
            if d is None:
                return
            k, v = d
            if deps.get(k, 0) < v:
                deps[k] = v
        for t in reads:
            add(t.w)
        for t in writes:
            if t.w is not None:
                if not (self_waw_ok and t.w[0][0] == "e" and t.w[0][1] == eng):
                    add(t.w)
            for k, v in t.r.items():
                if k[0] == "e" and k[1] == eng:
                    continue
                add((k, v))
        out = []
        wd = self.waited[eng]
        for k, v in deps.items():
            if eng == "tensor" and k[0] == "e" and k[1] == "tensor":
                continue
            if wd.get(k, 0) >= v:
                continue
            wd[k] = v
            out.append((k, v))
        return out

    def _mark(self, comp, reads, writes):
        k, v = comp
        for t in reads:
            if t.r.get(k, 0) < v:
                t.r[k] = v
        for t in writes:
            t.w = comp
            t.r = {}

    def op(self, eng, fn, reads=(), writes=(), self_waw_ok=False):
        waits = self._deps(eng, reads, writes, self_waw_ok)
        self.cnt[eng] += 1
        ep, v = divmod(self.cnt[eng] - 1, EPOCH)
        key = ("e", eng, ep)
        comp = (key, v + 1)
        self._sem(key)
        self.q[eng].append((waits, fn, key, 1))
        self._mark(comp, reads, writes)
        return comp

    def dma(self, eng, fn, reads=(), writes=(), sem_tok=None):
        waits = self._deps(eng, reads, writes, False)
        key = sem_tok.dsem
        self.dcount[key] += 16
        comp = (key, self.dcount[key])
        self.q[eng].append((waits, fn, key, 16))
        self._mark(comp, reads, writes)
        return comp

    def barrier(self):
        allw = []
        for e in ENGS:
            if self.cnt[e] > 0:
                ep, v = divmod(self.cnt[e] - 1, EPOCH)
                allw.append((("e", e, ep), v + 1))
        for k, v in self.dcount.items():
            if v > 0:
                allw.append((k, v))
        for e in ENGS:
            wd = self.waited[e]
            waits = []
            for k, v in allw:
                if k[0] == "e" and k[1] == e:
                    continue
                if wd.get(k, 0) >= v:
                    continue
                wd[k] = v
                waits.append((k, v))
            if waits:
                self.q[e].append((waits, None, None, 0))

    def emit(self, block):
        fin = [(k, v) for k, v in self.dcount.items() if v > 0]
        sems = self.sems
        q = self.q

        def runner(e):
            def run(engine):
                for waits, fn, key, inc in q[e]:
                    for k, v in waits:
                        engine.wait_ge(sems[k], v)
                    if fn is None:
                        continue
                    ins = fn(engine)
                    ins.then_inc(sems[key], inc)
                if e == "sync":
                    for k, v in fin:
                        engine.wait_ge(sems[k], v)
            return run
        block.tensor(runner("tensor"))
        block.vector(runner("vector"))
        block.scalar(runner("scalar"))
        block.gpsimd(runner("gpsimd"))
        block.sync(runner("sync"))


class Rot:
    def __init__(self, items):
        self.items = items
        self.i = 0

    def next(self):
        it = self.items[self.i % len(self.items)]
        self.i += 1
        return it


ATTN_PATTERNS = ((128, 1), (512, 4), (2048, 16))


def host_consts():
    c = {}
    i = np.arange(128)
    lp, l = i[:, None], i[None, :]
    c["c_ident"] = np.eye(128, dtype=np.float32)
    tri = np.stack([(lp <= l), (lp > l), (lp < l), (lp >= l)], axis=1).astype(np.float32)
    c["c_tri"] = np.ascontiguousarray(tri)
    s, ll = i[:, None], i[None, :]
    neg = np.stack([np.where(ll < s, -30000.0, 0.0), np.where(ll > s, 30000.0, 0.0)], axis=1).astype(np.float32)
    c["c_neg"] = np.ascontiguousarray(neg)
    slopes = np.exp2(-8.0 * np.arange(1, 13, dtype=np.float64) / 12.0)
    a, b = i[:, None], i[None, :]
    ab = np.zeros((128, 12, 2, 128), np.float32)
    for h in range(12):
        d = ATTN_PATTERNS[h // 4][1]
        relA = a - b - 64
        relB = a - b + 64
        ab[:, h, 0, :] = np.where(a >= b, -slopes[h] * d * np.abs(relA), -30000.0)
        ab[:, h, 1, :] = np.where(a <= b, -slopes[h] * d * np.abs(relB), -30000.0)
    c["c_abias"] = ab.reshape(128, 24, 128)
    return c


def host_layout(inp):
    L = DEPTH
    o = {}
    o["p_cw"] = np.ascontiguousarray(inp["ssd_conv_w"].transpose(0, 2, 1).reshape(L, 24, 128, 5).transpose(0, 2, 1, 3))
    o["p_cb"] = np.ascontiguousarray(inp["ssd_conv_b"].reshape(L, 24, 128).transpose(0, 2, 1))
    o["p_dw"] = np.ascontiguousarray(inp["conv_dw_w"].transpose(0, 2, 1).reshape(L, 16, 128, 31).transpose(0, 2, 1, 3))
    o["p_dwb"] = np.ascontiguousarray(inp["conv_dw_b"].reshape(L, 16, 128).transpose(0, 2, 1))
    o["p_lng"] = np.ascontiguousarray(inp["conv_ln_g"].reshape(L, 16, 128).transpose(0, 2, 1))
    o["p_lnb"] = np.ascontiguousarray(inp["conv_ln_b"].reshape(L, 16, 128).transpose(0, 2, 1))
    o["p_dtb"] = np.ascontiguousarray(inp["ssd_dt_bias"].reshape(L, 1, 64))
    o["p_alog"] = np.ascontiguousarray(inp["ssd_a_log"].reshape(L, 1, 64))
    o["p_dsk"] = np.ascontiguousarray(inp["ssd_d"].reshape(L, 1, 32))
    o["p_qg"] = np.ascontiguousarray(inp["q_norm_g"].reshape(L, 128, 1))
    o["p_kg"] = np.ascontiguousarray(inp["k_norm_g"].reshape(L, 128, 1))
    for k in ["norm1_g", "norm2_g", "ssd_norm_g"]:
        o["p_" + k] = np.ascontiguousarray(inp[k].reshape(L, 1, D))
    return o


def build(T, depth=DEPTH, dbg=False, phases=None):
    nc = bass.Bass("TRN2", target_bir_lowering=False)
    TS = min(2048, T)
    NS = T // TS
    NT = T // 128
    NTS = TS // 128

    def din(name, shape, dt=F32):
        return nc.dram_tensor(name, list(shape), dt, kind="ExternalInput").ap()

    def dscr(name, shape, dt):
        return nc.dram_tensor(name, list(shape), dt, kind=("ExternalOutput" if dbg else "Internal")).ap()

    x_in = din("x", [T, D])
    w_in = din("w_in", [depth, D, N_IN])
    w_ssd_o = din("w_ssd_o", [depth, D, D])
    w_attn_o = din("w_attn_o", [depth, 512, D])
    w_conv_o = din("w_conv_o", [depth, D, D])
    w_out = din("w_out", [depth, D, D])
    w_up = din("w_mlp_up", [depth, D, 4 * D])
    w_down = din("w_mlp_down", [depth, 4 * D, D])
    p_cw = din("p_cw", [depth, 128, 24, 5])
    p_cb = din("p_cb", [depth, 128, 24])
    p_dw = din("p_dw", [depth, 128, 16, 31])
    p_dwb = din("p_dwb", [depth, 128, 16])
    p_lng = din("p_lng", [depth, 128, 16])
    p_lnb = din("p_lnb", [depth, 128, 16])
    p_dtb = din("p_dtb", [depth, 1, 64])
    p_alog = din("p_alog", [depth, 1, 64])
    p_dsk = din("p_dsk", [depth, 1, 32])
    p_qg = din("p_qg", [depth, 128, 1])
    p_kg = din("p_kg", [depth, 128, 1])
    p_n1 = din("p_norm1_g", [depth, 1, D])
    p_n2 = din("p_norm2_g", [depth, 1, D])
    p_ng = din("p_ssd_norm_g", [depth, 1, D])
    c_ident = din("c_ident", [128, 128])
    c_tri = din("c_tri", [128, 4, 128])
    c_neg = din("c_neg", [128, 2, 128])
    c_abias = din("c_abias", [128, 24, 128])
    out = nc.dram_tensor("out", [T, D], F32, kind="ExternalOutput").ap()

    xres = dscr("xres", [T, D], F32)
    zs = dscr("zs", [T, D], BF16)
    xbc_pre = dscr("xbc_pre", [3072, T], BF16)
    xbcT = dscr("xbcT", [3072, T], BF16)
    dts = dscr("dts", [T, 64], F32)
    qT = dscr("qT", [12, 128, T], BF16)
    kT = dscr("kT", [12, 128, T], BF16)
    vs = dscr("vs", [T, 1536], BF16)
    uT = dscr("uT", [D, T], BF16)
    gT = dscr("gT", [3 * D, T], BF16)
    yb = dscr("yb", [T, D], F32)
    yg = dscr("yg", [T, D], F32)
    oT = dscr("oT", [512, T], BF16)
    ycT = dscr("ycT", [D, T], BF16)
    cT = dscr("cT", [D, T], BF16)

    with contextlib.ExitStack() as st:
        P = Prog(nc, st)

        uniq = [0]

        def sb(name, shape, dt, stack=st):
            uniq[0] += 1
            return stack.enter_context(nc.sbuf_tensor("%s_%d" % (name, uniq[0]), list(shape), dt))

        pbanks = []
        for i in range(8):
            pbanks.append((st.enter_context(nc.psum_tensor("pb%d" % i, [128, 512], F32)), P.tok("pb%d" % i)))
        prot = Rot(pbanks)
        bank = prot.next

        identf = sb("identf", [128, 128], F32)
        identb = sb("identb", [128, 128], BF16)
        onesb = sb("onesb", [128, 128], BF16)
        onesf = sb("onesf", [128, 128], F32)
        tri = sb("tri", [128, 4, 128], F32)
        negb = sb("negb", [128, 2, 128], BF16)
        trib = sb("trib", [128, 4, 128], BF16)
        abias = sb("abias", [128, 24, 128], F32)
        tconst = P.dtok("const")
        tconst2 = P.dtok("const2", sw=True)

        block = st.enter_context(nc.Block())

        P.dma("sync", lambda e: e.dma_start(out=identf[:], in_=c_ident[:]), writes=[tconst], sem_tok=tconst)
        P.dma("sync", lambda e: e.dma_start(out=tri[:], in_=c_tri[:]), writes=[tconst], sem_tok=tconst)
        P.dma("sync", lambda e: e.dma_start(out=abias[:], in_=c_abias[:]), writes=[tconst], sem_tok=tconst)
        P.dma("gpsimd", lambda e: e.dma_start(out=negb[:], in_=c_neg[:]), writes=[tconst2], sem_tok=tconst2)
        P.dma("gpsimd", lambda e: e.dma_start(out=identb[:], in_=c_ident[:]), writes=[tconst2], sem_tok=tconst2)
        P.dma("gpsimd", lambda e: e.dma_start(out=trib[:], in_=c_tri[:]), writes=[tconst2], sem_tok=tconst2)
        P.op("vector", lambda e: e.memset(onesb[:], 1.0), writes=[tconst])
        P.op("vector", lambda e: e.memset(onesf[:], 1.0), writes=[tconst])
        P.barrier()

        def rms_rstd(eng_ss, ss, rstd, tss, n):
            P.op("scalar", lambda e: e.activation(out=rstd, in_=ss, func=AF.Sqrt, bias=EPS, scale=1.0 / n), reads=[tss], writes=[tss])
            P.op("vector", lambda e: e.reciprocal(out=rstd, in_=rstd), reads=[tss], writes=[tss])

        def norm_transpose(ph, src, tsrc, gb, tgb, row0, ntok, actT, tact):
            for t in range(ntok // 128):
                xt, txt = ph["xrot"].next()
                r0 = row0 + t * 128
                P.dma("sync", lambda e, xt=xt, r0=r0: e.dma_start(out=xt[:], in_=src[r0:r0 + 128, :]), reads=[tsrc], writes=[txt], sem_tok=txt)
                xn, txn = ph["xnrot"].next()
                ss, tss = ph["ssrot"].next()
                P.op("scalar", lambda e, xt=xt, xn=xn, ss=ss: e.activation(out=xn[:], in_=xt[:], func=AF.Square, accum_out=ss[:, 0:1]), reads=[txt], writes=[txn, tss])
                rstd = ss[:, 1:2]
                rms_rstd("vector", ss[:, 0:1], rstd, tss, D)
                P.op("vector", lambda e, xt=xt, xn=xn, rstd=rstd: e.scalar_tensor_tensor(out=xn[:], in0=xt[:], scalar=rstd, in1=gb[:], op0=ALU.mult, op1=ALU.mult), reads=[txt, tss, tgb], writes=[txn])
                for half in range(2):
                    ps, tps = bank()
                    psb = ps[:].bitcast(BF16)
                    for k in range(8):
                        kc = half * 8 + k
                        P.op("tensor", lambda e, psb=psb, xn=xn, k=k, kc=kc: e.transpose(psb[:, k * 128:(k + 1) * 128], xn[:, kc * 128:(kc + 1) * 128], identb[:]), reads=[txn], writes=[tps])
                    dst = actT[:, half * 8:half * 8 + 8, t * 128:(t + 1) * 128]
                    srcp = psb[:, 0:1024].rearrange("p (k c) -> p k c", c=128)
                    if half == 0:
                        P.op("scalar", lambda e, dst=dst, srcp=srcp: e.activation(out=dst, in_=srcp, func=AF.Copy), reads=[tps], writes=[tact[t]])
                    else:
                        P.op("vector", lambda e, dst=dst, srcp=srcp: e.tensor_copy(out=dst, in_=srcp), reads=[tps], writes=[tact[t]])

        def load_w(ph, W2d, kcn, r0, c0, cw):
            buf, tk = ph["wrot"].next()
            src = W2d[r0:r0 + kcn * 128, c0:c0 + cw].rearrange("(kc p) n -> p kc n", p=128)
            dst = buf[:, 0:kcn * cw].rearrange("p (kc c) -> p kc c", c=cw)
            P.dma("gpsimd", lambda e: e.dma_start(out=dst, in_=src), writes=[tk], sem_tok=tk)
            return dst, tk

        def mm_fm(wv, wtk, j, kcn, actT, tact, tt, ntok=512):
            ps, tps = bank()
            rd = [wtk] + tact[(tt * 512) // 128:(tt * 512 + ntok) // 128]
            for kc in range(kcn):
                P.op("tensor", lambda e, ps=ps, kc=kc: e.matmul(ps[:, 0:ntok], lhsT=wv[:, kc, j * 128:(j + 1) * 128], rhs=actT[:, kc, tt * 512:tt * 512 + ntok], start=(kc == 0), stop=(kc == kcn - 1)), reads=rd, writes=[tps])
            return ps, tps

        def mm_tm(wv, wtk, cw, kcn, actT, tact, t, pst=None, first=True, last=True, kc0=0):
            ps, tps = pst if pst is not None else bank()
            rd = [wtk, tact[t]]
            for kc in range(kcn):
                P.op("tensor", lambda e, ps=ps, kc=kc: e.matmul(ps[:, 0:cw], lhsT=actT[:, kc0 + kc, t * 128:(t + 1) * 128], rhs=wv[:, kc, 0:cw], start=(first and kc == 0), stop=(last and kc == kcn - 1)), reads=rd, writes=[tps])
            return ps, tps

        def load_bcast(dst, src_row, tk, n):
            P.dma("sync", lambda e: e.dma_start(out=dst, in_=src_row.to_broadcast([128, n])), writes=[tk], sem_tok=tk)

        def make_phase(pst, TSUB, nw=3, nx=2, wsize=8192):
            ph = {}
            ph["actT"] = sb("actT", [128, KC, TSUB], BF16, pst)
            ph["tact"] = [P.tok("act%d" % i) for i in range(TSUB // 128)]
            ph["wrot"] = Rot([(sb("wb%d" % i, [128, wsize], BF16, pst), P.dtok("wb%d" % i, sw=True)) for i in range(nw)])
            ph["xrot"] = Rot([(sb("xt%d" % i, [128, D], F32, pst), P.dtok("xt%d" % i)) for i in range(nx)])
            ph["xnrot"] = Rot([(sb("xn%d" % i, [128, D], BF16, pst), P.tok("xn%d" % i)) for i in range(2)])
            ph["ssrot"] = Rot([(sb("ss%d" % i, [128, 2], F32, pst), P.tok("ss%d" % i)) for i in range(4)])
            ph["gb"] = (sb("gb", [128, D], F32, pst), P.dtok("gb"))
            ph["stb"] = Rot([(sb("stb%d" % i, [128, 512], BF16, pst), P.dtok("stb%d" % i)) for i in range(4)])
            ph["stf"] = Rot([(sb("stf%d" % i, [128, 512], F32, pst), P.dtok("stf%d" % i)) for i in range(4)])
            return ph

        dumps = {}

        def dump(name, ap, shape, tk):
            if not dbg:
                return
            dt_ = nc.dram_tensor("d_" + name, list(shape), F32, kind="ExternalOutput").ap()
            tkd = P.dtok("dump")
            P.dma("gpsimd", lambda e: e.dma_start(out=dt_, in_=ap), reads=[tk], sem_tok=tkd)

        def store(dst, src, tsrc):
            P.dma("sync", lambda e: e.dma_start(out=dst, in_=src), reads=[tsrc], sem_tok=tsrc)

        def phase_A(l, xsrc):
            with contextlib.ExitStack() as pst:
                ph = make_phase(pst, TS)
                actT, tact = ph["actT"], ph["tact"]
                gb, tgb = ph["gb"]
                load_bcast(gb[:], p_n1[l], tgb, D)
                qg = sb("qg", [128, 2], F32, pst)
                tqg = P.dtok("qg")
                P.dma("sync", lambda e: e.dma_start(out=qg[:, 0:1], in_=p_qg[l]), writes=[tqg], sem_tok=tqg)
                P.dma("sync", lambda e: e.dma_start(out=qg[:, 1:2], in_=p_kg[l]), writes=[tqg], sem_tok=tqg)
                P.op("vector", lambda e: e.tensor_scalar(out=qg[:, 1:2], in0=qg[:, 1:2], scalar1=float(np.sqrt(128.0)), scalar2=None, op0=ALU.mult), reads=[tqg], writes=[tqg])
                sq = [(sb("sq%d" % i, [128, 512], BF16, pst), P.tok("sq%d" % i)) for i in range(2)]
                sqrot = Rot(sq)
                W = w_in[l]
                tnone = P.tok("none")
                for s in range(NS):
                    g0 = s * TS
                    norm_transpose(ph, xsrc, tnone, gb, tgb, g0, TS, actT, tact)
                    NTT = TS // 512
                    for nb in range(4):
                        wv, wtk = load_w(ph, W, 16, 0, C_Z + nb * 512, 512)
                        for t in range(NTS):
                            ps, tps = mm_tm(wv, wtk, 512, 16, actT, tact, t)
                            stg, tst = ph["stb"].next()
                            P.op("scalar", lambda e, stg=stg, ps=ps: e.activation(out=stg[:], in_=ps[:], func=AF.Silu), reads=[tps], writes=[tst])
                            store(zs[g0 + t * 128:g0 + (t + 1) * 128, nb * 512:(nb + 1) * 512], stg[:], tst)
                    for nb in range(6):
                        wv, wtk = load_w(ph, W, 16, 0, C_XBC + nb * 512, 512)
                        for j in range(4):
                            for tt in range(NTT):
                                ps, tps = mm_fm(wv, wtk, j, 16, actT, tact, tt)
                                stg, tst = ph["stb"].next()
                                P.op("vector", lambda e, stg=stg, ps=ps: e.tensor_copy(out=stg[:], in_=ps[:]), reads=[tps], writes=[tst])
                                c0 = nb * 512 + j * 128
                                store(xbc_pre[c0:c0 + 128, g0 + tt * 512:g0 + (tt + 1) * 512], stg[:], tst)
                    wv, wtk = load_w(ph, W, 16, 0, C_DT, 64)
                    for t in range(NTS):
                        ps, tps = mm_tm(wv, wtk, 64, 16, actT, tact, t)
                        stg, tst = ph["stf"].next()
                        P.op("vector", lambda e, stg=stg, ps=ps: e.tensor_copy(out=stg[:, 0:64], in_=ps[:, 0:64]), reads=[tps], writes=[tst])
                        store(dts[g0 + t * 128:g0 + (t + 1) * 128, :], stg[:, 0:64], tst)
                    for qk in range(2):
                        dstT = qT if qk == 0 else kT
                        for nb in range(3):
                            wv, wtk = load_w(ph, W, 16, 0, (C_Q if qk == 0 else C_K) + nb * 512, 512)
                            for j in range(4):
                                h = nb * 4 + j
                                for tt in range(NTT):
                                    ps, tps = mm_fm(wv, wtk, j, 16, actT, tact, tt)
                                    sqb, tsq = sqrot.next()
                                    P.op("scalar", lambda e, sqb=sqb, ps=ps: e.activation(out=sqb[:], in_=ps[:], func=AF.Square), reads=[tps], writes=[tsq])
                                    ps2, tps2 = bank()
                                    P.op("tensor", lambda e, ps2=ps2, sqb=sqb: e.matmul(ps2[:], lhsT=onesb[:], rhs=sqb[:], start=True, stop=True), reads=[tsq], writes=[tps2])
                                    rr, trr = ph["stf"].next()
                                    P.op("scalar", lambda e, rr=rr, ps2=ps2: e.activation(out=rr[:], in_=ps2[:], func=AF.Sqrt, bias=128.0 * EPS), reads=[tps2], writes=[trr])
                                    P.op("vector", lambda e, rr=rr: e.reciprocal(out=rr[:], in_=rr[:]), reads=[trr], writes=[trr])
                                    stg, tst = ph["stb"].next()
                                    P.op("vector", lambda e, stg=stg, ps=ps, rr=rr, qk=qk: e.scalar_tensor_tensor(out=stg[:], in0=ps[:], scalar=qg[:, qk:qk + 1], in1=rr[:], op0=ALU.mult, op1=ALU.mult), reads=[tps, trr, tqg], writes=[tst])
                                    store(dstT[h, :, g0 + tt * 512:g0 + (tt + 1) * 512], stg[:], tst)
                    for nb in range(3):
                        wv, wtk = load_w(ph, W, 16, 0, C_V + nb * 512, 512)
                        for t in range(NTS):
                            ps, tps = mm_tm(wv, wtk, 512, 16, actT, tact, t)
                            stg, tst = ph["stb"].next()
                            P.op("scalar", lambda e, stg=stg, ps=ps: e.activation(out=stg[:], in_=ps[:], func=AF.Copy), reads=[tps], writes=[tst])
                            store(vs[g0 + t * 128:g0 + (t + 1) * 128, nb * 512:(nb + 1) * 512], stg[:], tst)
                    for nb in range(4):
                        wa, wta = load_w(ph, W, 16, 0, C_GLU + nb * 512, 512)
                        wg, wtg = load_w(ph, W, 16, 0, C_GLU + D + nb * 512, 512)
                        for j in range(4):
                            for tt in range(NTT):
                                psg, tpsg = mm_fm(wg, wtg, j, 16, actT, tact, tt)
                                sg, tsg = ph["stf"].next()
                                P.op("scalar", lambda e, sg=sg, psg=psg: e.activation(out=sg[:], in_=psg[:], func=AF.Sigmoid), reads=[tpsg], writes=[tsg])
                                psa, tpsa = mm_fm(wa, wta, j, 16, actT, tact, tt)
                                stg, tst = ph["stb"].next()
                                P.op("vector", lambda e, stg=stg, psa=psa, sg=sg: e.tensor_tensor(out=stg[:], in0=psa[:], in1=sg[:], op=ALU.mult), reads=[tpsa, tsg], writes=[tst])
                                c0 = nb * 512 + j * 128
                                store(uT[c0:c0 + 128, g0 + tt * 512:g0 + (tt + 1) * 512], stg[:], tst)
                    for nb in range(12):
                        wv, wtk = load_w(ph, W, 16, 0, C_GATE + nb * 512, 512)
                        for j in range(4):
                            for tt in range(NTT):
                                ps, tps = mm_fm(wv, wtk, j, 16, actT, tact, tt)
                                stg, tst = ph["stb"].next()
                                P.op("scalar", lambda e, stg=stg, ps=ps: e.activation(out=stg[:], in_=ps[:], func=AF.Sigmoid), reads=[tps], writes=[tst])
                                c0 = nb * 512 + j * 128
                                store(gT[c0:c0 + 128, g0 + tt * 512:g0 + (tt + 1) * 512], stg[:], tst)
                P.barrier()

        def phase_ssd(l):
            with contextlib.ExitStack() as pst:
                cw = sb("cw", [128, 24, 5], F32, pst)
                cbv = sb("cbv", [128, 24], F32, pst)
                tcw = P.dtok("cw")
                P.dma("sync", lambda e: e.dma_start(out=cw[:], in_=p_cw[l]), writes=[tcw], sem_tok=tcw)
                P.dma("sync", lambda e: e.dma_start(out=cbv[:], in_=p_cb[l]), writes=[tcw], sem_tok=tcw)
                cins = [(sb("cin%d" % i, [128, T + 4], BF16, pst), P.dtok("cin%d" % i)) for i in range(2)]
                for cin, tcin in cins:
                    P.op("vector", lambda e, cin=cin: e.memset(cin[:, 0:2], 0.0), writes=[tcin])
                    P.op("vector", lambda e, cin=cin: e.memset(cin[:, T + 2:T + 4], 0.0), writes=[tcin])
                cinrot = Rot(cins)
                dgr = Rot([(sb("sdg%d" % i, [128, 5, 128], BF16, pst), P.tok("sdg%d" % i)) for i in range(2)])
                sstr = Rot([(sb("sst%d" % i, [128, 512], BF16, pst), P.dtok("sst%d" % i)) for i in range(4)])
                for c in range(24):
                    cin, tcin = cinrot.next()
                    P.dma("sync", lambda e, cin=cin, c=c: e.dma_start(out=cin[:, 2:2 + T], in_=xbc_pre[c * 128:(c + 1) * 128, :]), writes=[tcin], sem_tok=tcin)
                    dg, tdg = dgr.next()
                    for k in range(5):
                        P.op("vector", lambda e, dg=dg, k=k, c=c: e.tensor_scalar(out=dg[:, k, :], in0=identf[:], scalar1=cw[:, c, k:k + 1], scalar2=None, op0=ALU.mult), reads=[tcw], writes=[tdg], self_waw_ok=True)
                    for tt in range(T // 512):
                        ps, tps = bank()
                        for k in range(5):
                            P.op("tensor", lambda e, ps=ps, dg=dg, cin=cin, k=k, tt=tt: e.matmul(ps[:], lhsT=dg[:, k, :], rhs=cin[:, tt * 512 + k:tt * 512 + k + 512], start=(k == 0), stop=(k == 4)), reads=[tdg, tcin], writes=[tps])
                        stg, tst = sstr.next()
                        P.op("scalar", lambda e, stg=stg, ps=ps, c=c: e.activation(out=stg[:], in_=ps[:], func=AF.Silu, bias=cbv[:, c:c + 1]), reads=[tps, tcw], writes=[tst])
                        store(xbcT[c * 128:(c + 1) * 128, tt * 512:(tt + 1) * 512], stg[:], tst)
                P.barrier()
            with contextlib.ExitStack() as pst:
                adt = sb("adt", [128, NT, 64], F32, pst)
                dec = sb("dec", [128, NT, 64], F32, pst)
                adh = sb("adh", [128, NT, 64], BF16, pst)
                adl = sb("adl", [128, NT, 64], BF16, pst)
                biasd = [sb("biasd%d" % i, [128, NT, 32], F32, pst) for i in range(2)]
                wgt = [sb("wgt%d" % i, [128, NT, 32], F32, pst) for i in range(2)]
                esc = [sb("esc%d" % i, [128, NT, 32], F32, pst) for i in range(2)]
                dskb = sb("dskb", [128, 32], F32, pst)
                DI = sb("DI", [128, 32, 128], BF16, pst)
                tprep = P.dtok("prep")
                with contextlib.ExitStack() as pst2:
                    dtr = sb("dtr", [128, NT, 64], F32, pst2)
                    dtv = sb("dtv", [128, NT, 64], F32, pst2)
                    lndt = sb("lndt", [128, NT, 64], F32, pst2)
                    cs = [sb("cs%d" % i, [128, NT, 64], F32, pst2) for i in range(4)]
                    dtb = sb("dtb", [128, 64], F32, pst2)
                    alg = sb("alg", [128, 64], F32, pst2)
                    P.dma("sync", lambda e: e.dma_start(out=dtr[:], in_=dts.rearrange("(c p) j -> p c j", p=128)), writes=[tprep], sem_tok=tprep)
                    load_bcast(dtb[:], p_dtb[l], tprep, 64)
                    load_bcast(alg[:], p_alog[l], tprep, 64)
                    load_bcast(dskb[:], p_dsk[l], tprep, 32)
                    tp = [tprep]
                    P.op("vector", lambda e: e.tensor_tensor(out=dtr[:], in0=dtr[:], in1=dtb[:].unsqueeze(1).to_broadcast([128, NT, 64]), op=ALU.add), reads=tp, writes=tp)
                    P.op("scalar", lambda e: e.activation(out=dtr[:], in_=dtr[:], func=AF.Exp), reads=tp, writes=tp)
                    P.op("scalar", lambda e: e.activation(out=dtv[:], in_=dtr[:], func=AF.Ln, bias=1.0), reads=tp, writes=tp)
                    sm = cs[0]
                    mk = cs[1]
                    P.op("vector", lambda e: e.tensor_scalar(out=sm[:], in0=dtr[:], scalar1=-0.25, scalar2=1.0 / 3.0, op0=ALU.mult, op1=ALU.add), reads=tp, writes=tp)
                    P.op("vector", lambda e: e.tensor_tensor(out=sm[:], in0=sm[:], in1=dtr[:], op=ALU.mult), reads=tp, writes=tp)
                    P.op("vector", lambda e: e.tensor_scalar(out=sm[:], in0=sm[:], scalar1=-0.5, scalar2=None, op0=ALU.add), reads=tp, writes=tp)
                    P.op("vector", lambda e: e.tensor_tensor(out=sm[:], in0=sm[:], in1=dtr[:], op=ALU.mult), reads=tp, writes=tp)
                    P.op("vector", lambda e: e.tensor_scalar(out=sm[:], in0=sm[:], scalar1=1.0, scalar2=None, op0=ALU.add), reads=tp, writes=tp)
                    P.op("vector", lambda e: e.tensor_tensor(out=sm[:], in0=sm[:], in1=dtr[:], op=ALU.mult), reads=tp, writes=tp)
                    P.op("vector", lambda e: e.tensor_scalar(out=mk[:], in0=dtr[:], scalar1=0.1, scalar2=None, op0=ALU.is_lt), reads=tp, writes=tp)
                    P.op("vector", lambda e: e.tensor_tensor(out=sm[:], in0=sm[:], in1=dtv[:], op=ALU.subtract), reads=tp, writes=tp)
                    P.op("vector", lambda e: e.tensor_tensor(out=sm[:], in0=sm[:], in1=mk[:], op=ALU.mult), reads=tp, writes=tp)
                    P.op("vector", lambda e: e.tensor_tensor(out=dtv[:], in0=dtv[:], in1=sm[:], op=ALU.add), reads=tp, writes=tp)
                    dump("dt", dtv[:], [128, NT, 64], tprep)
                    P.op("scalar", lambda e: e.activation(out=lndt[:], in_=dtv[:], func=AF.Ln), reads=tp, writes=tp)
                    P.op("scalar", lambda e: e.activation(out=alg[:], in_=alg[:], func=AF.Exp), reads=tp, writes=tp)
                    P.op("vector", lambda e: e.scalar_tensor_tensor(out=adt[:], in0=dtv[:], scalar=-1.0, in1=alg[:].unsqueeze(1).to_broadcast([128, NT, 64]), op0=ALU.mult, op1=ALU.mult), reads=tp, writes=tp)
                    P.op("vector", lambda e: e.tensor_copy(out=adh[:], in_=adt[:]), reads=tp, writes=tp)
                    P.op("vector", lambda e: e.tensor_tensor(out=dtr[:], in0=adt[:], in1=adh[:], op=ALU.subtract), reads=tp, writes=tp)
                    P.op("vector", lambda e: e.tensor_copy(out=adl[:], in_=dtr[:]), reads=tp, writes=tp)
                    adf = adt[:].rearrange("p c j -> p (c j)")
                    npc = (NT * 64 + 511) // 512
                    for m in range(5):
                        dstt = (cs[m] if m < 4 else dec)[:].rearrange("p c j -> p (c j)")
                        for pc in range(npc):
                            n0 = pc * 512
                            n1 = min(NT * 64, n0 + 512)
                            ps, tps = bank()
                            lh = tri[:, m, :] if m < 4 else onesf[:]
                            P.op("tensor", lambda e, ps=ps, lh=lh, n0=n0, n1=n1: e.matmul(ps[:, 0:n1 - n0], lhsT=lh, rhs=adf[:, n0:n1], start=True, stop=True), reads=tp + [tconst], writes=[tps])
                            if m < 4:
                                P.op("scalar", lambda e, ps=ps, dstt=dstt, n0=n0, n1=n1: e.activation(out=dstt[:, n0:n1], in_=ps[:, 0:n1 - n0], func=AF.Copy), reads=[tps], writes=tp)
                            else:
                                P.op("scalar", lambda e, ps=ps, dstt=dstt, n0=n0, n1=n1: e.activation(out=dstt[:, n0:n1], in_=ps[:, 0:n1 - n0], func=AF.Exp), reads=[tps], writes=tp)
                    P.op("vector", lambda e: e.tensor_tensor(out=biasd[0][:], in0=lndt[:, :, 0:32], in1=cs[0][:, :, 0:32], op=ALU.subtract), reads=tp, writes=tp)
                    P.op("vector", lambda e: e.tensor_tensor(out=wgt[0][:], in0=cs[1][:, :, 0:32], in1=lndt[:, :, 0:32], op=ALU.add), reads=tp, writes=tp)
                    P.op("scalar", lambda e: e.activation(out=wgt[0][:], in_=wgt[0][:], func=AF.Exp), reads=tp, writes=tp)
                    P.op("scalar", lambda e: e.activation(out=esc[0][:], in_=cs[0][:, :, 0:32], func=AF.Exp), reads=tp, writes=tp)
                    P.op("vector", lambda e: e.tensor_tensor(out=biasd[1][:], in0=cs[2][:, :, 32:64], in1=lndt[:, :, 32:64], op=ALU.add), reads=tp, writes=tp)
                    P.op("scalar", lambda e: e.activation(out=wgt[1][:], in_=biasd[1][:], func=AF.Exp), reads=tp, writes=tp)
                    P.op("scalar", lambda e: e.activation(out=esc[1][:], in_=cs[3][:, :, 32:64], func=AF.Exp), reads=tp, writes=tp)
                    dump("lndt", lndt[:], [128, NT, 64], tprep)
                    dump("adt", adt[:], [128, NT, 64], tprep)
                    dump("dec", dec[:], [128, NT, 64], tprep)
                    dump("bias1", biasd[1][:], [128, NT, 32], tprep)
                    dump("wgt1", wgt[1][:], [128, NT, 32], tprep)
                    dump("esc1", esc[1][:], [128, NT, 32], tprep)
                    dump("cs2", cs[2][:], [128, NT, 64], tprep)
                    for h in range(32):
                        P.op("vector", lambda e, h=h: e.tensor_scalar(out=DI[:, h, :], in0=identf[:], scalar1=dskb[:, h:h + 1], scalar2=None, op0=ALU.mult), reads=tp, writes=tp)
                    P.barrier()
                xcl = Rot([(sb("xcl%d" % i, [128, 24, 128], BF16, pst), P.dtok("xcl%d" % i)) for i in range(2)])
                xtmr = Rot([(sb("xtm%d" % i, [128, 2560], BF16, pst), P.tok("xtm%d" % i)) for i in range(2)])
                xwr = Rot([(sb("xw%d" % i, [128, 2048], BF16, pst), P.tok("xw%d" % i)) for i in range(2)])
                ltr = Rot([(sb("lt%d" % i, [128, 8, 128], BF16, pst), P.tok("lt%d" % i)) for i in range(2)])
                mpr = Rot([(sb("mp%d" % i, [128, 8, 128], BF16, pst), P.tok("mp%d" % i)) for i in range(2)])
                cbr = Rot([(sb("cbs%d" % i, [128, 128], F32, pst), P.tok("cbs%d" % i)) for i in range(2)])
                hst = sb("hst", [128, 2048], F32, pst)
                hbf = sb("hbf", [128, 2048], BF16, pst)
                thst = [P.tok("hst%d" % g) for g in range(4)]
                thbf = [P.tok("hbf%d" % g) for g in range(4)]
                ytr = Rot([(sb("ytmp%d" % i, [128, 512], F32, pst), P.tok("ytmp%d" % i)) for i in range(2)])
                yaccr = Rot([(sb("yacc%d" % i, [128, 2048], F32, pst), P.dtok("yacc%d" % i)) for i in range(2)])
                yblr = Rot([(sb("ybl%d" % i, [128, 2048], F32, pst), P.dtok("ybl%d" % i)) for i in range(1)])
                ztr = Rot([(sb("zt%d" % i, [128, 2048], BF16, pst), P.dtok("zt%d" % i)) for i in range(1)])
                xbv = xbcT.rearrange("(k p) t -> p k t", p=128)
                if dbg:
                    dlt = nc.dram_tensor("d_lt", [128, 1024], BF16, kind="ExternalOutput").ap()
                    dmp = nc.dram_tensor("d_mp", [128, 1024], BF16, kind="ExternalOutput").ap()
                    dcb = nc.dram_tensor("d_cb", [128, 128], F32, kind="ExternalOutput").ap()
                    tdbg = P.dtok("dbgst")
                for dr_ in (1, 0):
                    P.op("vector", lambda e: e.memset(hst[:], 0.0), writes=thst)
                    P.op("vector", lambda e: e.memset(hbf[:], 0.0), writes=thbf)
                    order = range(NT) if dr_ == 0 else range(NT - 1, -1, -1)
                    mtri = 0 if dr_ == 0 else 2
                    order = list(order)
                    pre = {}

                    def issue(cc):
                        xa_, txa_ = xcl.next()
                        P.dma("sync", lambda e, xa_=xa_, cc=cc: e.dma_start(out=xa_[:], in_=xbv[:, :, cc * 128:(cc + 1) * 128]), writes=[txa_], sem_tok=txa_)
                        pre[cc] = (xa_, txa_)
                    issue(order[0])
                    for oi, c in enumerate(order):
                        xa, txa = pre.pop(c)
                        if oi + 1 < len(order):
                            issue(order[oi + 1])
                        rows = slice(c * 128, (c + 1) * 128)
                        if dr_ == 0:
                            ybl, tybl = yblr.next()
                            zt, tzt = ztr.next()
                            P.dma("sync", lambda e, ybl=ybl, rows=rows: e.dma_start(out=ybl[:], in_=yb[rows, :]), writes=[tybl], sem_tok=tybl)
                            P.dma("sync", lambda e, zt=zt, rows=rows: e.dma_start(out=zt[:], in_=zs[rows, :]), writes=[tzt], sem_tok=tzt)
                        xtm, txtm = xtmr.next()
                        for part in range(3):
                            ps, tps = bank()
                            psb = ps[:].bitcast(BF16)
                            nk = 8 if part < 2 else 4
                            for k in range(nk):
                                P.op("tensor", lambda e, psb=psb, xa=xa, k=k, part=part: e.transpose(psb[:, k * 128:(k + 1) * 128], xa[:, part * 8 + k, :], identb[:]), reads=[txa], writes=[tps])
                            if part == 1:
                                P.op("vector", lambda e, xtm=xtm, psb=psb, part=part, nk=nk: e.tensor_copy(out=xtm[:, part * 1024:part * 1024 + nk * 128], in_=psb[:, 0:nk * 128]), reads=[tps], writes=[txtm])
                            else:
                                P.op("scalar", lambda e, xtm=xtm, psb=psb, part=part, nk=nk: e.activation(out=xtm[:, part * 1024:part * 1024 + nk * 128], in_=psb[:, 0:nk * 128], func=AF.Copy), reads=[tps], writes=[txtm])
                        xw, txw = xwr.next()
                        P.op("gpsimd", lambda e, xw=xw, xtm=xtm, c=c, dr_=dr_: e.tensor_tensor(out=xw[:].rearrange("p (h d) -> p h d", d=64), in0=xtm[:, 0:2048].rearrange("p (h d) -> p h d", d=64), in1=wgt[dr_][:, c, :].unsqueeze(2).to_broadcast([128, 32, 64]), op=ALU.mult), reads=[txtm], writes=[txw])
                        yacc, tyacc = yaccr.next()
                        for g in range(4):
                            psc, tpsc = bank()
                            P.op("tensor", lambda e, psc=psc, xa=xa, g=g: e.matmul(psc[:, 0:128], lhsT=xa[:, 16 + g, :], rhs=xa[:, 20 + g, :], start=True, stop=True), reads=[txa], writes=[tpsc])
                            cbs, tcbs = cbr.next()
                            P.op("scalar", lambda e, cbs=cbs, psc=psc: e.activation(out=cbs[:], in_=psc[:, 0:128], func=AF.Copy), reads=[tpsc], writes=[tcbs])
                            lt, tlt = ltr.next()
                            pds = [bank(), bank()]
                            for hh in range(8):
                                h = g * 8 + hh
                                pd, tpd = pds[hh // 4]
                                tgt = pd[:, (hh % 4) * 128:(hh % 4 + 1) * 128]
                                col = dr_ * 32 + h
                                P.op("tensor", lambda e, tgt=tgt, c=c, col=col, mtri=mtri: e.matmul(tgt, lhsT=adh[:, c, col:col + 1].to_broadcast([128, 128]), rhs=trib[:, mtri, :], start=True, stop=False), writes=[tpd])
                                P.op("tensor", lambda e, tgt=tgt, c=c, col=col, mtri=mtri: e.matmul(tgt, lhsT=adl[:, c, col:col + 1].to_broadcast([128, 128]), rhs=trib[:, mtri, :], start=False, stop=False), writes=[tpd])
                                P.op("tensor", lambda e, tgt=tgt, dr_=dr_: e.matmul(tgt, lhsT=identb[:], rhs=negb[:, dr_, :], start=False, stop=True), writes=[tpd])
                            for hh in range(8):
                                h = g * 8 + hh
                                pd, tpd = pds[hh // 4]
                                tgt = pd[:, (hh % 4) * 128:(hh % 4 + 1) * 128]
                                P.op("scalar", lambda e, lt=lt, hh=hh, tgt=tgt, c=c, h=h, dr_=dr_: e.activation(out=lt[:, hh, :], in_=tgt, func=AF.Exp, bias=biasd[dr_][:, c, h:h + 1], scale=(1.0 if dr_ == 0 else -1.0)), reads=[tpd], writes=[tlt], self_waw_ok=True)
                            mp, tmp_ = mpr.next()
                            P.op("vector", lambda e, mp=mp, lt=lt, cbs=cbs: e.tensor_tensor(out=mp[:], in0=lt[:], in1=cbs[:].unsqueeze(1).to_broadcast([128, 8, 128]), op=ALU.mult), reads=[tlt, tcbs], writes=[tmp_])
                            if dbg and dr_ == 1 and c == NT - 1 and g == 0:
                                P.dma("sync", lambda e, lt=lt: e.dma_start(out=dlt, in_=lt[:].rearrange("p a b -> p (a b)")), reads=[tlt], sem_tok=tdbg)
                                P.dma("sync", lambda e, mp=mp: e.dma_start(out=dmp, in_=mp[:].rearrange("p a b -> p (a b)")), reads=[tmp_], sem_tok=tdbg)
                                P.dma("sync", lambda e, cbs=cbs: e.dma_start(out=dcb, in_=cbs[:]), reads=[tcbs], sem_tok=tdbg)
                            psy, tpsy = bank()
                            for hh in range(8):
                                h = g * 8 + hh
                                P.op("tensor", lambda e, psy=psy, mp=mp, hh=hh, h=h, xtm=xtm, dr_=dr_: e.matmul(psy[:, hh * 64:(hh + 1) * 64], lhsT=mp[:, hh, :], rhs=xtm[:, h * 64:(h + 1) * 64], start=True, stop=(dr_ == 1), skip_group_check=True), reads=[tmp_, txtm], writes=[tpsy])
                                if dr_ == 0:
                                    P.op("tensor", lambda e, psy=psy, hh=hh, h=h, xtm=xtm: e.matmul(psy[:, hh * 64:(hh + 1) * 64], lhsT=DI[:, h, :], rhs=xtm[:, h * 64:(h + 1) * 64], start=False, stop=True, skip_group_check=True), reads=[txtm], writes=[tpsy])
                            pso, tpso = bank()
                            P.op("tensor", lambda e, pso=pso, xa=xa, g=g: e.matmul(pso[:], lhsT=xa[:, 20 + g, :], rhs=hbf[:, g * 512:(g + 1) * 512], start=True, stop=True), reads=[txa, thbf[g]], writes=[tpso])
                            yt, tyt = ytr.next()
                            P.op("vector", lambda e, yt=yt, pso=pso, c=c, g=g, dr_=dr_: e.tensor_tensor(out=yt[:].rearrange("p (h d) -> p h d", d=64), in0=pso[:].rearrange("p (h d) -> p h d", d=64), in1=esc[dr_][:, c, g * 8:(g + 1) * 8].unsqueeze(2).to_broadcast([128, 8, 64]), op=ALU.mult), reads=[tpso], writes=[tyt])
                            P.op("vector", lambda e, yacc=yacc, psy=psy, yt=yt, g=g: e.tensor_tensor(out=yacc[:, g * 512:(g + 1) * 512], in0=psy[:], in1=yt[:], op=ALU.add), reads=[tpsy, tyt], writes=[tyacc], self_waw_ok=True)
                            pss, tpss = bank()
                            P.op("tensor", lambda e, pss=pss, xtm=xtm, xw=xw, g=g: e.matmul(pss[:], lhsT=xtm[:, 2048 + g * 128:2048 + (g + 1) * 128], rhs=xw[:, g * 512:(g + 1) * 512], start=True, stop=True), reads=[txtm, txw], writes=[tpss])
                            hs = hst[:, g * 512:(g + 1) * 512]
                            P.op("vector", lambda e, hs=hs, c=c, g=g, dr_=dr_: e.tensor_tensor(out=hs.rearrange("p (h d) -> p h d", d=64), in0=hs.rearrange("p (h d) -> p h d", d=64), in1=dec[:, c, dr_ * 32 + g * 8:dr_ * 32 + (g + 1) * 8].unsqueeze(2).to_broadcast([128, 8, 64]), op=ALU.mult), reads=[thst[g]], writes=[thst[g]])
                            P.op("vector", lambda e, hs=hs, pss=pss: e.tensor_tensor(out=hs, in0=hs, in1=pss[:], op=ALU.add), reads=[thst[g], tpss], writes=[thst[g]])
                            P.op("gpsimd", lambda e, hs=hs, g=g: e.tensor_copy(out=hbf[:, g * 512:(g + 1) * 512], in_=hs), reads=[thst[g]], writes=[thbf[g]])
                        if dr_ == 1:
                            store(yb[rows, :], yacc[:], tyacc)
                        else:
                            P.op("gpsimd", lambda e, yacc=yacc, ybl=ybl: e.tensor_tensor(out=yacc[:], in0=yacc[:], in1=ybl[:], op=ALU.add), reads=[tyacc, tybl], writes=[tyacc])
                            P.op("gpsimd", lambda e, yacc=yacc, zt=zt: e.tensor_tensor(out=yacc[:], in0=yacc[:], in1=zt[:], op=ALU.mult), reads=[tyacc, tzt], writes=[tyacc])
                            store(yg[rows, :], yacc[:], tyacc)
                    P.barrier()


        def phase_attn(l):
            with contextlib.ExitStack() as pst:
                qr = Rot([(sb("qsb%d" % i, [128, T], BF16, pst), P.dtok("qsb%d" % i)) for i in range(2)])
                kr = Rot([(sb("ksb%d" % i, [128, T], BF16, pst), P.dtok("ksb%d" % i)) for i in range(2)])
                vr = Rot([(sb("vt%d" % i, [128, NT, 128], BF16, pst), P.dtok("vt%d" % i)) for i in range(2)])
                num = sb("num", [128, T], F32, pst)
                den = sb("den", [128, T], F32, pst)
                tnum, tden = P.tok("num"), P.tok("den")
                osb = sb("osb", [128, T], BF16, pst)
                tosb = P.dtok("osb")
                ptr = Rot([(sb("pT%d" % i, [128, 128], BF16, pst), P.tok("pT%d" % i)) for i in range(3)])
                for j in range(4):
                    for g in range(3):
                        h = 4 * g + j
                        d = ATTN_PATTERNS[g][1]
                        S = T // d
                        NK = S // 128
                        qs, tqs = qr.next()
                        ks, tks = kr.next()
                        vt, tvt = vr.next()
                        P.dma("sync", lambda e, qs=qs, h=h: e.dma_start(out=qs[:], in_=qT[h]), writes=[tqs], sem_tok=tqs)
                        P.dma("sync", lambda e, ks=ks, h=h: e.dma_start(out=ks[:], in_=kT[h]), writes=[tks], sem_tok=tks)
                        vv = vt[:].rearrange("a (r kt) e -> a r kt e", r=d)
                        for r in range(d):
                            P.dma("sync", lambda e, vv=vv, h=h, d=d, r=r: e.dma_start(out=vv[:, r], in_=vs[:, h * 128:(h + 1) * 128].rearrange("(kt a r) e -> a r kt e", a=128, r=d)[:, r]), writes=[tvt], sem_tok=tvt)
                        for r in range(d):
                            for m in range(NK + 1):
                                b0 = 64 if m == 0 else 0
                                b1 = 64 if m == NK else 128
                                nq = b1 - b0
                                i0 = 128 * m - 64 + b0
                                q0 = i0 * d + r
                                qsl = qs[:, q0:q0 + (nq - 1) * d + 1:d]
                                pso, tpso = bank()
                                psd, tpsd = bank()
                                kts = [kt for kt in (m - 1, m) if 0 <= kt < NK]
                                for idx, kt in enumerate(kts):
                                    typ = 0 if kt == m - 1 else 1
                                    pss, tpss = bank()
                                    k0 = 128 * kt * d + r
                                    ksl = ks[:, k0:k0 + 127 * d + 1:d]
                                    P.op("tensor", lambda e, pss=pss, ksl=ksl, qsl=qsl, nq=nq: e.matmul(pss[:, 0:nq], lhsT=ksl, rhs=qsl, start=True, stop=False), reads=[tqs, tks], writes=[tpss])
                                    P.op("tensor", lambda e, pss=pss, nq=nq, h=h, typ=typ, b0=b0, b1=b1: e.matmul(pss[:, 0:nq], lhsT=identf[:], rhs=abias[:, h * 2 + typ, b0:b1], start=False, stop=True), writes=[tpss])
                                    pt, tpt = ptr.next()
                                    P.op("scalar", lambda e, pt=pt, pss=pss, nq=nq: e.activation(out=pt[:, 0:nq], in_=pss[:, 0:nq], func=AF.Exp), reads=[tpss], writes=[tpt])
                                    fl = dict(start=(idx == 0), stop=(idx == len(kts) - 1))
                                    P.op("tensor", lambda e, pso=pso, vv=vv, r=r, kt=kt, pt=pt, nq=nq, fl=fl: e.matmul(pso[:, 0:nq], lhsT=vv[:, r, kt, :], rhs=pt[:, 0:nq], **fl), reads=[tvt, tpt], writes=[tpso])
                                    P.op("tensor", lambda e, psd=psd, pt=pt, nq=nq, fl=fl: e.matmul(psd[:, 0:nq], lhsT=onesb[:], rhs=pt[:, 0:nq], **fl), reads=[tpt], writes=[tpsd])
                                nsl = num[:, q0:q0 + (nq - 1) * d + 1:d]
                                dsl = den[:, q0:q0 + (nq - 1) * d + 1:d]
                                if g == 0:
                                    P.op("vector", lambda e, nsl=nsl, pso=pso, nq=nq: e.tensor_copy(out=nsl, in_=pso[:, 0:nq]), reads=[tpso], writes=[tnum], self_waw_ok=True)
                                    P.op("scalar", lambda e, dsl=dsl, psd=psd, nq=nq: e.activation(out=dsl, in_=psd[:, 0:nq], func=AF.Copy), reads=[tpsd], writes=[tden], self_waw_ok=True)
                                else:
                                    P.op("vector", lambda e, nsl=nsl, pso=pso, nq=nq: e.tensor_tensor(out=nsl, in0=nsl, in1=pso[:, 0:nq], op=ALU.add), reads=[tpso], writes=[tnum], self_waw_ok=True)
                                    P.op("vector", lambda e, dsl=dsl, psd=psd, nq=nq: e.tensor_tensor(out=dsl, in0=dsl, in1=psd[:, 0:nq], op=ALU.add), reads=[tpsd], writes=[tden], self_waw_ok=True)
                    P.op("vector", lambda e: e.reciprocal(out=den[:], in_=den[:]), reads=[tden], writes=[tden])
                    P.op("vector", lambda e: e.tensor_tensor(out=osb[:], in0=num[:], in1=den[:], op=ALU.mult), reads=[tnum, tden], writes=[tosb])
                    store(oT[j * 128:(j + 1) * 128, :], osb[:], tosb)
                P.barrier()

        def phase_conf(l):
            with contextlib.ExitStack() as pst:
                dwp = sb("dwp", [128, 16, 31], F32, pst)
                dwb = sb("dwb", [128, 16], F32, pst)
                tdw = P.dtok("dw")
                P.dma("sync", lambda e: e.dma_start(out=dwp[:], in_=p_dw[l]), writes=[tdw], sem_tok=tdw)
                P.dma("sync", lambda e: e.dma_start(out=dwb[:], in_=p_dwb[l]), writes=[tdw], sem_tok=tdw)
                cins = [(sb("ccin%d" % i, [128, T + 30], BF16, pst), P.dtok("ccin%d" % i)) for i in range(2)]
                for cin, tcin in cins:
                    P.op("vector", lambda e, cin=cin: e.memset(cin[:, 0:15], 0.0), writes=[tcin])
                    P.op("vector", lambda e, cin=cin: e.memset(cin[:, T + 15:T + 30], 0.0), writes=[tcin])
                cinrot = Rot(cins)
                dgr = Rot([(sb("dg%d" % i, [128, 31, 128], BF16, pst), P.tok("dg%d" % i)) for i in range(2)])
                str_ = Rot([(sb("cst%d" % i, [128, 512], BF16, pst), P.dtok("cst%d" % i)) for i in range(3)])
                for cc in range(16):
                    cin, tcin = cinrot.next()
                    P.dma("sync", lambda e, cin=cin, cc=cc: e.dma_start(out=cin[:, 15:15 + T], in_=uT[cc * 128:(cc + 1) * 128, :]), writes=[tcin], sem_tok=tcin)
                    dg, tdg = dgr.next()
                    for k in range(31):
                        eng = "vector" if k % 2 == 0 else "gpsimd"
                        P.op(eng, lambda e, dg=dg, k=k, cc=cc: e.tensor_scalar(out=dg[:, k, :], in0=identf[:], scalar1=dwp[:, cc, k:k + 1], scalar2=None, op0=ALU.mult), reads=[tdw], writes=[tdg], self_waw_ok=True)
                    P.op("vector", lambda e, dg=dg: e.tensor_copy(out=dg[:, 30, 0:1], in_=dg[:, 30, 0:1]), reads=[tdg], writes=[tdg])
                    for tt in range(T // 512):
                        ps, tps = bank()
                        for k in range(31):
                            P.op("tensor", lambda e, ps=ps, dg=dg, cin=cin, k=k, tt=tt: e.matmul(ps[:], lhsT=dg[:, k, :], rhs=cin[:, tt * 512 + k:tt * 512 + k + 512], start=(k == 0), stop=(k == 30)), reads=[tdg, tcin], writes=[tps])
                        stg, tst = str_.next()
                        P.op("scalar", lambda e, stg=stg, ps=ps, cc=cc: e.activation(out=stg[:], in_=ps[:], func=AF.Identity, bias=dwb[:, cc:cc + 1]), reads=[tps, tdw], writes=[tst])
                        store(ycT[cc * 128:(cc + 1) * 128, tt * 512:(tt + 1) * 512], stg[:], tst)
                P.barrier()
            with contextlib.ExitStack() as pst:
                lng = sb("lng", [128, 16], F32, pst)
                lnb = sb("lnb", [128, 16], F32, pst)
                tln = P.dtok("ln")
                P.dma("sync", lambda e: e.dma_start(out=lng[:], in_=p_lng[l]), writes=[tln], sem_tok=tln)
                P.dma("sync", lambda e: e.dma_start(out=lnb[:], in_=p_lnb[l]), writes=[tln], sem_tok=tln)
                ylr = Rot([(sb("yl%d" % i, [128, 16, 512], BF16, pst), P.dtok("yl%d" % i)) for i in range(2)])
                sqr = Rot([(sb("sqy%d" % i, [128, 16, 512], BF16, pst), P.tok("sqy%d" % i)) for i in range(1)])
                mur = Rot([(sb("mu%d" % i, [128, 3, 512], F32, pst), P.tok("mu%d" % i)) for i in range(2)])
                t1r = Rot([(sb("t1%d" % i, [128, 512], F32, pst), P.tok("t1%d" % i)) for i in range(3)])
                str_ = Rot([(sb("cst2%d" % i, [128, 512], BF16, pst), P.dtok("cst2%d" % i)) for i in range(3)])
                ycv = ycT.rearrange("(k p) t -> p k t", p=128)
                for tt in range(T // 512):
                    yl, tyl = ylr.next()
                    P.dma("sync", lambda e, yl=yl, tt=tt: e.dma_start(out=yl[:], in_=ycv[:, :, tt * 512:(tt + 1) * 512]), writes=[tyl], sem_tok=tyl)
                    sqy, tsq = sqr.next()
                    P.op("gpsimd", lambda e, sqy=sqy, yl=yl: e.tensor_tensor(out=sqy[:], in0=yl[:], in1=yl[:], op=ALU.mult), reads=[tyl], writes=[tsq])
                    ps1, tps1 = bank()
                    ps2, tps2 = bank()
                    for cc in range(16):
                        P.op("tensor", lambda e, ps1=ps1, yl=yl, cc=cc: e.matmul(ps1[:], lhsT=onesb[:], rhs=yl[:, cc, :], start=(cc == 0), stop=(cc == 15)), reads=[tyl], writes=[tps1])
                    for cc in range(16):
                        P.op("tensor", lambda e, ps2=ps2, sqy=sqy, cc=cc: e.matmul(ps2[:], lhsT=onesb[:], rhs=sqy[:, cc, :], start=(cc == 0), stop=(cc == 15)), reads=[tsq], writes=[tps2])
                    mu, tmu = mur.next()
                    P.op("scalar", lambda e, mu=mu, ps1=ps1: e.activation(out=mu[:, 0, :], in_=ps1[:], func=AF.Copy, scale=1.0 / D), reads=[tps1], writes=[tmu])
                    P.op("vector", lambda e, mu=mu: e.tensor_tensor(out=mu[:, 1, :], in0=mu[:, 0, :], in1=mu[:, 0, :], op=ALU.mult), reads=[tmu], writes=[tmu])
                    P.op("vector", lambda e, mu=mu, ps2=ps2: e.scalar_tensor_tensor(out=mu[:, 2, :], in0=ps2[:], scalar=1.0 / D, in1=mu[:, 1, :], op0=ALU.mult, op1=ALU.subtract), reads=[tmu, tps2], writes=[tmu])
                    P.op("scalar", lambda e, mu=mu: e.activation(out=mu[:, 2, :], in_=mu[:, 2, :], func=AF.Sqrt, bias=EPS), reads=[tmu], writes=[tmu])
                    P.op("vector", lambda e, mu=mu: e.reciprocal(out=mu[:, 2, :], in_=mu[:, 2, :]), reads=[tmu], writes=[tmu])
                    for cc in range(16):
                        t1, tt1 = t1r.next()
                        eng = "vector" if cc % 2 == 0 else "gpsimd"
                        P.op(eng, lambda e, t1=t1, yl=yl, mu=mu, cc=cc: e.tensor_tensor(out=t1[:], in0=yl[:, cc, :], in1=mu[:, 0, :], op=ALU.subtract), reads=[tyl, tmu], writes=[tt1])
                        P.op(eng, lambda e, t1=t1, mu=mu: e.tensor_tensor(out=t1[:], in0=t1[:], in1=mu[:, 2, :], op=ALU.mult), reads=[tt1, tmu], writes=[tt1])
                        stg, tst = str_.next()
                        P.op("scalar", lambda e, stg=stg, t1=t1, cc=cc: e.activation(out=stg[:], in_=t1[:], func=AF.Silu, bias=lnb[:, cc:cc + 1], scale=lng[:, cc:cc + 1]), reads=[tt1, tln], writes=[tst])
                        store(cT[cc * 128:(cc + 1) * 128, tt * 512:(tt + 1) * 512], stg[:], tst)
                P.barrier()

        def phase_B(l, xsrc):
            with contextlib.ExitStack() as pst:
                TB = min(1024, T)
                NTB = TB // 128
                CW = 256
                NJ = CW // 128
                ph = make_phase(pst, TB, nw=6, wsize=16 * CW)
                actT, tact = ph["actT"], ph["tact"]
                gb, tgb = ph["gb"]
                load_bcast(gb[:], p_ng[l], tgb, D)
                mergedT = sb("mergedT", [128, KC, TB], BF16, pst)
                NTT = TB // 512
                tm = [P.tok("mer%d" % i) for i in range(NTT)]
                tmer_list = [tm[t // 4] for t in range(NTB)]
                glr = Rot([(sb("gl%d" % i, [128, 512], BF16, pst), P.dtok("gl%d" % i)) for i in range(3)])
                tactld = P.dtok("actld")
                tnone = P.tok("none")

                def branch(W, kcn, br, first, g0):
                    for nb in range(D // CW):
                        wv, wtk = load_w(ph, W, kcn, 0, nb * CW, CW)
                        for j in range(NJ):
                            cb = nb * NJ + j
                            for tt in range(NTT):
                                gl, tgl = glr.next()
                                P.dma("sync", lambda e, gl=gl, cb=cb, tt=tt: e.dma_start(out=gl[:], in_=gT[br * D + cb * 128:br * D + (cb + 1) * 128, g0 + tt * 512:g0 + (tt + 1) * 512]), writes=[tgl], sem_tok=tgl)
                                ps, tps = mm_fm(wv, wtk, j, kcn, actT, tact, tt)
                                msl = mergedT[:, cb, tt * 512:(tt + 1) * 512]
                                if first:
                                    P.op("vector", lambda e, msl=msl, ps=ps, gl=gl: e.tensor_tensor(out=msl, in0=ps[:], in1=gl[:], op=ALU.mult), reads=[tps, tgl], writes=[tm[tt]], self_waw_ok=True)
                                else:
                                    tmpb, ttmp = ph["stf"].next()
                                    P.op("vector", lambda e, tmpb=tmpb, ps=ps, gl=gl: e.tensor_tensor(out=tmpb[:], in0=ps[:], in1=gl[:], op=ALU.mult), reads=[tps, tgl], writes=[ttmp])
                                    P.op("vector", lambda e, msl=msl, tmpb=tmpb: e.tensor_tensor(out=msl, in0=msl, in1=tmpb[:], op=ALU.add), reads=[ttmp], writes=[tm[tt]], self_waw_ok=True)

                for s in range(T // TB):
                    g0 = s * TB
                    norm_transpose(ph, yg, tnone, gb, tgb, g0, TB, actT, tact)
                    branch(w_ssd_o[l], 16, 0, True, g0)
                    for kc in range(4):
                        P.dma("sync", lambda e, kc=kc, g0=g0: e.dma_start(out=actT[:, kc, :], in_=oT[kc * 128:(kc + 1) * 128, g0:g0 + TB]), writes=tact, sem_tok=tactld)
                    branch(w_attn_o[l], 4, 1, False, g0)
                    for kc in range(16):
                        P.dma("sync", lambda e, kc=kc, g0=g0: e.dma_start(out=actT[:, kc, :], in_=cT[kc * 128:(kc + 1) * 128, g0:g0 + TB]), writes=tact, sem_tok=tactld)
                    branch(w_conv_o[l], 16, 2, False, g0)
                    for nb in range(D // CW):
                        wv, wtk = load_w(ph, w_out[l], 16, 0, nb * CW, CW)
                        for t in range(NTB):
                            xo, txo = ph["stf"].next()
                            rows = slice(g0 + t * 128, g0 + (t + 1) * 128)
                            P.dma("sync", lambda e, xo=xo, rows=rows, nb=nb: e.dma_start(out=xo[:, 0:CW], in_=xsrc[rows, nb * CW:(nb + 1) * CW]), writes=[txo], sem_tok=txo)
                            ps, tps = mm_tm(wv, wtk, CW, 16, mergedT, tmer_list, t)
                            P.op("vector", lambda e, xo=xo, ps=ps: e.tensor_tensor(out=xo[:, 0:CW], in0=xo[:, 0:CW], in1=ps[:, 0:CW], op=ALU.add), reads=[tps, txo], writes=[txo])
                            store(xres[rows, nb * CW:(nb + 1) * CW], xo[:, 0:CW], txo)
                P.barrier()

        def phase_mlp(l, dst):
            with contextlib.ExitStack() as pst:
                TM = min(1024, T)
                NTM = TM // 128
                NTT = TM // 512
                HH = 4096
                CW = 256
                NJ = CW // 128
                ph = make_phase(pst, TM, nw=4, nx=2, wsize=16 * CW)
                actT, tact = ph["actT"], ph["tact"]
                gb, tgb = ph["gb"]
                load_bcast(gb[:], p_n2[l], tgb, D)
                aT = sb("aT", [128, 32, TM], BF16, pst)
                ta = [P.tok("aT%d" % i) for i in range(2)]
                rr = Rot([(sb("relu%d" % i, [128, 512], BF16, pst), P.tok("relu%d" % i)) for i in range(3)])
                tnone = P.tok("none")
                for ti in range(T // TM):
                    r0 = ti * TM
                    norm_transpose(ph, xres, tnone, gb, tgb, r0, TM, actT, tact)
                    tdr = [[P.tok("xd") for _ in range(D // CW)] for _ in range(NTM)]
                    for hh in range(2):
                        for nb in range(HH // CW):
                            wv, wtk = load_w(ph, w_up[l], 16, 0, hh * HH + nb * CW, CW)
                            for j in range(NJ):
                                hc = nb * NJ + j
                                for tt in range(NTT):
                                    ps, tps = mm_fm(wv, wtk, j, 16, actT, tact, tt)
                                    rl, trl = rr.next()
                                    P.op("scalar", lambda e, rl=rl, ps=ps: e.activation(out=rl[:], in_=ps[:], func=AF.Relu), reads=[tps], writes=[trl])
                                    P.op("vector", lambda e, rl=rl, hc=hc, tt=tt: e.tensor_tensor(out=aT[:, hc, tt * 512:(tt + 1) * 512], in0=rl[:], in1=rl[:], op=ALU.mult), reads=[trl], writes=[ta[hc // 16]], self_waw_ok=True)
                        for nb in range(D // CW):
                            b8 = [bank() for _ in range(NTM)]
                            for kq in range(2):
                                wv, wtk = load_w(ph, w_down[l], 16, hh * HH + kq * 2048, nb * CW, CW)
                                for t in range(NTM):
                                    mm_tm(wv, wtk, CW, 16, aT, [ta[kq]] * NTM, t, pst=b8[t], first=(kq == 0), last=(kq == 1), kc0=kq * 16)
                            src = xres if hh == 0 else dst
                            for t in range(NTM):
                                ps, tps = b8[t]
                                xo, txo = ph["stf"].next()
                                rows = slice(r0 + t * 128, r0 + (t + 1) * 128)
                                P.dma("sync", lambda e, xo=xo, rows=rows, nb=nb, src=src: e.dma_start(out=xo[:, 0:CW], in_=src[rows, nb * CW:(nb + 1) * CW]), reads=[tdr[t][nb]], writes=[txo], sem_tok=txo)
                                P.op("vector", lambda e, xo=xo, ps=ps: e.tensor_tensor(out=xo[:, 0:CW], in0=xo[:, 0:CW], in1=ps[:, 0:CW], op=ALU.add), reads=[tps, txo], writes=[txo])
                                P.dma("sync", lambda e, xo=xo, rows=rows, nb=nb: e.dma_start(out=dst[rows, nb * CW:(nb + 1) * CW], in_=xo[:, 0:CW]), reads=[txo], writes=[tdr[t][nb]], sem_tok=txo)
                P.barrier()

        for l in range(depth):
            xsrc = x_in if l == 0 else xres
            for fn, args in [(phase_A, (l, xsrc)), (phase_ssd, (l,)), (phase_attn, (l,)), (phase_conf, (l,)),
                             (phase_B, (l, xsrc)), (phase_mlp, (l, out if l == depth - 1 else xres))]:
                if phases is not None and fn.__name__ not in phases:
                    continue
                P.begin_phase()
                fn(*args)
                P.end_phase()
        P.emit(block)
    return nc


_WNAMES = ["w_in", "w_ssd_o", "w_attn_o", "w_conv_o", "w_out", "w_mlp_up", "w_mlp_down"]


def run_cores(x_list, inputs, T, depth=DEPTH, dbg=False, phases=None):
    nc = build(T, depth=depth, dbg=dbg, phases=phases)
    consts = host_consts()
    lay = host_layout(inputs)
    common = {}
    for k in _WNAMES:
        common[k] = np.ascontiguousarray(inputs[k][:depth])
    for k, v in lay.items():
        common[k] = np.ascontiguousarray(v[:depth])
    common.update(consts)
    zeros = {k: np.zeros_like(v) for k, v in common.items()}
    in_maps = []
    for xb in x_list:
        if xb is None:
            m = dict(zeros)
            m["x"] = np.zeros((T, D), np.float32)
        else:
            m = dict(common)
            m["x"] = np.ascontiguousarray(xb)
        in_maps.append(m)
    res = run_bass_kernel_spmd(nc, in_maps, core_ids=list(range(len(x_list))))
    return res.results


def kernel(**inputs):
    x = np.asarray(inputs["x"], dtype=np.float32)
    B, T, _ = x.shape
    inp = {k: np.asarray(v, dtype=np.float32) for k, v in inputs.items()}
    cores = [0, 1, 4, 5]
    x_list = [None] * 8
    for b in range(B):
        x_list[cores[b]] = x[b]
    results = run_cores(x_list, inp, T)
    return np.stack([results[cores[b]]["out"] for b in range(B)], axis=0).astype(np.float32)
```
